# Optimizing a Trainium2 kernel written in Bass

```python
import math
import jax, jax.numpy as jnp
from jax import lax
import numpy as np

D_MODEL = 2048
BATCH = 2
SEQ = 16384
DEPTH = 2

CHUNK = 64
W_A = D_MODEL
H_A = 16
BH_A = W_A // H_A
LRU_C = 8.0
CONV_K = 4
W_B = D_MODEL
P_B = 64
H_B = W_B // P_B
G_B = 4
R_B = H_B // G_B
N_B = 128
SSD_CONV_DIM = W_B + 2 * G_B * N_B
D_IN_AB = 2 * W_A + W_B + SSD_CONV_DIM + H_B
H_C = 4
DK_C = D_MODEL // 2
DV_C = D_MODEL
DKH_C = DK_C // H_C
DVH_C = DV_C // H_C
GATE_RANK = 16
GATE_TAU = 16.0
D_IN_C = 2 * DK_C + 2 * DV_C + GATE_RANK
MEM_LEN = 256
XA_HEADS = 4
XA_DH = D_MODEL // XA_HEADS
D_FF = 3 * D_MODEL
FFN_K = 3

N_EVEN = (DEPTH + 1) // 2
N_ODD = DEPTH // 2
EPS = 1e-6

kernel_name = 'hybrid_rglru_ssd_gla_streaming_encoder'


def rms_norm(x, g):
    xf = x.astype(jnp.float32)
    y = xf * lax.rsqrt(jnp.mean(xf * xf, axis=-1, keepdims=True) + EPS)
    return (y * g.astype(jnp.float32)).astype(x.dtype)


def causal_dwconv(x, w, b):
    k, c = w.shape
    y = lax.conv_general_dilated(x, w[:, None, :].astype(x.dtype), window_strides=(1,),
                                 padding=[(k - 1, 0)],
                                 dimension_numbers=('NWC', 'WIO', 'NWC'),
                                 feature_group_count=c)
    return y + b.astype(x.dtype)


def causal_mask():
    pos = jnp.arange(CHUNK)
    return pos[:, None] >= pos[None, :]


def to_chunks(t):
    b, s = t.shape[:2]
    return jnp.moveaxis(t.reshape((b, s // CHUNK, CHUNK) + t.shape[2:]), 1, 0)


def from_chunks(t):
    t = jnp.moveaxis(t, 0, 1)
    return t.reshape((t.shape[0], t.shape[1] * t.shape[2]) + t.shape[3:])


def rg_lru(x, w_a, b_a, w_x, b_x, lam):
    bsz, s, _ = x.shape
    xh = x.reshape(bsz, s, H_A, BH_A)
    r = jax.nn.sigmoid((jnp.einsum('bshi,hij->bshj', xh, w_a).reshape(bsz, s, W_A) + b_a).astype(jnp.float32))
    i = jax.nn.sigmoid((jnp.einsum('bshi,hij->bshj', xh, w_x).reshape(bsz, s, W_A) + b_x).astype(jnp.float32))
    log_a = -LRU_C * r * jax.nn.softplus(-lam.astype(jnp.float32))
    a = jnp.exp(log_a)
    u = jnp.sqrt(-jnp.expm1(2.0 * log_a)) * (i * x.astype(jnp.float32))

    def combine(left, right):
        a_l, u_l = left
        a_r, u_r = right
        return a_r * a_l, a_r * u_l + u_r

    _, h = lax.associative_scan(combine, (a, u), axis=1)
    return h.astype(x.dtype)


def ssd_chunked(xs, dt, a_neg, bm, cm):
    bsz = xs.shape[0]
    mask5 = causal_mask()[None, :, :, None, None]

    def step(state, inp):
        xc, dtc, bc, cc = inp
        cs = jnp.cumsum(dtc * a_neg, axis=1)
        seg = cs[:, :, None] - cs[:, None, :]
        lmat = jnp.exp(jnp.where(mask5, seg, -jnp.inf))
        cb = jnp.einsum('bign,bjgn->bijg', cc, bc)
        y_intra = jnp.einsum('bijg,bijgr,bjgr,bjgrp->bigrp', cb, lmat, dtc, xc)
        y_inter = jnp.einsum('bign,bgrpn->bigrp', cc, state) * jnp.exp(cs)[..., None]
        w_end = jnp.exp(cs[:, -1:] - cs) * dtc
        new_state = state * jnp.exp(cs[:, -1])[..., None, None] + \
            jnp.einsum('bjgn,bjgr,bjgrp->bgrpn', bc, w_end, xc)
        return new_state, y_intra + y_inter

    state0 = jnp.zeros((bsz, G_B, R_B, P_B, N_B), jnp.float32)
    _, ys = lax.scan(step, state0, (to_chunks(xs), to_chunks(dt), to_chunks(bm), to_chunks(cm)))
    return from_chunks(ys)


def ab_mixer(xn, w_in, lru_conv_w, lru_conv_b, lru_w_a, lru_b_a, lru_w_x, lru_b_x, lru_lambda,
             ssd_conv_w, ssd_conv_b, ssd_dt_bias, ssd_a_log, ssd_d, ssd_norm_g, w_out):
    bsz, s, _ = xn.shape
    proj = xn @ w_in
    gate_a, x_a, z_b, xbc, dt_raw = jnp.split(
        proj, [W_A, 2 * W_A, 2 * W_A + W_B, 2 * W_A + W_B + SSD_CONV_DIM], axis=-1)
    h_a = rg_lru(causal_dwconv(x_a, lru_conv_w, lru_conv_b),
                 lru_w_a, lru_b_a, lru_w_x, lru_b_x, lru_lambda)
    y_a = jax.nn.gelu(gate_a) * h_a
    xbc = jax.nn.silu(causal_dwconv(xbc, ssd_conv_w, ssd_conv_b))
    xs, bm, cm = jnp.split(xbc, [W_B, W_B + G_B * N_B], axis=-1)
    xs = xs.reshape(bsz, s, G_B, R_B, P_B).astype(jnp.float32)
    bm = bm.reshape(bsz, s, G_B, N_B).astype(jnp.float32)
    cm = cm.reshape(bsz, s, G_B, N_B).astype(jnp.float32)
    dt = jax.nn.softplus((dt_raw + ssd_dt_bias).astype(jnp.float32)).reshape(bsz, s, G_B, R_B)
    a_neg = -jnp.exp(ssd_a_log.astype(jnp.float32)).reshape(G_B, R_B)
    y = ssd_chunked(xs, dt, a_neg, bm, cm)
    y = y + ssd_d.astype(jnp.float32).reshape(G_B, R_B)[..., None] * xs
    y = y.reshape(bsz, s, W_B) * jax.nn.silu(z_b.astype(jnp.float32))
    yg = y.reshape(bsz, s, G_B, W_B // G_B)
    yg = yg * lax.rsqrt(jnp.mean(yg * yg, axis=-1, keepdims=True) + EPS)
    y_b = (yg.reshape(bsz, s, W_B) * ssd_norm_g.astype(jnp.float32)).astype(xn.dtype)
    return jnp.concatenate([y_a, y_b], axis=-1) @ w_out


def gla_chunked(q, k, v, log_alpha):
    bsz = q.shape[0]
    mask5 = causal_mask()[None, :, :, None, None]

    def step(state, inp):
        qc, kc, vc, lac = inp
        bcum = jnp.cumsum(lac, axis=1)
        diff = bcum[:, :, None] - bcum[:, None, :]
        decay = jnp.exp(jnp.where(mask5, diff, -jnp.inf))
        att = jnp.einsum('bihk,bjhk,bijhk->bhij', qc, kc, decay)
        o_intra = jnp.einsum('bhij,bjhv->bihv', att, vc)
        o_inter = jnp.einsum('bihk,bhkv->bihv', qc * jnp.exp(bcum), state)
        b_last = bcum[:, -1]
        k_dec = kc * jnp.exp(b_last[:, None] - bcum)
        new_state = jnp.exp(b_last)[..., None] * state + jnp.einsum('bjhk,bjhv->bhkv', k_dec, vc)
        return new_state, o_intra + o_inter

    state0 = jnp.zeros((bsz, H_C, DKH_C, DVH_C), jnp.float32)
    _, ys = lax.scan(step, state0, (to_chunks(q), to_chunks(k), to_chunks(v), to_chunks(log_alpha)))
    return from_chunks(ys)


def gla_mixer(xn, w_in, w_gate_up, b_gate, norm_g, w_out):
    bsz, s, _ = xn.shape
    proj = xn @ w_in
    q, k, v, r, g_low = jnp.split(
        proj, [DK_C, 2 * DK_C, 2 * DK_C + DV_C, 2 * DK_C + 2 * DV_C], axis=-1)
    log_alpha = jax.nn.log_sigmoid((g_low @ w_gate_up + b_gate).astype(jnp.float32)) / GATE_TAU
    qh = q.reshape(bsz, s, H_C, DKH_C).astype(jnp.float32) * (DKH_C ** -0.5)
    kh = k.reshape(bsz, s, H_C, DKH_C).astype(jnp.float32)
    vh = v.reshape(bsz, s, H_C, DVH_C).astype(jnp.float32)
    o = gla_chunked(qh, kh, vh, log_alpha.reshape(bsz, s, H_C, DKH_C))
    o = o * lax.rsqrt(jnp.mean(o * o, axis=-1, keepdims=True) + EPS)
    o = o.reshape(bsz, s, DV_C) * norm_g.astype(jnp.float32) * jax.nn.silu(r.astype(jnp.float32))
    return o.astype(xn.dtype) @ w_out


def cross_attention(xn, memn, w_q, w_kv, w_o):
    bsz, s, _ = xn.shape
    q = (xn @ w_q).reshape(bsz, s, XA_HEADS, XA_DH)
    k, v = jnp.split(memn @ w_kv, 2, axis=-1)
    k = k.reshape(bsz, MEM_LEN, XA_HEADS, XA_DH)
    v = v.reshape(bsz, MEM_LEN, XA_HEADS, XA_DH)
    scores = jnp.einsum('bshd,bmhd->bhsm', q, k).astype(jnp.float32) * (XA_DH ** -0.5)
    p = jax.nn.softmax(scores, axis=-1).astype(v.dtype)
    o = jnp.einsum('bhsm,bmhd->bshd', p, v).reshape(bsz, s, D_MODEL)
    return o @ w_o


def conv_ffn(xn, w_in, conv_w, conv_b, w_out):
    h = causal_dwconv(xn @ w_in, conv_w, conv_b)
    val, gate = jnp.split(h, 2, axis=-1)
    return (val * jax.nn.gelu(gate)) @ w_out


def setup_inputs(seed: int = 0) -> dict:
    key = jax.random.key(seed)
    keys = list(jax.random.split(key, 64))
    f32 = jnp.float32

    def nk():
        return keys.pop()

    def dense(shape, fan_in, scale=1.0):
        return jax.random.normal(nk(), shape, f32) * (scale * fan_in ** -0.5)

    def gain(shape):
        return 1.0 + 0.02 * jax.random.normal(nk(), shape, f32)

    def small(shape, s=0.02):
        return s * jax.random.normal(nk(), shape, f32)

    a_c = jax.random.uniform(nk(), (N_EVEN, W_A), f32, 0.9, 0.999)
    a_base = a_c ** (1.0 / LRU_C)
    lru_lambda = jnp.log(a_base) - jnp.log1p(-a_base)
    dt0 = jnp.exp(jax.random.uniform(nk(), (N_EVEN, H_B), f32, math.log(1e-3), math.log(1e-1)))
    ssd_dt_bias = dt0 + jnp.log(-jnp.expm1(-dt0))
    ssd_a_log = jnp.log(jax.random.uniform(nk(), (N_EVEN, H_B), f32, 1.0, 16.0))

    return {
        'x': jax.random.normal(nk(), (BATCH, SEQ, D_MODEL), f32),
        'mem': jax.random.normal(nk(), (BATCH, MEM_LEN, D_MODEL), f32),
        'norm_mix_g': gain((DEPTH, D_MODEL)),
        'norm_cross_g': gain((DEPTH, D_MODEL)),
        'norm_ffn_g': gain((DEPTH, D_MODEL)),
        'ab_w_in': dense((N_EVEN, D_MODEL, D_IN_AB), D_MODEL),
        'lru_conv_w': dense((N_EVEN, CONV_K, W_A), CONV_K),
        'lru_conv_b': small((N_EVEN, W_A)),
        'lru_w_a': dense((N_EVEN, H_A, BH_A, BH_A), BH_A),
        'lru_b_a': small((N_EVEN, W_A)),
        'lru_w_x': dense((N_EVEN, H_A, BH_A, BH_A), BH_A),
        'lru_b_x': small((N_EVEN, W_A)),
        'lru_lambda': lru_lambda,
        'ssd_conv_w': dense((N_EVEN, CONV_K, SSD_CONV_DIM), CONV_K),
        'ssd_conv_b': small((N_EVEN, SSD_CONV_DIM)),
        'ssd_dt_bias': ssd_dt_bias,
        'ssd_a_log': ssd_a_log,
        'ssd_d': gain((N_EVEN, H_B)),
        'ssd_norm_g': gain((N_EVEN, W_B)),
        'ab_w_out': dense((N_EVEN, W_A + W_B, D_MODEL), W_A + W_B, 0.5),
        'gla_w_in': dense((N_ODD, D_MODEL, D_IN_C), D_MODEL),
        'gla_w_gate_up': dense((N_ODD, GATE_RANK, DK_C), GATE_RANK),
        'gla_b_gate': small((N_ODD, DK_C), 0.1),
        'gla_norm_g': gain((N_ODD, DV_C)),
        'gla_w_out': dense((N_ODD, DV_C, D_MODEL), DV_C, 0.5),
        'mem_norm_g': gain((D_MODEL,)),
        'xa_w_q': dense((DEPTH, D_MODEL, D_MODEL), D_MODEL),
        'xa_w_kv': dense((DEPTH, D_MODEL, 2 * D_MODEL), D_MODEL),
        'xa_w_o': dense((DEPTH, D_MODEL, D_MODEL), D_MODEL, 0.5),
        'ffn_w_in': dense((DEPTH, D_MODEL, 2 * D_FF), D_MODEL),
        'ffn_conv_w': dense((DEPTH, FFN_K, 2 * D_FF), FFN_K),
        'ffn_conv_b': small((DEPTH, 2 * D_FF)),
        'ffn_w_out': dense((DEPTH, D_FF, D_MODEL), D_FF, 0.5),
        'final_norm_g': gain((D_MODEL,)),
    }


def reference(x, mem, norm_mix_g, norm_cross_g, norm_ffn_g,
              ab_w_in, lru_conv_w, lru_conv_b, lru_w_a, lru_b_a, lru_w_x, lru_b_x, lru_lambda,
              ssd_conv_w, ssd_conv_b, ssd_dt_bias, ssd_a_log, ssd_d, ssd_norm_g, ab_w_out,
              gla_w_in, gla_w_gate_up, gla_b_gate, gla_norm_g, gla_w_out,
              mem_norm_g, xa_w_q, xa_w_kv, xa_w_o,
              ffn_w_in, ffn_conv_w, ffn_conv_b, ffn_w_out, final_norm_g):
    memn = rms_norm(mem, mem_norm_g)
    for layer in range(DEPTH):
        j = layer // 2
        xn = rms_norm(x, norm_mix_g[layer])
        if layer % 2 == 0:
            x = x + ab_mixer(xn, ab_w_in[j], lru_conv_w[j], lru_conv_b[j], lru_w_a[j], lru_b_a[j],
                             lru_w_x[j], lru_b_x[j], lru_lambda[j], ssd_conv_w[j], ssd_conv_b[j],
                             ssd_dt_bias[j], ssd_a_log[j], ssd_d[j], ssd_norm_g[j], ab_w_out[j])
        else:
            x = x + gla_mixer(xn, gla_w_in[j], gla_w_gate_up[j], gla_b_gate[j], gla_norm_g[j], gla_w_out[j])
        x = x + cross_attention(rms_norm(x, norm_cross_g[layer]), memn,
                                xa_w_q[layer], xa_w_kv[layer], xa_w_o[layer])
        x = x + conv_ffn(rms_norm(x, norm_ffn_g[layer]), ffn_w_in[layer], ffn_conv_w[layer],
                         ffn_conv_b[layer], ffn_w_out[layer])
    return rms_norm(x, final_norm_g)
```

```python
import contextlib
import numpy as np
import concourse.bass as bass
import concourse.mybir as mybir
from concourse.bass_utils import run_bass_kernel_spmd

F32 = mybir.dt.float32
BF16 = mybir.dt.bfloat16
AF = mybir.ActivationFunctionType
ALU = mybir.AluOpType
AX = mybir.AxisListType

D = 2048
KD = 16
T = 512
EPS = 1e-6
NCORES = 8


class Dep:
    __slots__ = ("w", "r")

    def __init__(self):
        self.w = {}
        self.r = {}


class _Rec:
    def __init__(self, lst):
        self.lst = lst

    def __getattr__(self, name):
        def f(*a, **kw):
            self.lst.append(["ins", name, a, kw, None, None])
            return self
        return f


class Prog:
    NDMA = 24
    NCC = 4
    ARENA = 52000

    def __init__(self):
        self.nc = bass.Bass("TRN2", target_bir_lowering=False)
        nc = self.nc
        self.es = contextlib.ExitStack()
        self.engs = {"pe": nc.tensor, "act": nc.scalar, "dve": nc.vector, "pool": nc.gpsimd, "sp": nc.sync}
        self.rec = {e: [] for e in self.engs}
        self.prox = {e: _Rec(self.rec[e]) for e in self.engs}
        self.sems = {}
        self.cnt = {}
        for e in self.engs:
            self.sems[e] = self.es.enter_context(nc.semaphore("sem_" + e))
            self.cnt[e] = 0
        for fam in ("d", "g"):
            for i in range(self.NDMA):
                self.sems[(fam, i)] = self.es.enter_context(nc.semaphore("sem%s%d" % (fam, i)))
                self.cnt[(fam, i)] = 0
        for i in range(self.NCC):
            self.sems[("c", i)] = self.es.enter_context(nc.semaphore("semcc%d" % i))
            self.cnt[("c", i)] = 0
        self.ncc = 0
        self.dnext = {"d": 0, "g": 0}
        self.known = {e: {} for e in self.engs}
        self.nins = 0
        self.arena = self.es.enter_context(nc.sbuf_tensor("arena", [128, self.ARENA], F32))
        self.psum = self.es.enter_context(nc.psum_tensor("psum_all", [128, 4096], F32))
        self.off = 0
        self.bank = 0

    @staticmethod
    def _view(base, shape, dt):
        v = base if dt == F32 else base.bitcast(dt)
        n = 1
        for d_ in shape[1:]:
            n *= d_
        v = v[0:shape[0], 0:n]
        if len(shape) == 3:
            v = v.rearrange("p (a b) -> p a b", a=shape[1])
        elif len(shape) == 4:
            v = v.rearrange("p (a b c) -> p a b c", a=shape[1], b=shape[2])
        return v

    def sb(self, name, shape, dt):
        n = 1
        for d_ in shape[1:]:
            n *= d_
        words = (n * (4 if dt == F32 else 2) + 3) // 4
        words = (words + 15) // 16 * 16
        assert self.off + words <= self.ARENA, ("SBUF arena overflow", name, self.off, words)
        base = self.arena[:, self.off:self.off + words]
        self.off += words
        self.peak = max(getattr(self, "peak", 0), self.off)
        return self._view(base, list(shape), dt)

    def ps(self, name, shape, dt=F32):
        n = 1
        for d_ in shape[1:]:
            n *= d_
        nb = (n * (4 if dt == F32 else 2) + 2047) // 2048
        assert self.bank + nb <= 8, ("PSUM overflow", name)
        base = self.psum[:, self.bank * 512:(self.bank + nb) * 512]
        self.bank += nb
        return self._view(base, list(shape), dt)

    def dram(self, name, shape, dt, kind):
        return self.nc.dram_tensor(name, list(shape), dt, kind=kind).ap()

    @contextlib.contextmanager
    def scope(self):
        off, bank = self.off, self.bank
        try:
            yield
        finally:
            self.barrier()
            self.off, self.bank = off, bank

    SAME_ENGINE_WAITS = True

    def _waits(self, eng, reads, writes):
        need = {}
        if not self.SAME_ENGINE_WAITS and eng in ("act", "dve"):
            for d in list(reads) + list(writes):
                for k, v in d.w.items():
                    if k != eng and need.get(k, 0) < v:
                        need[k] = v
            for d in writes:
                for k, v in d.r.items():
                    if k != eng and need.get(k, 0) < v:
                        need[k] = v
            return need
        for d in reads:
            for k, v in d.w.items():
                if eng == "pe" and k == "pe":
                    continue
                if need.get(k, 0) < v:
                    need[k] = v
        for d in writes:
            for k, v in d.w.items():
                if eng == "pe" and k == "pe":
                    continue
                if need.get(k, 0) < v:
                    need[k] = v
            for k, v in d.r.items():
                if k == eng:
                    continue
                if need.get(k, 0) < v:
                    need[k] = v
        return need

    def _emit_waits(self, eng, need):
        kn = self.known[eng]
        for k, v in need.items():
            if kn.get(k, 0) >= v:
                continue
            self.rec[eng].append(["wait", k, v])
            kn[k] = v
            self.nins += 1

    dry = False

    def op(self, eng, fn, reads=(), writes=()):
        if self.dry:
            return
        need = self._waits(eng, reads, writes)
        self._emit_waits(eng, need)
        n0 = len(self.rec[eng])
        fn(self.prox[eng])
        assert len(self.rec[eng]) == n0 + 1
        self.cnt[eng] += 1
        v = self.cnt[eng]
        self.rec[eng][-1][4] = eng
        self.rec[eng][-1][5] = 1
        self.nins += 1
        for d in reads:
            d.r[eng] = v
        for d in writes:
            d.w = {eng: v}
            d.r = {}

    ALLPOOL = False

    def dma(self, q, out, in_, reads=(), writes=()):
        if self.dry:
            return
        if self.ALLPOOL:
            q = "pool"
        fam = "g" if q == "pool" else "d"
        i = self.dnext[fam]
        self.dnext[fam] = (i + 1) % self.NDMA
        sk = (fam, i)
        need = self._waits(q, reads, writes)
        if self.cnt[sk] > 0:
            need[sk] = max(need.get(sk, 0), self.cnt[sk])
        self._emit_waits(q, need)
        self.cnt[sk] += 16
        v = self.cnt[sk]
        self.rec[q].append(["ins", "dma_start", (), dict(out=out, in_=in_), sk, 16])
        self.nins += 1
        for d in reads:
            d.r[sk] = v
        for d in writes:
            d.w = {sk: v}
            d.r = {}

    def barrier(self):
        if self.dry:
            return
        allv = {k: v for k, v in self.cnt.items() if v > 0}
        for eng in self.engs:
            self._emit_waits(eng, {k: v for k, v in allv.items() if k != eng})

    def allgather(self, src, dst):
        sk = ("c", self.ncc)
        self.ncc += 1
        self.rec["pool"].append(["cc", src, dst, sk])
        self.cnt[sk] = 1
        self.nins += 1

    def wait_deps(self, deps, engines=("pe", "act", "dve")):
        if self.dry:
            return
        for eng in engines:
            need = {}
            for d in deps:
                for k, v in d.w.items():
                    if need.get(k, 0) < v:
                        need[k] = v
            self._emit_waits(eng, need)

    def _replay(self, eng, e):
        sems = self.sems
        for it in self.rec[eng]:
            if it[0] == "wait":
                e.wait_ge(sems[it[1]], it[2])
            elif it[0] == "ins":
                ins = getattr(e, it[1])(*it[2], **it[3])
                ins.then_inc(sems[it[4]], it[5])
            else:
                ins = e.collective_compute("AllGather", ALU.bypass, replica_groups=[list(range(NCORES))],
                                           ins=[it[1].ap().opt()], outs=[it[2].ap().opt()])
                ins.then_inc(sems[it[3]])

    def finish(self):
        self.barrier()
        with self.nc.Block() as blk:
            blk.tensor(lambda e: self._replay("pe", e))
            blk.scalar(lambda e: self._replay("act", e))
            blk.vector(lambda e: self._replay("dve", e))
            blk.gpsimd(lambda e: self._replay("pool", e))
            blk.sync(lambda e: self._replay("sp", e))
        self.es.close()


class WStream:
    def __init__(self, p, nslot=4, slot_elems=8192, name="ws"):
        self.p = p
        self.nslot = nslot
        self.slots = [p.sb("%s%d" % (name, i), [128, slot_elems], BF16) for i in range(nslot)]
        self.deps = [Dep() for _ in range(nslot)]
        self.specs = []
        self.issued = 0
        self.used = 0

    def plan(self, specs):
        self.specs.extend(specs)

    def _issue(self, i):
        W, k0, nk, f0, nf = self.specs[i]
        s = i % self.nslot
        p = self.p
        key = (W.name, k0, nk, f0, nf)
        cache = p.__dict__.setdefault("wcache", {})
        if key not in cache:
            scr = p.nc.dram_tensor("wbf_%d" % len(cache), [128, nk * nf], BF16).ap()
            dep = Dep()
            src = W.rearrange("(k p) f -> p k f", p=128)[:, k0:k0 + nk, f0:f0 + nf]
            p.dma("pool", out=scr.rearrange("p (k f) -> p k f", f=nf), in_=src, writes=[dep])
            cache[key] = (scr, dep)
        scr, dep = cache[key]
        p.dma("sp", out=self.slots[s][:, 0:nk * nf], in_=scr, reads=[dep], writes=[self.deps[s]])

    def next(self, spec=None):
        if self.p.dry:
            self.__dict__.setdefault("recording", []).append(spec)
            return None, None
        i = self.used
        if spec is not None:
            assert tuple(spec[1:]) == tuple(self.specs[i][1:]), (spec[1:], self.specs[i][1:])
        self.used += 1
        while self.issued < len(self.specs) and self.issued <= i + self.nslot - 2:
            self._issue(self.issued)
            self.issued += 1
        W, k0, nk, f0, nf = self.specs[i]
        s = i % self.nslot
        view = self.slots[s][:, 0:nk * nf].rearrange("p (k f) -> p k f", f=nf)
        return view, self.deps[s]


def load_const(p, name, ap, shape, dt=F32, q="sp"):
    t = p.sb(name, shape, dt)
    d = Dep()
    p.dma(q, out=t[:], in_=ap, writes=[d])
    return t, d


def rstd_from_ms(p, C, pss, pssd, n, scale=1.0):
    rstd, rstdd = C["rstd"], C["rstdd"]
    p.op("act", lambda e: e.activation(out=rstd[:, 0:n], in_=pss[:, 0:n], func=AF.Sqrt, bias=C["eps"][:, 0:1], scale=scale),
         reads=[pssd, C["constd"]], writes=[rstdd])
    p.op("dve", lambda e: e.reciprocal(out=rstd[:, 0:n], in_=rstd[:, 0:n]), reads=[rstdd], writes=[rstdd])


def rmsnorm(p, C, x, xdep, n, g, gdep, xn, xndep):
    sq, sqd = C["sq"], C["sqd"]
    p.op("act", lambda e: e.activation(out=sq[:, :, 0:n], in_=x[:, :, 0:n], func=AF.Square), reads=[xdep], writes=[sqd])
    pss, pssd = C["ps_ssq"], C["ps_ssqd"]
    for k in range(KD):
        p.op("pe", lambda e, k=k: e.matmul(pss[:, 0:n], lhsT=C["ones_bf"][:], rhs=sq[:, k, 0:n], start=(k == 0), stop=(k == KD - 1)),
             reads=[sqd, C["constd"]], writes=[pssd])
    rstd, rstdd = C["rstd"], C["rstdd"]
    rstd_from_ms(p, C, pss, pssd, n)
    for k in range(KD):
        p.op("dve", lambda e, k=k: e.scalar_tensor_tensor(out=xn[:, k, 0:n], in0=x[:, k, 0:n], scalar=g[:, k:k + 1], in1=rstd[:, 0:n],
                                                           op0=ALU.mult, op1=ALU.mult),
             reads=[xdep, rstdd, gdep], writes=[xndep])


def common(p):
    C = {}
    C["constd"] = Dep()
    C["ones_bf"] = p.sb("ones_bf", [128, 128], BF16)
    p.op("dve", lambda e: e.memset(C["ones_bf"][:], 1.0 / D), writes=[C["constd"]])
    C["eps"] = p.sb("eps_c", [128, 1], F32)
    p.op("dve", lambda e: e.memset(C["eps"][:], EPS), writes=[C["constd"]])
    C["sq"] = p.sb("sq", [128, KD, T], BF16)
    C["sqd"] = Dep()
    C["ps_ssq"] = p.ps("ps_ssq", [128, T])
    C["ps_ssqd"] = Dep()
    C["rstd"] = p.sb("rstd", [128, T], F32)
    C["rstdd"] = Dep()
    return C


def stage_ffn(p, C, ws, io, ntiles, final):
    NH = 2
    HC = 24
    xT, xh, yT = io["xT"], io.get("xh"), io["yT"]
    g, gd = load_const(p, "ffn_g", io["g"], [128, KD])
    cw, cwd = load_const(p, "ffn_cw", io["cw"], [128, 96, 3])
    cb, cbd = load_const(p, "ffn_cb", io["cb"], [128, 96])
    if final:
        fg, fgd = load_const(p, "fin_g", io["fg"], [128, KD])
    x = p.sb("ffn_x", [128, KD, T], F32)
    xd = Dep()
    xn = p.sb("ffn_xn", [128, KD, T], BF16)
    xnd = Dep()
    xhal = p.sb("ffn_xhal", [128, KD, 2], F32)
    xhald = Dep()
    xnh = p.sb("ffn_xnh", [128, KD, 2], BF16)
    xnhd = Dep()
    h = p.sb("ffn_h", [128, HC, T], BF16)
    hd = Dep()
    hal = p.sb("ffn_hal", [128, 96, 2], F32)
    hald = [Dep() for _ in range(96)]
    raw = [[p.sb("ffn_raw%d%d" % (i, j), [128, T + 2], F32) for j in range(2)] for i in range(2)]
    rawd = [[Dep() for j in range(2)] for i in range(2)]
    acc = [[p.sb("ffn_acc%d%d" % (i, j), [128, T], F32) for j in range(2)] for i in range(2)]
    accd = [[Dep() for j in range(2)] for i in range(2)]
    psA = [[p.ps("ffn_ps%d%d" % (i, j), [128, T]) for j in range(2)] for i in range(2)]
    psAd = [[Dep() for j in range(2)] for i in range(2)]
    psO = [p.ps("ffn_pso%d" % i, [128, T]) for i in range(2)]
    psOd = [Dep() for i in range(2)]
    psH = p.ps("ffn_psh", [128, 512])
    psHd = Dep()

    w_in, w_out = io["w_in"], io["w_out"]
    specs = []
    for t in range(ntiles):
        for hf in range(NH):
            for cg in range(HC // 4):
                c0 = hf * HC + cg * 4
                specs.append((w_in, 0, KD, c0 * 128, 512))
                specs.append((w_in, 0, KD, (48 + c0) * 128, 512))
            for og in range(8):
                specs.append((w_out, hf * HC, HC, og * 256, 256))
    ws.plan(specs)

    if "gath_h" in io:
        gh = p.sb("ffn_gh", [128, NCORES, 32], F32)
        ghd = Dep()
        p.dma("sp", out=gh[:], in_=io["gath_h"].ap().rearrange("(r p) w -> p r w", p=128), writes=[ghd])
        usel, useld = load_const(p, "ffn_usel", io["usel"], [128, NCORES])
        xh2 = xhal[:, :, :].rearrange("p k t -> p (k t)")
        p.op("dve", lambda e: e.memset(xhal[:], 0.0), writes=[xhald])
        for r in range(NCORES):
            p.op("dve", lambda e: e.scalar_tensor_tensor(out=xh2, in0=gh[:, r, :], scalar=usel[:, r:r + 1], in1=xh2, op0=ALU.mult, op1=ALU.add),
                 reads=[ghd, useld, xhald], writes=[xhald])
    else:
        p.dma("sp", out=xhal[:], in_=xh.rearrange("(k p) t -> p k t", p=128), writes=[xhald])
    rmsnorm(p, C, xhal, xhald, 2, g, gd, xnh, xnhd)

    pair = 0
    for t in range(ntiles):
        t0 = t * T
        p.dma("sp", out=x[:], in_=xT.rearrange("(k p) t -> p k t", p=128)[:, :, t0:t0 + T], writes=[xd])
        rmsnorm(p, C, x, xd, T, g, gd, xn, xnd)
        for hf in range(NH):
            for cg in range(HC // 4):
                wv, wvd = ws.next()
                wg, wgd = ws.next()
                for j in range(4):
                    cl = cg * 4 + j
                    cidx = [hf * HC + cl, 48 + hf * HC + cl]
                    b = pair % 2
                    pair += 1
                    for vg, (wt, wd) in enumerate(((wv, wvd), (wg, wgd))):
                        c = cidx[vg]
                        if t == 0:
                            for k in range(KD):
                                p.op("pe", lambda e, k=k, wt=wt, c=c: e.matmul(psH[:, 2 * c:2 * c + 2], lhsT=wt[:, k, j * 128:(j + 1) * 128],
                                                                                  rhs=xnh[:, k, :], start=(k == 0), stop=(k == KD - 1)),
                                     reads=[wd, xnhd], writes=[psHd])
                            p.op("act", lambda e, c=c: e.copy(out=hal[:, c, :], in_=psH[:, 2 * c:2 * c + 2]), reads=[psHd], writes=[hald[c]])
                        for k in range(KD):
                            p.op("pe", lambda e, k=k, wt=wt, vg=vg: e.matmul(psA[vg][b][:], lhsT=wt[:, k, j * 128:(j + 1) * 128], rhs=xn[:, k, :],
                                                                            start=(k == 0), stop=(k == KD - 1)),
                                 reads=[wd, xnd], writes=[psAd[vg][b]])
                        r_, rd_ = raw[vg][b], rawd[vg][b]
                        a_, ad_ = acc[vg][b], accd[vg][b]
                        p.op("act", lambda e, r_=r_, vg=vg: e.copy(out=r_[:, 2:T + 2], in_=psA[vg][b][:]), reads=[psAd[vg][b]], writes=[rd_])
                        p.op("act", lambda e, r_=r_, c=c: e.copy(out=r_[:, 0:2], in_=hal[:, c, :]), reads=[hald[c]], writes=[rd_])
                        p.op("dve", lambda e, r_=r_, a_=a_, c=c: e.tensor_scalar(out=a_[:], in0=r_[:, 0:T], scalar1=cw[:, c, 0:1], scalar2=cb[:, c:c + 1],
                                                                                 op0=ALU.mult, op1=ALU.add),
                             reads=[rd_, cwd, cbd], writes=[ad_])
                        for kk in (1, 2):
                            p.op("dve", lambda e, r_=r_, a_=a_, c=c, kk=kk: e.scalar_tensor_tensor(out=a_[:], in0=r_[:, kk:kk + T], scalar=cw[:, c, kk:kk + 1],
                                                                                                  in1=a_[:], op0=ALU.mult, op1=ALU.add),
                                 reads=[rd_, ad_, cwd], writes=[ad_])
                        p.op("act", lambda e, r_=r_, c=c: e.copy(out=hal[:, c, :], in_=r_[:, T:T + 2]), reads=[rd_], writes=[hald[c]])
                    p.op("act", lambda e: e.activation(out=acc[1][b][:], in_=acc[1][b][:], func=AF.Gelu_apprx_tanh), reads=[accd[1][b]], writes=[accd[1][b]])
                    p.op("dve", lambda e, cl=cl: e.tensor_tensor(out=h[:, cl, :], in0=acc[0][b][:], in1=acc[1][b][:], op=ALU.mult),
                         reads=[accd[0][b], accd[1][b]], writes=[hd])
            for og in range(8):
                wo, wod = ws.next()
                for j in range(2):
                    do = og * 2 + j
                    ob = do % 2
                    for k in range(HC):
                        p.op("pe", lambda e, k=k, wo=wo, j=j, ob=ob: e.matmul(psO[ob][:], lhsT=wo[:, k, j * 128:(j + 1) * 128], rhs=h[:, k, :],
                                                                             start=(k == 0), stop=(k == HC - 1)),
                             reads=[wod, hd], writes=[psOd[ob]])
                    p.op("dve", lambda e, do=do, ob=ob: e.tensor_tensor(out=x[:, do, :], in0=x[:, do, :], in1=psO[ob][:], op=ALU.add),
                         reads=[psOd[ob], xd], writes=[xd])
        if final:
            sq = C["sq"]
            p.op("act", lambda e: e.activation(out=sq[:], in_=x[:], func=AF.Square), reads=[xd], writes=[C["sqd"]])
            for k in range(KD):
                p.op("pe", lambda e, k=k: e.matmul(C["ps_ssq"][:], lhsT=C["ones_bf"][:], rhs=sq[:, k, :], start=(k == 0), stop=(k == KD - 1)),
                     reads=[C["sqd"], C["constd"]], writes=[C["ps_ssqd"]])
            rstd = C["rstd"]
            rstd_from_ms(p, C, C["ps_ssq"], C["ps_ssqd"], T)
            for k in range(KD):
                p.op("dve", lambda e, k=k: e.scalar_tensor_tensor(out=x[:, k, :], in0=x[:, k, :], scalar=fg[:, k:k + 1], in1=rstd[:],
                                                                   op0=ALU.mult, op1=ALU.mult),
                     reads=[xd, C["rstdd"], fgd], writes=[xd])
        p.dma("sp", out=yT.rearrange("(k p) t -> p k t", p=128)[:, :, t0:t0 + T], in_=x[:], reads=[xd])


def vecT(v):
    return np.ascontiguousarray(np.asarray(v, np.float32).reshape(-1, 128).T)


def build_ffn(ntok, final):
    p = Prog()
    io = {
        "xT": p.dram("xT", [D, ntok], F32, "ExternalInput"),
        "xh": p.dram("xh", [D, 2], F32, "ExternalInput"),
        "g": p.dram("g", [128, KD], F32, "ExternalInput"),
        "cw": p.dram("cw", [128, 96, 3], F32, "ExternalInput"),
        "cb": p.dram("cb", [128, 96], F32, "ExternalInput"),
        "fg": p.dram("fg", [128, KD], F32, "ExternalInput"),
        "w_in": p.dram("w_in", [D, 12288], F32, "ExternalInput"),
        "w_out": p.dram("w_out", [6144, D], F32, "ExternalInput"),
        "yT": p.dram("yT", [D, ntok], F32, "ExternalOutput"),
    }
    C = common(p)
    ws = WStream(p)
    stage_ffn(p, C, ws, io, ntok // T, final)
    p.finish()
    return p


def ffn_inputs(inp, layer):
    return {
        "g": vecT(inp["norm_ffn_g"][layer]),
        "cw": np.ascontiguousarray(np.asarray(inp["ffn_conv_w"][layer], np.float32).T.reshape(96, 128, 3).transpose(1, 0, 2)),
        "cb": vecT(inp["ffn_conv_b"][layer]),
        "fg": vecT(inp["final_norm_g"]),
        "w_in": np.ascontiguousarray(inp["ffn_w_in"][layer], dtype=np.float32),
        "w_out": np.ascontiguousarray(inp["ffn_w_out"][layer], dtype=np.float32),
    }


def stage_xa(p, C, ws, io, ntiles):
    NM = 256
    scale = 512.0 ** -0.5
    xT, yT = io["xT"], io["yT"]
    g, gd = load_const(p, "xa_g", io["g"], [128, KD])
    mg, mgd = load_const(p, "xa_mg", io["mg"], [128, KD])
    ident = p.sb("xa_ident", [128, 128], BF16)
    identd = Dep()
    p.dma("pool", out=ident[:], in_=io["ident"], writes=[identd])
    x = p.sb("xa_x", [128, KD, T], F32)
    xd = Dep()
    xn = p.sb("xa_xn", [128, KD, T], BF16)
    xnd = Dep()
    KT = p.sb("xa_KT", [128, KD, NM], BF16)
    KTd = Dep()
    V = p.sb("xa_V", [128, 2, D], BF16)
    Vd = Dep()
    qT = p.sb("xa_qT", [128, KD, T], BF16)
    qTd = Dep()
    oT, oTd = C["sq"], C["sqd"]
    PT = p.sb("xa_PT", [128, 2, T], BF16)
    PTd = Dep()
    P = [p.sb("xa_P%d" % i, [128, 4, NM], F32) for i in range(2)]
    Pd = [Dep() for _ in range(2)]
    Pn = [p.sb("xa_Pn%d" % i, [128, 4, NM], BF16) for i in range(2)]
    Pnd = [Dep() for _ in range(2)]
    st = [p.sb("xa_st%d" % i, [128, 12], F32) for i in range(2)]
    std = [Dep() for _ in range(2)]
    psA = [p.ps("xa_psA%d" % i, [128, T]) for i in range(2)]
    psAd = [Dep() for _ in range(2)]
    psS = [p.ps("xa_psS%d" % i, [128, 4, NM]) for i in range(2)]
    psSd = [Dep() for _ in range(2)]
    psT = p.ps("xa_psT", [128, 2 * T], BF16)
    psTd = Dep()

    w_q, w_kv, w_o = io["w_q"], io["w_kv"], io["w_o"]
    specs = [(w_kv, 0, KD, fg * 512, 512) for fg in range(4)] + [(w_kv, 0, KD, D + fg * 512, 512) for fg in range(4)]
    for t in range(ntiles):
        specs += [(w_q, 0, KD, fg * 512, 512) for fg in range(4)]
        specs += [(w_o, 0, KD, fg * 512, 512) for fg in range(4)]
    ws.plan(specs)

    p.dma("sp", out=x[:, :, 0:NM], in_=io["memT"].rearrange("(k p) t -> p k t", p=128), writes=[xd])
    rmsnorm(p, C, x, xd, NM, mg, mgd, xn, xnd)
    nb = 0
    for fg in range(4):
        wt, wd = ws.next()
        for j in range(4):
            fc = fg * 4 + j
            b = nb % 2
            nb += 1
            for k in range(KD):
                p.op("pe", lambda e: e.matmul(psA[b][:, 0:NM], lhsT=wt[:, k, j * 128:(j + 1) * 128], rhs=xn[:, k, 0:NM], start=(k == 0), stop=(k == KD - 1)),
                     reads=[wd, xnd], writes=[psAd[b]])
            p.op("act", lambda e: e.copy(out=KT[:, fc, :], in_=psA[b][:, 0:NM]), reads=[psAd[b]], writes=[KTd])
    for fg in range(4):
        wt, wd = ws.next()
        for mc in range(2):
            b = nb % 2
            nb += 1
            for k in range(KD):
                p.op("pe", lambda e: e.matmul(psA[b][:], lhsT=xn[:, k, mc * 128:(mc + 1) * 128], rhs=wt[:, k, :], start=(k == 0), stop=(k == KD - 1)),
                     reads=[wd, xnd], writes=[psAd[b]])
            p.op("act", lambda e: e.copy(out=V[:, mc, fg * 512:(fg + 1) * 512], in_=psA[b][:]), reads=[psAd[b]], writes=[Vd])

    for t in range(ntiles):
        t0 = t * T
        p.dma("sp", out=x[:], in_=xT.rearrange("(k p) t -> p k t", p=128)[:, :, t0:t0 + T], writes=[xd])
        rmsnorm(p, C, x, xd, T, g, gd, xn, xnd)
        for fg in range(4):
            wt, wd = ws.next()
            for j in range(4):
                fc = fg * 4 + j
                b = nb % 2
                nb += 1
                for k in range(KD):
                    p.op("pe", lambda e: e.matmul(psA[b][:], lhsT=wt[:, k, j * 128:(j + 1) * 128], rhs=xn[:, k, :], start=(k == 0), stop=(k == KD - 1)),
                         reads=[wd, xnd], writes=[psAd[b]])
                p.op("act", lambda e: e.copy(out=qT[:, fc, :], in_=psA[b][:]), reads=[psAd[b]], writes=[qTd])
        for hh in range(4):
            b = hh % 2
            for sc in range(4):
                for dc in range(4):
                    p.op("pe", lambda e: e.matmul(psS[b][:, sc, :], lhsT=qT[:, hh * 4 + dc, sc * 128:(sc + 1) * 128], rhs=KT[:, hh * 4 + dc, :],
                                                  start=(dc == 0), stop=(dc == 3)),
                         reads=[qTd, KTd], writes=[psSd[b]])
            p.op("dve", lambda e: e.reduce_max(out=st[b][:, 0:4], in_=psS[b][:, :, :], axis=AX.X), reads=[psSd[b]], writes=[std[b]])
            p.op("dve", lambda e: e.tensor_tensor(out=P[b][:], in0=psS[b][:, :, :], in1=st[b][:, 0:4].unsqueeze(2).to_broadcast([128, 4, NM]), op=ALU.subtract),
                 reads=[psSd[b], std[b]], writes=[Pd[b]])
            p.op("act", lambda e: e.activation(out=P[b][:], in_=P[b][:], func=AF.Exp, scale=scale), reads=[Pd[b]], writes=[Pd[b]])
            p.op("dve", lambda e: e.reduce_sum(out=st[b][:, 4:8], in_=P[b][:], axis=AX.X), reads=[Pd[b]], writes=[std[b]])
            p.op("dve", lambda e: e.reciprocal(out=st[b][:, 8:12], in_=st[b][:, 4:8]), reads=[std[b]], writes=[std[b]])
            p.op("dve", lambda e: e.tensor_tensor(out=Pn[b][:], in0=P[b][:], in1=st[b][:, 8:12].unsqueeze(2).to_broadcast([128, 4, NM]), op=ALU.mult),
                 reads=[Pd[b], std[b]], writes=[Pnd[b]])
            for sc in range(4):
                for mc in range(2):
                    p.op("pe", lambda e: e.transpose(out=psT[:, (sc * 2 + mc) * 128:(sc * 2 + mc + 1) * 128], in_=Pn[b][:, sc, mc * 128:(mc + 1) * 128], identity=ident[:]),
                         reads=[Pnd[b], identd], writes=[psTd])
            p.op("act", lambda e: e.copy(out=PT[:, :, :].rearrange("p m (c s) -> p c m s", c=4),
                                         in_=psT[:, :].rearrange("p (c m s) -> p c m s", c=4, m=2)),
                 reads=[psTd], writes=[PTd])
            for dc in range(4):
                b = nb % 2
                nb += 1
                for mc in range(2):
                    p.op("pe", lambda e: e.matmul(psA[b][:], lhsT=V[:, mc, hh * 512 + dc * 128:hh * 512 + (dc + 1) * 128], rhs=PT[:, mc, :],
                                                  start=(mc == 0), stop=(mc == 1)),
                         reads=[Vd, PTd], writes=[psAd[b]])
                p.op("act", lambda e: e.copy(out=oT[:, hh * 4 + dc, :], in_=psA[b][:]), reads=[psAd[b]], writes=[oTd])
        for fg in range(4):
            wt, wd = ws.next()
            for j in range(4):
                do = fg * 4 + j
                b = nb % 2
                nb += 1
                for k in range(KD):
                    p.op("pe", lambda e: e.matmul(psA[b][:], lhsT=wt[:, k, j * 128:(j + 1) * 128], rhs=oT[:, k, :], start=(k == 0), stop=(k == KD - 1)),
                         reads=[wd, oTd], writes=[psAd[b]])
                p.op("dve", lambda e: e.tensor_tensor(out=x[:, do, :], in0=x[:, do, :], in1=psA[b][:], op=ALU.add),
                     reads=[psAd[b], xd], writes=[xd])
        p.dma("sp", out=yT.rearrange("(k p) t -> p k t", p=128)[:, :, t0:t0 + T], in_=x[:], reads=[xd])
        if t == ntiles - 1 and "hout" in io:
            p.dma("sp", out=io["hout"].ap().rearrange("p (k t) -> p k t", t=2), in_=x[:, :, T - 2:T], reads=[xd], writes=[io["houtd"]])


def build_xa(ntok):
    p = Prog()
    io = {
        "xT": p.dram("xT", [D, ntok], F32, "ExternalInput"),
        "memT": p.dram("memT", [D, 256], F32, "ExternalInput"),
        "g": p.dram("g", [128, KD], F32, "ExternalInput"),
        "mg": p.dram("mg", [128, KD], F32, "ExternalInput"),
        "ident": p.dram("ident", [128, 128], F32, "ExternalInput"),
        "w_q": p.dram("w_q", [D, D], F32, "ExternalInput"),
        "w_kv": p.dram("w_kv", [D, 2 * D], F32, "ExternalInput"),
        "w_o": p.dram("w_o", [D, D], F32, "ExternalInput"),
        "yT": p.dram("yT", [D, ntok], F32, "ExternalOutput"),
    }
    C = common(p)
    ws = WStream(p)
    stage_xa(p, C, ws, io, ntok // T)
    p.finish()
    return p


def xa_inputs(inp, layer):
    return {
        "g": vecT(inp["norm_cross_g"][layer]),
        "mg": vecT(inp["mem_norm_g"]),
        "ident": np.eye(128, dtype=np.float32),
        "w_q": np.ascontiguousarray(inp["xa_w_q"][layer], dtype=np.float32),
        "w_kv": np.ascontiguousarray(inp["xa_w_kv"][layer], dtype=np.float32),
        "w_o": np.ascontiguousarray(inp["xa_w_o"][layer], dtype=np.float32),
    }


def stage_gla(p, C, ws, io, ntiles, phase):
    xT = io["xT"]
    full = phase == 2
    g, gd = load_const(p, "gla_g", io["g"], [128, KD])
    ng, ngd = load_const(p, "gla_ng", io["ng"], [128, KD])
    bg, bgd = load_const(p, "gla_bg", io["bg"], [128, 8])
    mask, maskd = load_const(p, "gla_mask", io["mask"], [128, 128])
    ones = p.sb("gla_ones", [128, 128], F32)
    onesd = Dep()
    p.op("dve", lambda e: e.memset(ones[:], 1.0), writes=[onesd])
    one1 = p.sb("gla_one1", [128, 1], F32)
    p.op("dve", lambda e: e.memset(one1[:], 1.0), writes=[onesd])
    p.op("dve", lambda e: e.tensor_scalar(out=bg[:], in0=bg[:], scalar1=-1.0, scalar2=None, op0=ALU.mult), reads=[bgd], writes=[bgd])
    ident = p.sb("gla_ident", [128, 128], BF16)
    identd = Dep()
    p.dma("pool", out=ident[:], in_=io["ident"], writes=[identd])
    wgu = p.sb("gla_wgu", [16, 1024], BF16)
    wgud = Dep()
    p.dma("pool", out=wgu[:], in_=io["wgu"], writes=[wgud])

    x = p.sb("gla_x", [128, KD, T], F32)
    xd = Dep()
    xn = p.sb("gla_xn", [128, KD, T], BF16)
    xnd = Dep()
    oT, oTd = C["sq"], C["sqd"]
    S = p.sb("gla_S", [128, 8, 512], F32)
    Sd = [Dep() for _ in range(8)]
    Sb = p.sb("gla_Sb", [128, 8, 512], BF16)
    Sbd = [Dep() for _ in range(8)]
    Bt = p.sb("gla_Bt", [128, 8], F32)
    Btd = Dep()
    glow = p.sb("gla_glow", [16, T], BF16)
    glowd = Dep()
    cs = p.sb("gla_cs", [128, 2, T], F32)
    csd = Dep()
    lt = p.sb("gla_lt", [128, 2, T], F32)
    ltd = Dep()
    eb = p.sb("gla_eb", [128, 2, T], F32)
    ebd = Dep()
    enb = p.sb("gla_enb", [128, 2, T], F32)
    enbd = Dep()
    ebl = p.sb("gla_ebl", [128, 2, T], F32)
    ebld = Dep()
    sm = p.sb("gla_sm", [128, 2, 4, 2], F32)
    smd = Dep()
    qd_ = p.sb("gla_qd", [128, 2, T], BF16)
    qdd = Dep()
    kd_ = p.sb("gla_kd", [128, 2, T], BF16)
    kdd = Dep()
    ke = p.sb("gla_ke", [128, 2, T], BF16)
    ked = Dep()
    vt = p.sb("gla_vt", [128, 4, 512], BF16)
    vtd = Dep()
    rg = p.sb("gla_rg", [128, 4, T], BF16)
    rgd = Dep()
    rtmp = p.sb("gla_rtmp", [128, T], F32)
    rtmpd = Dep()
    osq = p.sb("gla_osq", [128, 512], F32)
    osqd = Dep()
    on = p.sb("gla_on", [128, 512], BF16)
    ond = Dep()
    st = p.sb("gla_st", [128, 4], F32)
    std = Dep()
    attm = p.sb("gla_attm", [128, 128], BF16)
    attmd = Dep()
    keT = p.sb("gla_keT", [128, 256], BF16)
    keTd = Dep()
    psA = [p.ps("gla_psA%d" % i, [128, T]) for i in range(2)]
    psAd = [Dep() for _ in range(2)]
    psAtt = p.ps("gla_psAtt", [128, T])
    psAttd = Dep()
    psO = p.ps("gla_psO", [128, T])
    psOd = Dep()
    psU = p.ps("gla_psU", [128, T])
    psUd = Dep()
    psT = p.ps("gla_psT", [128, 2 * T], BF16)
    psTd = Dep()
    psT2d = psTd

    w_in, w_out = io["w_in"], io["w_out"]
    specs = []
    for t in range(ntiles):
        specs.append((w_in, 0, KD, 6144, 16))
        for hh in range(4):
            if full:
                specs.append((w_in, 0, KD, hh * 256, 256))
            specs.append((w_in, 0, KD, 1024 + hh * 256, 256))
            specs += [(w_in, 0, KD, 2048 + hh * 512 + hv * 256, 256) for hv in range(2)]
            if full:
                specs += [(w_in, 0, KD, 4096 + hh * 512 + hv * 256, 256) for hv in range(2)]
        if full:
            specs += [(w_out, 0, KD, og * 256, 256) for og in range(8)]
    ws.plan(specs)

    if full:
        gath = io["gath"].ap().rearrange("(r p) w -> p r w", p=128)
        use, used = load_const(p, "gla_use", io["use"], [128, NCORES])
        bsm = p.sb("gla_bsm", [128, NCORES, 8], F32)
        bsmd = Dep()
        p.dma("sp", out=bsm[:], in_=gath[:, :, 4096:4104], writes=[bsmd])
        p.op("dve", lambda e: e.memset(S[:], 0.0), writes=Sd)
        tmp = x[:, 0:8, :]
        for r in range(NCORES):
            p.op("act", lambda e: e.activation(out=bsm[:, r, :], in_=bsm[:, r, :], func=AF.Exp, scale=use[:, r:r + 1]), reads=[bsmd, used], writes=[bsmd])
            p.dma("sp", out=tmp, in_=gath[:, r, 0:4096].rearrange("p (c v) -> p c v", c=8), writes=[xd])
            for c in range(8):
                p.op("dve", lambda e: e.tensor_scalar(out=S[:, c, :], in0=S[:, c, :], scalar1=bsm[:, r, c:c + 1], scalar2=None, op0=ALU.mult),
                     reads=[Sd[c], bsmd], writes=[Sd[c]])
                p.op("dve", lambda e: e.scalar_tensor_tensor(out=S[:, c, :], in0=x[:, c, :], scalar=use[:, r:r + 1], in1=S[:, c, :], op0=ALU.mult, op1=ALU.add),
                     reads=[Sd[c], used, xd], writes=[Sd[c]])
    else:
        p.op("dve", lambda e: e.memset(S[:], 0.0), writes=Sd)
        p.op("dve", lambda e: e.memset(Bt[:], 0.0), writes=[Btd])
    for c in range(8):
        p.op("act", lambda e: e.copy(out=Sb[:, c, :], in_=S[:, c, :]), reads=[Sd[c]], writes=[Sbd[c]])

    nb = 0
    for t in range(ntiles):
        t0 = t * T
        p.dma("sp", out=x[:], in_=xT.rearrange("(k p) t -> p k t", p=128)[:, :, t0:t0 + T], writes=[xd])
        rmsnorm(p, C, x, xd, T, g, gd, xn, xnd)
        wt, wd = ws.next()
        b = nb % 2
        nb += 1
        for k in range(KD):
            p.op("pe", lambda e: e.matmul(psA[b][0:16, :], lhsT=wt[:, k, 0:16], rhs=xn[:, k, :], start=(k == 0), stop=(k == KD - 1)),
                 reads=[wd, xnd], writes=[psAd[b]])
        p.op("act", lambda e: e.copy(out=glow[:], in_=psA[b][0:16, :]), reads=[psAd[b]], writes=[glowd])
        for hh in range(4):
            for kc in range(2):
                fc = hh * 2 + kc
                b = nb % 2
                nb += 1
                p.op("pe", lambda e: e.matmul(psA[b][:], lhsT=wgu[0:16, fc * 128:(fc + 1) * 128], rhs=glow[0:16, :], start=True, stop=True),
                     reads=[wgud, glowd], writes=[psAd[b]])
                p.op("act", lambda e: e.activation(out=lt[:, kc, :], in_=psA[b][:], func=AF.Exp, bias=bg[:, fc:fc + 1], scale=-1.0),
                     reads=[psAd[b], bgd], writes=[ltd])
                p.op("act", lambda e: e.activation(out=lt[:, kc, :], in_=lt[:, kc, :], func=AF.Ln, bias=one1[:, 0:1], scale=1.0),
                     reads=[ltd, onesd], writes=[ltd])
                for sc in range(4):
                    c0 = sc * 128
                    p.op("dve", lambda e: e.tensor_tensor_scan(out=cs[:, kc, c0:c0 + 128], data0=ones[:], data1=lt[:, kc, c0:c0 + 128], initial=0.0,
                                                               op0=ALU.mult, op1=ALU.add),
                         reads=[ltd, onesd], writes=[csd])
                    p.op("dve", lambda e: e.tensor_scalar(out=sm[:, kc, sc, 0:1], in0=cs[:, kc, c0 + 127:c0 + 128], scalar1=-1.0 / 16, scalar2=None, op0=ALU.mult),
                         reads=[csd], writes=[smd])
                    p.op("act", lambda e: e.activation(out=sm[:, kc, sc, 1:2], in_=sm[:, kc, sc, 0:1], func=AF.Exp), reads=[smd], writes=[smd])
                    p.op("act", lambda e: e.activation(out=ebl[:, kc, c0:c0 + 128], in_=cs[:, kc, c0:c0 + 128], func=AF.Exp, bias=sm[:, kc, sc, 0:1], scale=1.0 / 16),
                         reads=[csd, smd], writes=[ebld])
                    if not full:
                        p.op("dve", lambda e: e.tensor_tensor(out=Bt[:, fc:fc + 1], in0=Bt[:, fc:fc + 1], in1=sm[:, kc, sc, 0:1], op=ALU.add),
                             reads=[smd, Btd], writes=[Btd])
                if full:
                    p.op("act", lambda e: e.activation(out=eb[:, kc, :], in_=cs[:, kc, :], func=AF.Exp, scale=-1.0 / 16), reads=[csd], writes=[ebd])
                    p.op("act", lambda e: e.activation(out=enb[:, kc, :], in_=cs[:, kc, :], func=AF.Exp, scale=1.0 / 16), reads=[csd], writes=[enbd])
            if full:
                wt, wd = ws.next()
                for kc in range(2):
                    b = nb % 2
                    nb += 1
                    for k in range(KD):
                        p.op("pe", lambda e: e.matmul(psA[b][:], lhsT=wt[:, k, kc * 128:(kc + 1) * 128], rhs=xn[:, k, :], start=(k == 0), stop=(k == KD - 1)),
                             reads=[wd, xnd], writes=[psAd[b]])
                    p.op("dve", lambda e: e.scalar_tensor_tensor(out=qd_[:, kc, :], in0=psA[b][:], scalar=1.0 / 16, in1=eb[:, kc, :], op0=ALU.mult, op1=ALU.mult),
                         reads=[psAd[b], ebd], writes=[qdd])
            wt, wd = ws.next()
            for kc in range(2):
                b = nb % 2
                nb += 1
                for k in range(KD):
                    p.op("pe", lambda e: e.matmul(psA[b][:], lhsT=wt[:, k, kc * 128:(kc + 1) * 128], rhs=xn[:, k, :], start=(k == 0), stop=(k == KD - 1)),
                         reads=[wd, xnd], writes=[psAd[b]])
                if full:
                    p.op("dve", lambda e: e.tensor_tensor(out=kd_[:, kc, :], in0=psA[b][:], in1=enb[:, kc, :], op=ALU.mult),
                         reads=[psAd[b], enbd], writes=[kdd])
                p.op("dve", lambda e: e.tensor_tensor(out=ke[:, kc, :], in0=psA[b][:], in1=ebl[:, kc, :], op=ALU.mult),
                     reads=[psAd[b], ebld], writes=[ked])
            for hv in range(2):
                wt, wd = ws.next()
                for sc in range(4):
                    b = nb % 2
                    nb += 1
                    for k in range(KD):
                        p.op("pe", lambda e: e.matmul(psA[b][:, 0:256], lhsT=xn[:, k, sc * 128:(sc + 1) * 128], rhs=wt[:, k, :], start=(k == 0), stop=(k == KD - 1)),
                             reads=[wd, xnd], writes=[psAd[b]])
                    p.op("act", lambda e: e.copy(out=vt[:, sc, hv * 256:(hv + 1) * 256], in_=psA[b][:, 0:256]), reads=[psAd[b]], writes=[vtd])
            if full:
                for hv in range(2):
                    wt, wd = ws.next()
                    for j in range(2):
                        vc = hv * 2 + j
                        b = nb % 2
                        nb += 1
                        for k in range(KD):
                            p.op("pe", lambda e: e.matmul(psA[b][:], lhsT=wt[:, k, j * 128:(j + 1) * 128], rhs=xn[:, k, :], start=(k == 0), stop=(k == KD - 1)),
                                 reads=[wd, xnd], writes=[psAd[b]])
                        p.op("act", lambda e: e.activation(out=rtmp[:], in_=psA[b][:], func=AF.Silu), reads=[psAd[b]], writes=[rtmpd])
                        p.op("dve", lambda e: e.tensor_scalar(out=rg[:, vc, :], in0=rtmp[:], scalar1=ng[:, hh * 4 + vc:hh * 4 + vc + 1], scalar2=None, op0=ALU.mult),
                             reads=[rtmpd, ngd], writes=[rgd])
            for sc in range(4):
                c0 = sc * 128
                if full:
                    for kc in range(2):
                        p.op("pe", lambda e: e.matmul(psAtt[:, 0:128], lhsT=kd_[:, kc, c0:c0 + 128], rhs=qd_[:, kc, c0:c0 + 128], start=(kc == 0), stop=(kc == 1)),
                             reads=[kdd, qdd], writes=[psAttd])
                    p.op("dve", lambda e: e.tensor_tensor(out=attm[:], in0=psAtt[:, 0:128], in1=mask[:], op=ALU.mult), reads=[psAttd, maskd], writes=[attmd])
                    for kc in range(2):
                        p.op("pe", lambda e: e.matmul(psO[:], lhsT=qd_[:, kc, c0:c0 + 128], rhs=Sb[:, hh * 2 + kc, :], start=(kc == 0), stop=False),
                             reads=[qdd, Sbd[hh * 2 + kc]], writes=[psOd])
                    p.op("pe", lambda e: e.matmul(psO[:], lhsT=attm[:], rhs=vt[:, sc, :], start=False, stop=True), reads=[attmd, vtd], writes=[psOd])
                    p.op("act", lambda e: e.activation(out=osq[:], in_=psO[:], func=AF.Square), reads=[psOd], writes=[osqd])
                    p.op("dve", lambda e: e.reduce_sum(out=st[:, 0:1], in_=osq[:], axis=AX.X), reads=[osqd], writes=[std])
                    p.op("act", lambda e: e.activation(out=st[:, 1:2], in_=st[:, 0:1], func=AF.Sqrt, bias=C["eps"][:, 0:1], scale=1.0 / 512),
                         reads=[std, C["constd"]], writes=[std])
                    p.op("dve", lambda e: e.reciprocal(out=st[:, 2:3], in_=st[:, 1:2]), reads=[std], writes=[std])
                    p.op("dve", lambda e: e.tensor_scalar(out=on[:], in0=psO[:], scalar1=st[:, 2:3], scalar2=None, op0=ALU.mult),
                         reads=[psOd, std], writes=[ond])
                    for vc in range(4):
                        p.op("pe", lambda e: e.transpose(out=psT[:, vc * 128:(vc + 1) * 128], in_=on[:, vc * 128:(vc + 1) * 128], identity=ident[:]),
                             reads=[ond, identd], writes=[psTd])
                    p.op("dve", lambda e: e.tensor_tensor(out=oT[:, hh * 4:(hh + 1) * 4, c0:c0 + 128],
                                                          in0=psT[:, 0:512].rearrange("p (v s) -> p v s", v=4), in1=rg[:, :, c0:c0 + 128], op=ALU.mult),
                         reads=[psTd, rgd], writes=[oTd])
                for kc in range(2):
                    p.op("pe", lambda e: e.transpose(out=psT[:, 512 + kc * 128:512 + (kc + 1) * 128], in_=ke[:, kc, c0:c0 + 128], identity=ident[:]),
                         reads=[ked, identd], writes=[psT2d])
                p.op("act", lambda e: e.copy(out=keT[:], in_=psT[:, 512:768]), reads=[psT2d], writes=[keTd])
                for kc in range(2):
                    sidx = hh * 2 + kc
                    p.op("pe", lambda e: e.matmul(psU[:], lhsT=keT[:, kc * 128:(kc + 1) * 128], rhs=vt[:, sc, :], start=True, stop=True),
                         reads=[keTd, vtd], writes=[psUd])
                    p.op("dve", lambda e: e.scalar_tensor_tensor(out=S[:, sidx, :], in0=S[:, sidx, :], scalar=sm[:, kc, sc, 1:2], in1=psU[:],
                                                                  op0=ALU.mult, op1=ALU.add),
                         reads=[Sd[sidx], smd, psUd], writes=[Sd[sidx]])
                    p.op("act", lambda e: e.copy(out=Sb[:, sidx, :], in_=S[:, sidx, :]), reads=[Sd[sidx]], writes=[Sbd[sidx]])
        if full:
            for og in range(8):
                wt, wd = ws.next()
                for j in range(2):
                    do = og * 2 + j
                    b = nb % 2
                    nb += 1
                    for k in range(KD):
                        p.op("pe", lambda e: e.matmul(psA[b][:], lhsT=wt[:, k, j * 128:(j + 1) * 128], rhs=oT[:, k, :], start=(k == 0), stop=(k == KD - 1)),
                             reads=[wd, oTd], writes=[psAd[b]])
                    p.op("dve", lambda e: e.tensor_tensor(out=x[:, do, :], in0=x[:, do, :], in1=psA[b][:], op=ALU.add),
                         reads=[psAd[b], xd], writes=[xd])
            p.dma("sp", out=io["yT"].rearrange("(k p) t -> p k t", p=128)[:, :, t0:t0 + T], in_=x[:], reads=[xd])
    if not full:
        bn = io["bounce"].ap()
        p.dma("sp", out=bn[:, 0:4096].rearrange("p (c v) -> p c v", c=8), in_=S[:], reads=Sd, writes=[io["bounced"]])
        p.dma("sp", out=bn[:, 4096:4104], in_=Bt[:], reads=[Btd], writes=[io["bounced"]])


def build_gla(ntok, phase):
    p = Prog()
    io = {
        "xT": p.dram("xT", [D, ntok], F32, "ExternalInput"),
        "g": p.dram("g", [128, KD], F32, "ExternalInput"),
        "ng": p.dram("ng", [128, KD], F32, "ExternalInput"),
        "bg": p.dram("bg", [128, 8], F32, "ExternalInput"),
        "mask": p.dram("mask", [128, 128], F32, "ExternalInput"),
        "ident": p.dram("ident", [128, 128], F32, "ExternalInput"),
        "wgu": p.dram("wgu", [16, 1024], F32, "ExternalInput"),
        "w_in": p.dram("w_in", [D, 6160], F32, "ExternalInput"),
        "w_out": p.dram("w_out", [D, D], F32, "ExternalInput"),
    }
    if phase == 2:
        io["Sprev"] = p.dram("Sprev", [3, 128, 8, 512], F32, "ExternalInput")
        io["Bprev"] = p.dram("Bprev", [3, 128, 8], F32, "ExternalInput")
        io["yT"] = p.dram("yT", [D, ntok], F32, "ExternalOutput")
    else:
        io["Sout"] = p.dram("Sout", [128, 8, 512], F32, "ExternalOutput")
        io["Bout"] = p.dram("Bout", [128, 8], F32, "ExternalOutput")
    C = common(p)
    ws = WStream(p, slot_elems=4096)
    stage_gla(p, C, ws, io, ntok // T, phase)
    p.finish()
    return p


def gla_inputs(inp):
    return {
        "g": vecT(inp["norm_mix_g"][1]),
        "ng": vecT(inp["gla_norm_g"][0]),
        "bg": vecT(inp["gla_b_gate"][0]),
        "mask": np.triu(np.ones((128, 128), np.float32)),
        "ident": np.eye(128, dtype=np.float32),
        "wgu": np.ascontiguousarray(inp["gla_w_gate_up"][0], dtype=np.float32),
        "w_in": np.ascontiguousarray(inp["gla_w_in"][0], dtype=np.float32),
        "w_out": np.ascontiguousarray(inp["gla_w_out"][0], dtype=np.float32),
    }


TA = 256
TA1 = 512


def conv4(p, raw, rawd, acc, accd, cw, cb, cwd, c, n):
    p.op("dve", lambda e: e.tensor_scalar(out=acc[:, 0:n], in0=raw[:, 0:n], scalar1=cw[:, c, 0:1], scalar2=cb[:, c:c + 1], op0=ALU.mult, op1=ALU.add),
         reads=[rawd, cwd], writes=[accd])
    for kk in (1, 2, 3):
        p.op("dve", lambda e: e.scalar_tensor_tensor(out=acc[:, 0:n], in0=raw[:, kk:kk + n], scalar=cw[:, c, kk:kk + 1], in1=acc[:, 0:n],
                                                      op0=ALU.mult, op1=ALU.add),
             reads=[rawd, accd, cwd], writes=[accd])


def stage_ab(p, C, ws, io, ntiles, phase):
    off, bank = p.off, p.bank
    p.dry = True
    ws.recording = []
    _stage_ab(p, C, ws, io, ntiles, phase)
    p.dry = False
    p.off, p.bank = off, bank
    ws.plan(ws.recording)
    _stage_ab(p, C, ws, io, ntiles, phase)


def _interleave(gens):
    gens = list(gens)
    while gens:
        for g_ in list(gens):
            try:
                next(g_)
            except StopIteration:
                gens.remove(g_)


def _stage_ab(p, C, ws, io, ntiles, phase):
    full = phase == 2
    n = TA if full else TA1
    NCH = n // 128
    xT = io["xT"]
    g, gd = load_const(p, "ab_g", io["g"], [128, KD])
    lcw, lcwd = load_const(p, "ab_lcw", io["lcw"], [128, 16, 4])
    lcb, lcbd = load_const(p, "ab_lcb", io["lcb"], [128, 16])
    scw, scwd = load_const(p, "ab_scw", io["scw"], [128, 24, 4])
    scb, scbd = load_const(p, "ab_scb", io["scb"], [128, 24])
    ba, bad = load_const(p, "ab_ba", io["ba"], [128, 16])
    bx, bxd = load_const(p, "ab_bx", io["bx"], [128, 16])
    c8, c8d = load_const(p, "ab_lam", io["lam"], [128, 16])
    dtb, dtbd = load_const(p, "ab_dtb", io["dtb"], [128, 32])
    aneg, anegd = load_const(p, "ab_alog", io["alog"], [128, 32])
    Dv, Dvd = load_const(p, "ab_Dv", io["Dv"], [128, 16])
    sng, sngd = load_const(p, "ab_sng", io["sng"], [128, 16])
    mask, maskd = load_const(p, "ab_mask", io["mask"], [128, 128])
    U, Ud = load_const(p, "ab_U", io["U"], [128, 128])
    for d_ in (lcwd, scwd, bad, c8d, dtbd, anegd, Dvd, sngd, maskd, Ud):
        pass
    cd = Dep()
    ones = p.sb("ab_ones", [128, 128], F32)
    one1 = p.sb("ab_one1", [128, 1], F32)
    p.op("dve", lambda e: e.memset(ones[:], 1.0), writes=[cd])
    p.op("dve", lambda e: e.memset(one1[:], 1.0), writes=[cd])
    c16 = p.sb("ab_c16", [128, 16], F32)
    p.op("act", lambda e: e.activation(out=c8[:], in_=c8[:], func=AF.Exp, scale=-1.0), reads=[c8d], writes=[c8d])
    p.op("act", lambda e: e.activation(out=c8[:], in_=c8[:], func=AF.Ln, bias=one1[:, 0:1], scale=1.0), reads=[c8d, cd], writes=[c8d])
    p.op("dve", lambda e: e.tensor_scalar(out=c8[:], in0=c8[:], scalar1=-8.0, scalar2=None, op0=ALU.mult), reads=[c8d], writes=[c8d])
    p.op("dve", lambda e: e.tensor_scalar(out=c16[:], in0=c8[:], scalar1=2.0, scalar2=None, op0=ALU.mult), reads=[c8d], writes=[cd])
    p.op("act", lambda e: e.activation(out=aneg[:], in_=aneg[:], func=AF.Exp), reads=[anegd], writes=[anegd])
    p.op("dve", lambda e: e.tensor_scalar(out=aneg[:], in0=aneg[:], scalar1=-1.0, scalar2=None, op0=ALU.mult), reads=[anegd], writes=[anegd])
    ident = p.sb("ab_ident", [128, 128], BF16)
    identd = Dep()
    p.dma("pool", out=ident[:], in_=io["ident"], writes=[identd])
    wa = p.sb("ab_wa", [128, 16, 128], BF16)
    wx = p.sb("ab_wx", [128, 16, 128], BF16)
    wad = Dep()
    p.dma("pool", out=wa[:], in_=io["wa"], writes=[wad])
    p.dma("pool", out=wx[:], in_=io["wx"], writes=[wad])

    p.wait_deps([gd, lcwd, lcbd, scwd, scbd, bad, bxd, c8d, dtbd, anegd, Dvd, sngd, maskd, Ud, cd, identd, wad])

    x = p.sb("ab_x", [128, KD, n], F32)
    xd = Dep()
    xn = p.sb("ab_xn", [128, KD, n], BF16)
    xnd = Dep()
    xhal = p.sb("ab_xhal", [128, KD, 3], F32)
    xhald = Dep()
    xnh = p.sb("ab_xnh", [128, KD, 3], BF16)
    xnhd = Dep()
    hal = p.sb("ab_hal", [128, 40, 3], F32)
    hald = [Dep() for _ in range(40)]
    raw = [p.sb("ab_raw%d" % i, [128, n + 3], F32) for i in range(2)]
    rawd = [Dep() for _ in range(2)]
    acc = [p.sb("ab_acc%d" % i, [128, n], F32) for i in range(2)]
    accd = [Dep() for _ in range(2)]
    xbcT = p.sb("ab_xbcT", [128, 24, n], BF16)
    xbcTd = Dep()
    Xtok = p.sb("ab_Xtok", [128, NCH, D], BF16)
    Xtokd = Dep()
    Btok = p.sb("ab_Btok", [128, NCH, 512], BF16)
    Btokd = Dep()
    dt = p.sb("ab_dt", [128, NCH, 32], F32)
    dtd = Dep()
    dtA = p.sb("ab_dtA", [128, NCH, 32], F32)
    dtAd = Dep()
    cst = p.sb("ab_cst", [128, 64], F32)
    cstd = Dep()
    ecs = p.sb("ab_ecs", [128, 32], F32)
    etot = p.sb("ab_etot", [128, 32], F32)
    wend = p.sb("ab_wend", [128, 32], F32)
    smd = Dep()
    Xw = p.sb("ab_Xw", [128, D], BF16)
    Xwd = Dep()
    Xdt = p.sb("ab_Xdt", [128, D], BF16) if full else None
    Xdtd = Dep()
    R = p.sb("ab_R", [128, 8, 128], F32) if full else None
    Rd = Dep()
    LT = p.sb("ab_LT", [128, 8, 128], F32) if full else None
    LTd = Dep()
    mCB = p.sb("ab_mCB", [128, 128], F32) if full else None
    mCBd = Dep()
    ST = p.sb("ab_ST", [128, 8, 128], BF16) if full else None
    STd = Dep()
    t1 = p.sb("ab_t1", [128, 512], F32) if full else None
    t1d = Dep()
    ysb = p.sb("ab_ysb", [128, D], BF16) if full else None
    ysbd = Dep()
    yT = p.sb("ab_yT", [128, 16, n], BF16) if full else None
    yTd = Dep()
    St = p.sb("ab_St", [128, 4, 512], F32)
    Std = [Dep() for _ in range(4)]
    Stb = p.sb("ab_Stb", [128, 4, 512], BF16)
    Stbd = [Dep() for _ in range(4)]
    Tsum = p.sb("ab_Tsum", [128, 32], F32)
    Tsumd = Dep()
    hst = p.sb("ab_hst", [128, 16], F32)
    hstd = Dep()
    Ls = p.sb("ab_Ls", [128, 16], F32)
    Lsd = Dep()
    ycat = p.sb("ab_ycat", [128, 32, n], BF16) if full else None
    ycatd = Dep()
    ft = [(p.sb("ab_ft%d" % i, [128, n], F32) if (full or i >= 4) else None) for i in range(12)]
    ftd = [Dep() for _ in range(12)]
    xcb2 = [p.sb("ab_xcb%d" % i, [128, n], BF16) for i in range(2)]
    xcbd2 = [Dep() for _ in range(2)]
    sqc = [(p.sb("ab_sqc%d" % i, [128, n], BF16) if full else None) for i in range(2)]
    sqcd = [Dep() for _ in range(2)]
    psA = [p.ps("ab_psA%d" % i, [128, 512]) for i in range(2)]
    psAd = [Dep() for _ in range(2)]
    psT = p.ps("ab_psT", [128, 1024], BF16)
    psTd = Dep()
    psSeg = p.ps("ab_psSeg", [128, 1024])
    psSegd = Dep()
    psY = p.ps("ab_psY", [128, 512])
    psYd = Dep()
    psM = p.ps("ab_psM", [128, 512])
    psMd = Dep()
    pss, pssd = C["ps_ssq"], C["ps_ssqd"]

    w_in, w_out = io["w_in"], io["w_out"]

    if full:
        gath = io["gath"].ap().rearrange("(r p) w -> p r w", p=128)
        use, used = load_const(p, "ab_use", io["use"], [128, NCORES])
        gsm = p.sb("ab_gsm", [128, NCORES, 64], F32)
        gsmd = Dep()
        p.dma("sp", out=gsm[:], in_=gath[:, :, 2048:2112], writes=[gsmd])
        p.op("dve", lambda e: e.memset(St[:], 0.0), writes=Std)
        p.op("dve", lambda e: e.memset(hst[:], 0.0), writes=[hstd])
        tmpS = x[:, 0:8, :].rearrange("p a t -> p (a t)")
        St2 = St[:, :, :].rearrange("p g f -> p (g f)")
        for r in range(NCORES):
            p.op("act", lambda e: e.activation(out=gsm[:, r, 0:32], in_=gsm[:, r, 0:32], func=AF.Exp, scale=use[:, r:r + 1]), reads=[gsmd, used], writes=[gsmd])
            p.op("act", lambda e: e.activation(out=gsm[:, r, 48:64], in_=gsm[:, r, 48:64], func=AF.Exp, scale=use[:, r:r + 1]), reads=[gsmd, used], writes=[gsmd])
            p.op("dve", lambda e: e.tensor_tensor(out=hst[:], in0=hst[:], in1=gsm[:, r, 48:64], op=ALU.mult), reads=[gsmd, hstd], writes=[hstd])
            p.op("dve", lambda e: e.scalar_tensor_tensor(out=hst[:], in0=gsm[:, r, 32:48], scalar=use[:, r:r + 1], in1=hst[:], op0=ALU.mult, op1=ALU.add),
                 reads=[gsmd, used, hstd], writes=[hstd])
            p.dma("sp", out=tmpS, in_=gath[:, r, 0:2048], writes=[xd])
            p.op("dve", lambda e: e.tensor_tensor(out=St2.rearrange("p (h q) -> p h q", h=32), in0=St2.rearrange("p (h q) -> p h q", h=32),
                                                  in1=gsm[:, r, 0:32].unsqueeze(2).to_broadcast([128, 32, 64]), op=ALU.mult),
                 reads=[gsmd] + Std, writes=Std)
            p.op("dve", lambda e: e.scalar_tensor_tensor(out=St2, in0=tmpS, scalar=use[:, r:r + 1], in1=St2, op0=ALU.mult, op1=ALU.add),
                 reads=[xd, used] + Std, writes=Std)
    else:
        p.op("dve", lambda e: e.memset(St[:], 0.0), writes=Std)
        p.op("dve", lambda e: e.memset(Tsum[:], 0.0), writes=[Tsumd])
        p.op("dve", lambda e: e.memset(hst[:], 0.0), writes=[hstd])
        p.op("dve", lambda e: e.memset(Ls[:], 0.0), writes=[Lsd])
    for gq in range(4):
        p.op("act", lambda e: e.copy(out=Stb[:, gq, :], in_=St[:, gq, :]), reads=[Std[gq]], writes=[Stbd[gq]])

    p.dma("sp", out=xhal[:], in_=io["xh"].rearrange("(k p) t -> p k t", p=128), writes=[xhald])
    rmsnorm(p, C, xhal, xhald, 3, g, gd, xnh, xnhd)

    nb = 0
    nr = 0

    def proj_conv(wt, wd, j, hc, cw_, cb_, cwd_, cidx, t):
        nonlocal nb, nr
        b = nb % 2
        nb += 1
        rb = nr % 2
        nr += 1
        if t == 0:
            for k in range(KD):
                p.op("pe", lambda e: e.matmul(psM[:, 256 + 3 * hc:256 + 3 * hc + 3], lhsT=wt[:, k, j * 128:(j + 1) * 128], rhs=xnh[:, k, :],
                                              start=(k == 0), stop=(k == KD - 1)),
                     reads=[wd, xnhd], writes=[psMd])
            p.op("act", lambda e: e.copy(out=hal[:, hc, :], in_=psM[:, 256 + 3 * hc:256 + 3 * hc + 3]), reads=[psMd], writes=[hald[hc]])
        for k in range(KD):
            p.op("pe", lambda e: e.matmul(psA[b][:, 0:n], lhsT=wt[:, k, j * 128:(j + 1) * 128], rhs=xn[:, k, :], start=(k == 0), stop=(k == KD - 1)),
                 reads=[wd, xnd], writes=[psAd[b]])
        p.op("act", lambda e: e.copy(out=raw[rb][:, 3:n + 3], in_=psA[b][:, 0:n]), reads=[psAd[b]], writes=[rawd[rb]])
        p.op("act", lambda e: e.copy(out=raw[rb][:, 0:3], in_=hal[:, hc, :]), reads=[hald[hc]], writes=[rawd[rb]])
        conv4(p, raw[rb], rawd[rb], acc[rb], accd[rb], cw_, cb_, cwd_, cidx, n)
        p.op("act", lambda e: e.copy(out=hal[:, hc, :], in_=raw[rb][:, n:n + 3]), reads=[rawd[rb]], writes=[hald[hc]])
        return acc[rb], accd[rb]

    for t in range(ntiles):
        t0 = t * n
        p.dma("sp", out=x[:], in_=xT.rearrange("(k p) t -> p k t", p=128)[:, :, t0:t0 + n], writes=[xd])
        if not full:
            rmsnorm(p, C, x, xd, n, g, gd, xn, xnd)
        tc0 = t0 // 128
        dt_view = io["dt_s"].rearrange("(s p) h -> p s h", p=128)[:, tc0:tc0 + NCH, :]
        if full:
            p.dma("sp", out=dt[:], in_=dt_view, writes=[dtd])
        else:
            wt, wd = ws.next((w_in, 0, KD, 9216, 32))
            b = nb % 2
            nb += 1
        for sc in range(NCH if not full else 0):
            for k in range(KD):
                p.op("pe", lambda e: e.matmul(psA[b][:, sc * 32:(sc + 1) * 32], lhsT=xn[:, k, sc * 128:(sc + 1) * 128], rhs=wt[:, k, 0:32],
                                              start=(k == 0), stop=(k == KD - 1)),
                     reads=[wd, xnd], writes=[psAd[b]])
        if not full:
            p.op("dve", lambda e: e.tensor_tensor(out=dt[:], in0=psA[b][:, 0:NCH * 32].rearrange("p (s h) -> p s h", s=NCH),
                                                  in1=dtb[:, :].unsqueeze(1).to_broadcast([128, NCH, 32]), op=ALU.add),
                 reads=[psAd[b], dtbd], writes=[dtd])
            p.op("act", lambda e: e.activation(out=dt[:], in_=dt[:], func=AF.Exp), reads=[dtd], writes=[dtd])
            p.op("act", lambda e: e.activation(out=dt[:], in_=dt[:], func=AF.Ln, bias=one1[:, 0:1], scale=1.0), reads=[dtd, cd], writes=[dtd])
            p.dma("sp", out=dt_view, in_=dt[:], reads=[dtd])
        p.op("dve", lambda e: e.tensor_tensor(out=dtA[:], in0=dt[:], in1=aneg[:, :].unsqueeze(1).to_broadcast([128, NCH, 32]), op=ALU.mult),
             reads=[dtd, anegd], writes=[dtAd])
        xbc_view = io["xbc_s"].rearrange("(c p) t -> p c t", p=128)[:, :, t0:t0 + n]
        if full:
            p.dma("pool", out=xbcT[:], in_=xbc_view, writes=[xbcTd])
        for i in range(12 if not full else 0):
            wt, wd = ws.next((w_in, 0, KD, 6144 + i * 256, 256))
            for j in range(2):
                c = i * 2 + j
                a_, ad_ = proj_conv(wt, wd, j, 16 + c, scw, scb, scwd, c, t)
                p.op("act", lambda e: e.activation(out=xbcT[:, c, :], in_=a_[:, 0:n], func=AF.Silu), reads=[ad_], writes=[xbcTd])
        if not full:
            p.dma("pool", out=xbc_view, in_=xbcT[:], reads=[xbcTd])
        for sc in range(NCH):
            c0 = sc * 128
            for r in range(2):
                for cc in range(8):
                    c = r * 8 + cc
                    p.op("pe", lambda e: e.transpose(out=psT[:, cc * 128:(cc + 1) * 128], in_=xbcT[:, c, c0:c0 + 128], identity=ident[:]),
                         reads=[xbcTd, identd], writes=[psTd])
                p.op("act", lambda e: e.copy(out=Xtok[:, sc, r * 1024:(r + 1) * 1024], in_=psT[:, :]), reads=[psTd], writes=[Xtokd])
            for gq in range(4):
                p.op("pe", lambda e: e.transpose(out=psT[:, gq * 128:(gq + 1) * 128], in_=xbcT[:, 16 + gq, c0:c0 + 128], identity=ident[:]),
                     reads=[xbcTd, identd], writes=[psTd])
            p.op("act", lambda e: e.copy(out=Btok[:, sc, :], in_=psT[:, 0:512]), reads=[psTd], writes=[Btokd])
        def gen_ssd():
            nonlocal nb
            for sc in range(NCH):
                c0 = sc * 128
                p.op("pe", lambda e: e.matmul(psM[:, 0:32], lhsT=mask[:], rhs=dtA[:, sc, :], start=True, stop=True), reads=[maskd, dtAd], writes=[psMd])
                p.op("pe", lambda e: e.matmul(psM[:, 32:64], lhsT=ones[:], rhs=dtA[:, sc, :], start=True, stop=True), reads=[cd, dtAd], writes=[psMd])
                p.op("act", lambda e: e.copy(out=cst[:], in_=psM[:, 0:64]), reads=[psMd], writes=[cstd])
                p.op("act", lambda e: e.activation(out=ecs[:], in_=cst[:, 0:32], func=AF.Exp), reads=[cstd], writes=[smd])
                p.op("act", lambda e: e.activation(out=etot[:], in_=cst[:, 32:64], func=AF.Exp), reads=[cstd], writes=[smd])
                p.op("dve", lambda e: e.tensor_tensor(out=wend[:], in0=cst[:, 32:64], in1=cst[:, 0:32], op=ALU.subtract), reads=[cstd], writes=[smd])
                p.op("act", lambda e: e.activation(out=wend[:], in_=wend[:], func=AF.Exp), reads=[smd], writes=[smd])
                p.op("dve", lambda e: e.tensor_tensor(out=wend[:], in0=wend[:], in1=dt[:, sc, :], op=ALU.mult), reads=[smd, dtd], writes=[smd])
                if not full:
                    p.op("dve", lambda e: e.tensor_tensor(out=Tsum[:], in0=Tsum[:], in1=cst[:, 32:64], op=ALU.add), reads=[cstd, Tsumd], writes=[Tsumd])
                p.op("dve", lambda e: e.tensor_tensor(out=Xw[:, :].rearrange("p (h q) -> p h q", h=32), in0=Xtok[:, sc, :].rearrange("p (h q) -> p h q", h=32),
                                                      in1=wend[:, :].unsqueeze(2).to_broadcast([128, 32, 64]), op=ALU.mult),
                     reads=[Xtokd, smd], writes=[Xwd])
                if full:
                    p.op("dve", lambda e: e.tensor_tensor(out=Xdt[:, :].rearrange("p (h q) -> p h q", h=32), in0=Xtok[:, sc, :].rearrange("p (h q) -> p h q", h=32),
                                                          in1=dt[:, sc, :].unsqueeze(2).to_broadcast([128, 32, 64]), op=ALU.mult),
                         reads=[Xtokd, dtd], writes=[Xdtd])
                for gq in range(4):
                    if full:
                        p.op("dve", lambda e: e.tensor_tensor(out=R[:], in0=dtA[:, sc, gq * 8:(gq + 1) * 8].unsqueeze(2).to_broadcast([128, 8, 128]),
                                                              in1=mask[:, :].unsqueeze(1).to_broadcast([128, 8, 128]), op=ALU.mult),
                             reads=[dtAd, maskd], writes=[Rd])
                        for hf in range(2):
                            p.op("pe", lambda e: e.matmul(psSeg[:, hf * 512:(hf + 1) * 512], lhsT=U[:], rhs=R[:, hf * 4:(hf + 1) * 4, :].rearrange("p h i -> p (h i)"),
                                                          start=True, stop=True),
                                 reads=[Ud, Rd], writes=[psSegd])
                        for hf in range(2):
                            p.op("act", lambda e: e.activation(out=LT[:, hf * 4:(hf + 1) * 4, :].rearrange("p h i -> p (h i)"), in_=psSeg[:, hf * 512:(hf + 1) * 512], func=AF.Exp),
                                 reads=[psSegd], writes=[LTd])
                        p.op("pe", lambda e: e.matmul(psM[:, 128:256], lhsT=xbcT[:, 16 + gq, c0:c0 + 128], rhs=xbcT[:, 20 + gq, c0:c0 + 128], start=True, stop=True),
                             reads=[xbcTd], writes=[psMd])
                        p.op("dve", lambda e: e.tensor_tensor(out=mCB[:], in0=psM[:, 128:256], in1=mask[:], op=ALU.mult), reads=[psMd, maskd], writes=[mCBd])
                        p.op("dve", lambda e: e.tensor_tensor(out=ST[:], in0=LT[:], in1=mCB[:, :].unsqueeze(1).to_broadcast([128, 8, 128]), op=ALU.mult),
                             reads=[LTd, mCBd], writes=[STd])
                        for hl in range(8):
                            h_ = gq * 8 + hl
                            p.op("pe", lambda e: e.matmul(psY[:, hl * 64:(hl + 1) * 64], lhsT=ST[:, hl, :], rhs=Xdt[:, h_ * 64:(h_ + 1) * 64], start=True, stop=True),
                                 reads=[STd, Xdtd], writes=[psYd])
                        b = nb % 2
                        nb += 1
                        p.op("pe", lambda e: e.matmul(psA[b][:], lhsT=xbcT[:, 20 + gq, c0:c0 + 128], rhs=Stb[:, gq, :], start=True, stop=True),
                             reads=[xbcTd, Stbd[gq]], writes=[psAd[b]])
                        p.op("dve", lambda e: e.tensor_tensor(out=t1[:, :].rearrange("p (h q) -> p h q", h=8), in0=psA[b][:, :].rearrange("p (h q) -> p h q", h=8),
                                                              in1=ecs[:, gq * 8:(gq + 1) * 8].unsqueeze(2).to_broadcast([128, 8, 64]), op=ALU.mult),
                             reads=[psAd[b], smd], writes=[t1d])
                        p.op("dve", lambda e: e.tensor_tensor(out=ysb[:, gq * 512:(gq + 1) * 512], in0=t1[:], in1=psY[:], op=ALU.add),
                             reads=[t1d, psYd], writes=[ysbd])
                    b = nb % 2
                    nb += 1
                    p.op("pe", lambda e: e.matmul(psA[b][:], lhsT=Btok[:, sc, gq * 128:(gq + 1) * 128], rhs=Xw[:, gq * 512:(gq + 1) * 512], start=True, stop=True),
                         reads=[Btokd, Xwd], writes=[psAd[b]])
                    p.op("dve", lambda e: e.tensor_tensor(out=St[:, gq, :].rearrange("p (h q) -> p h q", h=8), in0=St[:, gq, :].rearrange("p (h q) -> p h q", h=8),
                                                          in1=etot[:, gq * 8:(gq + 1) * 8].unsqueeze(2).to_broadcast([128, 8, 64]), op=ALU.mult),
                         reads=[smd, Std[gq]], writes=[Std[gq]])
                    p.op("dve", lambda e: e.tensor_tensor(out=St[:, gq, :], in0=St[:, gq, :], in1=psA[b][:], op=ALU.add), reads=[psAd[b], Std[gq]], writes=[Std[gq]])
                    p.op("act", lambda e: e.copy(out=Stb[:, gq, :], in_=St[:, gq, :]), reads=[Std[gq]], writes=[Stbd[gq]])
                    yield
                if full:
                    for r in range(2):
                        for cc in range(8):
                            c = r * 8 + cc
                            p.op("pe", lambda e: e.transpose(out=psT[:, cc * 128:(cc + 1) * 128], in_=ysb[:, c * 128:(c + 1) * 128], identity=ident[:]),
                                 reads=[ysbd, identd], writes=[psTd])
                        p.op("act", lambda e: e.copy(out=yT[:, r * 8:(r + 1) * 8, c0:c0 + 128], in_=psT[:, :].rearrange("p (c s) -> p c s", c=8)),
                             reads=[psTd], writes=[yTd])

        def gen_lru():
            nonlocal nb
            for i in range(8):
                if not full:
                    wtx, wdx = ws.next((w_in, 0, KD, 2048 + i * 256, 256))
                if not full:
                    wtg, wdg = ws.next((w_in, 0, KD, i * 256, 256))
                for j in range(2):
                    hb = i * 2 + j
                    if full:
                        nonlocal nr
                        rb_ = nr % 2
                        nr += 1
                        xc, xcd = acc[rb_], accd[rb_]
                        p.dma("sp", out=xc[:, 0:n], in_=io["xc_s"][hb * 128:(hb + 1) * 128, t0:t0 + n], writes=[xcd])
                    else:
                        xc, xcd = proj_conv(wtx, wdx, j, hb, lcw, lcb, lcwd, hb, t)
                        p.dma("sp", out=io["xc_s"][hb * 128:(hb + 1) * 128, t0:t0 + n], in_=xc[:, 0:n], reads=[xcd])
                    par = hb % 2
                    xcb, xcbd = xcb2[par], xcbd2[par]
                    p.op("act", lambda e: e.copy(out=xcb[:], in_=xc[:, 0:n]), reads=[xcd], writes=[xcbd])
                    r_, rd_ = ft[4 + 4 * par], ftd[4 + 4 * par]
                    i_, id_ = ft[5 + 4 * par], ftd[5 + 4 * par]
                    a_, ad_ = ft[6 + 4 * par], ftd[6 + 4 * par]
                    s_, sd_ = ft[7 + 4 * par], ftd[7 + 4 * par]
                    b = nb % 2
                    nb += 1
                    p.op("pe", lambda e: e.matmul(psA[b][:, 0:n], lhsT=wa[:, hb, :], rhs=xcb[:], start=True, stop=True), reads=[wad, xcbd], writes=[psAd[b]])
                    p.op("act", lambda e: e.activation(out=r_[:], in_=psA[b][:, 0:n], func=AF.Sigmoid, bias=ba[:, hb:hb + 1], scale=1.0), reads=[psAd[b], bad], writes=[rd_])
                    b = nb % 2
                    nb += 1
                    p.op("pe", lambda e: e.matmul(psA[b][:, 0:n], lhsT=wx[:, hb, :], rhs=xcb[:], start=True, stop=True), reads=[wad, xcbd], writes=[psAd[b]])
                    p.op("act", lambda e: e.activation(out=i_[:], in_=psA[b][:, 0:n], func=AF.Sigmoid, bias=bx[:, hb:hb + 1], scale=1.0), reads=[psAd[b], bad], writes=[id_])
                    p.op("act", lambda e: e.activation(out=a_[:], in_=r_[:], func=AF.Exp, scale=c8[:, hb:hb + 1]), reads=[rd_, c8d], writes=[ad_])
                    p.op("act", lambda e: e.activation(out=s_[:], in_=r_[:], func=AF.Exp, scale=c16[:, hb:hb + 1]), reads=[rd_, cd], writes=[sd_])
                    p.op("act", lambda e: e.activation(out=s_[:], in_=s_[:], func=AF.Sqrt, bias=one1[:, 0:1], scale=-1.0), reads=[sd_, cd], writes=[sd_])
                    p.op("dve", lambda e: e.tensor_tensor(out=i_[:], in0=i_[:], in1=xc[:, 0:n], op=ALU.mult), reads=[id_, xcd], writes=[id_])
                    p.op("dve", lambda e: e.tensor_tensor(out=i_[:], in0=i_[:], in1=s_[:], op=ALU.mult), reads=[id_, sd_], writes=[id_])
                    p.op("dve", lambda e: e.tensor_tensor_scan(out=s_[:], data0=a_[:], data1=i_[:], initial=hst[:, hb:hb + 1], op0=ALU.mult, op1=ALU.add),
                         reads=[ad_, id_, hstd, sd_], writes=[sd_])
                    p.op("dve", lambda e: e.tensor_copy(out=hst[:, hb:hb + 1], in_=s_[:, n - 1:n]), reads=[sd_], writes=[hstd])
                    if not full:
                        p.op("dve", lambda e: e.reduce_sum(out=a_[:, 0:1], in_=r_[:], axis=AX.X), reads=[rd_, ad_], writes=[ad_])
                        p.op("dve", lambda e: e.tensor_tensor(out=Ls[:, hb:hb + 1], in0=Ls[:, hb:hb + 1], in1=a_[:, 0:1], op=ALU.add), reads=[ad_, Lsd], writes=[Lsd])
                        b = nb % 2
                        nb += 1
                        for k in range(KD):
                            p.op("pe", lambda e: e.matmul(psA[b][:, 0:n], lhsT=wtg[:, k, j * 128:(j + 1) * 128], rhs=xn[:, k, :], start=(k == 0), stop=(k == KD - 1)),
                                 reads=[wdg, xnd], writes=[psAd[b]])
                        p.op("act", lambda e: e.activation(out=r_[:], in_=psA[b][:, 0:n], func=AF.Gelu_apprx_tanh), reads=[psAd[b]], writes=[rd_])
                        p.dma("sp", out=io["gg_s"][hb * 128:(hb + 1) * 128, t0:t0 + n], in_=r_[:], reads=[rd_])
                    else:
                        p.dma("sp", out=r_[:], in_=io["gg_s"][hb * 128:(hb + 1) * 128, t0:t0 + n], writes=[rd_])
                        p.op("dve", lambda e: e.tensor_tensor(out=ycat[:, hb, :], in0=r_[:], in1=s_[:], op=ALU.mult), reads=[rd_, sd_], writes=[ycatd])
                    yield

        _interleave([gen_ssd(), gen_lru()])
        if not full:
            for i in range(8):
                wt, wd = ws.next((w_in, 0, KD, 4096 + i * 256, 256))
                for j in range(2):
                    c = i * 2 + j
                    b = nb % 2
                    nb += 1
                    for k in range(KD):
                        p.op("pe", lambda e: e.matmul(psA[b][:, 0:n], lhsT=wt[:, k, j * 128:(j + 1) * 128], rhs=xn[:, k, :], start=(k == 0), stop=(k == KD - 1)),
                             reads=[wd, xnd], writes=[psAd[b]])
                    f0, f0d = ft[4 + c % 8], ftd[4 + c % 8]
                    p.op("act", lambda e: e.activation(out=f0[:], in_=psA[b][:, 0:n], func=AF.Silu), reads=[psAd[b]], writes=[f0d])
                    p.dma("sp", out=io["gz_s"][c * 128:(c + 1) * 128, t0:t0 + n], in_=f0[:], reads=[f0d])
        if full:
            for i in range(8):
                for j in range(2):
                    c = i * 2 + j
                    f0, f0d = ft[c % 2], ftd[c % 2]
                    f1, f1d = ft[2 + c % 2], ftd[2 + c % 2]
                    p.dma("sp", out=f0[:], in_=io["gz_s"][c * 128:(c + 1) * 128, t0:t0 + n], writes=[f0d])
                    p.op("dve", lambda e: e.scalar_tensor_tensor(out=f1[:], in0=xbcT[:, c, :], scalar=Dv[:, c:c + 1], in1=yT[:, c, :], op0=ALU.mult, op1=ALU.add),
                         reads=[xbcTd, Dvd, yTd], writes=[f1d])
                    p.op("dve", lambda e: e.tensor_tensor(out=ycat[:, 16 + c, :], in0=f1[:], in1=f0[:], op=ALU.mult), reads=[f0d, f1d], writes=[ycatd])
                    sb_ = c % 2
                    p.op("act", lambda e: e.activation(out=sqc[sb_][:], in_=ycat[:, 16 + c, :], func=AF.Square), reads=[ycatd], writes=[sqcd[sb_]])
                    p.op("pe", lambda e: e.matmul(pss[:, 0:n], lhsT=C["ones_bf"][:], rhs=sqc[sb_][:], start=(c % 4 == 0), stop=(c % 4 == 3)),
                         reads=[sqcd[sb_], C["constd"]], writes=[pssd])
                    if c % 4 == 3:
                        rstd_from_ms(p, C, pss, pssd, n, scale=4.0)
                        for c2 in range(c - 3, c + 1):
                            p.op("dve", lambda e: e.scalar_tensor_tensor(out=ycat[:, 16 + c2, :], in0=ycat[:, 16 + c2, :], scalar=sng[:, c2:c2 + 1],
                                                                          in1=C["rstd"][:, 0:n], op0=ALU.mult, op1=ALU.mult),
                                 reads=[ycatd, sngd, C["rstdd"]], writes=[ycatd])
        if full:
            for og in range(8):
                w0, w0d = ws.next((w_out, 0, KD, og * 256, 256))
                w1, w1d = ws.next((w_out, KD, KD, og * 256, 256))
                for j in range(2):
                    do = og * 2 + j
                    b = nb % 2
                    nb += 1
                    for k in range(32):
                        wt_, wd_ = (w0, w0d) if k < 16 else (w1, w1d)
                        p.op("pe", lambda e: e.matmul(psA[b][:, 0:n], lhsT=wt_[:, k % 16, j * 128:(j + 1) * 128], rhs=ycat[:, k, :], start=(k == 0), stop=(k == 31)),
                             reads=[wd_, ycatd], writes=[psAd[b]])
                    p.op("dve", lambda e: e.tensor_tensor(out=x[:, do, :], in0=x[:, do, :], in1=psA[b][:, 0:n], op=ALU.add), reads=[psAd[b], xd], writes=[xd])
            p.dma("sp", out=io["yT"].rearrange("(k p) t -> p k t", p=128)[:, :, t0:t0 + n], in_=x[:], reads=[xd])
    if not full:
        p.op("dve", lambda e: e.tensor_tensor(out=Ls[:], in0=Ls[:], in1=c8[:], op=ALU.mult), reads=[Lsd, c8d], writes=[Lsd])
        bn = io["bounce"].ap()
        p.dma("sp", out=bn[:, 0:2048].rearrange("p (g f) -> p g f", g=4), in_=St[:], reads=Std, writes=[io["bounced"]])
        p.dma("sp", out=bn[:, 2048:2080], in_=Tsum[:], reads=[Tsumd], writes=[io["bounced"]])
        p.dma("sp", out=bn[:, 2080:2096], in_=hst[:], reads=[hstd], writes=[io["bounced"]])
        p.dma("sp", out=bn[:, 2096:2112], in_=Ls[:], reads=[Lsd], writes=[io["bounced"]])


def build_ab(ntok, phase):
    p = Prog()
    io = {"xT": p.dram("xT", [D, ntok], F32, "ExternalInput"), "xh": p.dram("xh", [D, 3], F32, "ExternalInput")}
    for nm, shp in (("g", [128, KD]), ("lcw", [128, 16, 4]), ("lcb", [128, 16]), ("scw", [128, 24, 4]), ("scb", [128, 24]), ("ba", [128, 16]),
                    ("bx", [128, 16]), ("lam", [128, 16]), ("dtb", [128, 32]), ("alog", [128, 32]), ("Dv", [128, 16]), ("sng", [128, 16]),
                    ("mask", [128, 128]), ("U", [128, 128]), ("ident", [128, 128]), ("wa", [128, 16, 128]), ("wx", [128, 16, 128]),
                    ("w_in", [D, 9248]), ("w_out", [2 * D, D])):
        io[nm] = p.dram(nm, shp, F32, "ExternalInput")
    if phase == 2:
        io["Sprev"] = p.dram("Sprev", [3, 128, 4, 512], F32, "ExternalInput")
        io["Tprev"] = p.dram("Tprev", [3, 128, 32], F32, "ExternalInput")
        io["Hprev"] = p.dram("Hprev", [3, 128, 2, 16], F32, "ExternalInput")
        io["yT"] = p.dram("yT", [D, ntok], F32, "ExternalOutput")
    else:
        io["Sout"] = p.dram("Sout", [128, 4, 512], F32, "ExternalOutput")
        io["Tout"] = p.dram("Tout", [128, 32], F32, "ExternalOutput")
        io["Hout"] = p.dram("Hout", [128, 2, 16], F32, "ExternalOutput")
    C = common(p)
    ws = WStream(p, slot_elems=4096)
    stage_ab(p, C, ws, io, ntok // TA, phase)
    p.finish()
    return p


def chunkT(w, nch, kk):
    return np.ascontiguousarray(np.asarray(w, np.float32).T.reshape(nch, 128, kk).transpose(1, 0, 2))


def ab_inputs(inp):
    f = lambda a: np.ascontiguousarray(a, dtype=np.float32)
    return {
        "g": vecT(inp["norm_mix_g"][0]),
        "lcw": chunkT(inp["lru_conv_w"][0], 16, 4), "lcb": vecT(inp["lru_conv_b"][0]),
        "scw": chunkT(inp["ssd_conv_w"][0], 24, 4), "scb": vecT(inp["ssd_conv_b"][0]),
        "ba": vecT(inp["lru_b_a"][0]), "bx": vecT(inp["lru_b_x"][0]), "lam": vecT(inp["lru_lambda"][0]),
        "dtb": f(np.tile(np.asarray(inp["ssd_dt_bias"][0], np.float32)[None, :], (128, 1))),
        "alog": f(np.tile(np.asarray(inp["ssd_a_log"][0], np.float32)[None, :], (128, 1))),
        "Dv": vecT(np.repeat(np.asarray(inp["ssd_d"][0], np.float32), 64)),
        "sng": vecT(inp["ssd_norm_g"][0]),
        "mask": np.triu(np.ones((128, 128), np.float32)),
        "U": np.tril(np.ones((128, 128), np.float32), -1),
        "ident": np.eye(128, dtype=np.float32),
        "wa": f(np.asarray(inp["lru_w_a"][0]).transpose(1, 0, 2)),
        "wx": f(np.asarray(inp["lru_w_x"][0]).transpose(1, 0, 2)),
        "w_in": f(inp["ab_w_in"][0]), "w_out": f(inp["ab_w_out"][0]),
    }


NSEG = 4

_AB_SHAPES = (("g", [128, KD]), ("lcw", [128, 16, 4]), ("lcb", [128, 16]), ("scw", [128, 24, 4]), ("scb", [128, 24]), ("ba", [128, 16]),
              ("bx", [128, 16]), ("lam", [128, 16]), ("dtb", [128, 32]), ("alog", [128, 32]), ("Dv", [128, 16]), ("sng", [128, 16]),
              ("mask", [128, 128]), ("U", [128, 128]), ("ident", [128, 128]), ("wa", [128, 16, 128]), ("wx", [128, 16, 128]),
              ("w_in", [D, 9248]), ("w_out", [2 * D, D]))
_GLA_SHAPES = (("g", [128, KD]), ("ng", [128, KD]), ("bg", [128, 8]), ("mask", [128, 128]), ("ident", [128, 128]), ("wgu", [16, 1024]),
               ("w_in", [D, 6160]), ("w_out", [D, D]))
_XA_SHAPES = (("g", [128, KD]), ("mg", [128, KD]), ("ident", [128, 128]), ("w_q", [D, D]), ("w_kv", [D, 2 * D]), ("w_o", [D, D]))
_FFN_SHAPES = (("g", [128, KD]), ("cw", [128, 96, 3]), ("cb", [128, 96]), ("fg", [128, KD]), ("w_in", [D, 12288]), ("w_out", [6144, D]))


def build_fused(nt):
    p = Prog()
    nc = p.nc

    def ext(prefix, shapes):
        return {nm: p.dram(prefix + nm, shp, F32, "ExternalInput") for nm, shp in shapes}

    xT = p.dram("xT", [D, nt], F32, "ExternalInput")
    xh3 = p.dram("xh3", [D, 3], F32, "ExternalInput")
    memT = p.dram("memT", [D, 256], F32, "ExternalInput")
    use = p.dram("use", [128, NCORES], F32, "ExternalInput")
    usel = p.dram("usel", [128, NCORES], F32, "ExternalInput")
    yT = p.dram("yT", [D, nt], F32, "ExternalOutput")
    ab = ext("ab_", _AB_SHAPES)
    gla = ext("gla_", _GLA_SHAPES)
    xa = [ext("xa%d_" % l, _XA_SHAPES) for l in range(2)]
    ffn = [ext("ffn%d_" % l, _FFN_SHAPES) for l in range(2)]
    actA = nc.dram_tensor("actA", [D, nt], F32).ap()
    actB = nc.dram_tensor("actB", [D, nt], F32).ap()
    bn_ab = nc.dram_tensor("bn_ab", [128, 2112], F32)
    ga_ab = nc.dram_tensor("ga_ab", [NCORES * 128, 2112], F32)
    bn_g = nc.dram_tensor("bn_g", [128, 4104], F32)
    ga_g = nc.dram_tensor("ga_g", [NCORES * 128, 4104], F32)
    bn_h = [nc.dram_tensor("bn_h%d" % l, [128, 32], F32) for l in range(2)]
    ga_h = [nc.dram_tensor("ga_h%d" % l, [NCORES * 128, 32], F32) for l in range(2)]

    ab["dt_s"] = nc.dram_tensor("dt_s", [nt, 32], F32).ap()
    ab["xbc_s"] = nc.dram_tensor("xbc_s", [3072, nt], F32).ap()
    ab["xc_s"] = nc.dram_tensor("xc_s", [D, nt], F32).ap()
    ab["gz_s"] = nc.dram_tensor("gz_s", [D, nt], F32).ap()
    ab["gg_s"] = nc.dram_tensor("gg_s", [D, nt], F32).ap()
    C = common(p)
    bd = Dep()
    with p.scope():
        stage_ab(p, C, WStream(p, slot_elems=4096), dict(ab, xT=xT, xh=xh3, bounce=bn_ab, bounced=bd), nt // TA1, 1)
    p.allgather(bn_ab, ga_ab)
    p.barrier()
    with p.scope():
        stage_ab(p, C, WStream(p, slot_elems=4096), dict(ab, xT=xT, xh=xh3, gath=ga_ab, use=use, yT=actA), nt // TA, 2)
    cur, nxt = actA, actB
    for l in range(2):
        if l == 1:
            with p.scope():
                stage_gla(p, C, WStream(p, slot_elems=4096), dict(gla, xT=cur, bounce=bn_g, bounced=bd), nt // T, 1)
            p.allgather(bn_g, ga_g)
            p.barrier()
            with p.scope():
                stage_gla(p, C, WStream(p, slot_elems=4096), dict(gla, xT=cur, gath=ga_g, use=use, yT=nxt), nt // T, 2)
            cur, nxt = nxt, cur
        with p.scope():
            stage_xa(p, C, WStream(p), dict(xa[l], xT=cur, memT=memT, yT=nxt, hout=bn_h[l], houtd=bd), nt // T)
        cur, nxt = nxt, cur
        p.allgather(bn_h[l], ga_h[l])
        p.barrier()
        with p.scope():
            stage_ffn(p, C, WStream(p), dict(ffn[l], xT=cur, gath_h=ga_h[l], usel=usel, yT=(yT if l == 1 else nxt)), nt // T, l == 1)
        cur, nxt = nxt, cur
    p.finish()
    return p


def make_maps(inp, nt):
    inp = {k: np.asarray(v) for k, v in inp.items()}
    x = np.asarray(inp["x"], np.float32)
    mem = np.asarray(inp["mem"], np.float32)
    shared = {}
    for pre, d in (("ab_", ab_inputs(inp)), ("gla_", gla_inputs(inp))):
        shared.update({pre + k: v for k, v in d.items()})
    for l in range(2):
        shared.update({"xa%d_%s" % (l, k): v for k, v in xa_inputs(inp, l).items()})
        shared.update({"ffn%d_%s" % (l, k): v for k, v in ffn_inputs(inp, l).items()})
    maps = []
    for c in range(NCORES):
        b, q = divmod(c, NSEG)
        m = dict(shared)
        m["xT"] = np.ascontiguousarray(x[b, q * nt:(q + 1) * nt].T)
        m["xh3"] = np.ascontiguousarray(x[b, q * nt - 3:q * nt].T) if q > 0 else np.zeros((D, 3), np.float32)
        m["memT"] = np.ascontiguousarray(mem[b].T)
        use = np.zeros((128, NCORES), np.float32)
        use[:, b * NSEG:c] = 1.0
        usel = np.zeros((128, NCORES), np.float32)
        if q > 0:
            usel[:, c - 1] = 1.0
        m["use"] = use
        m["usel"] = usel
        maps.append(m)
    return maps


def kernel_impl(inp, nt):
    maps = make_maps(inp, nt)
    p = build_fused(nt)
    res = run_bass_kernel_spmd(p.nc, maps, core_ids=list(range(NCORES))).results
    out = np.empty((2, NSEG * nt, D), np.float32)
    for c in range(NCORES):
        b, q = divmod(c, NSEG)
        out[b, q * nt:(q + 1) * nt] = res[c]["yT"].T
    return out


def build_stage(kind, nt, layer=0, phase=0):
    p = Prog()
    nc = p.nc

    def ext(shapes):
        return {nm: p.dram(nm, shp, F32, "ExternalInput") for nm, shp in shapes}

    C = common(p)
    if kind == "ab":
        io = ext(_AB_SHAPES)
        io["xT"] = p.dram("xT", [D, nt], F32, "ExternalInput")
        io["xh"] = p.dram("xh3", [D, 3], F32, "ExternalInput")
        kd_ = "ExternalOutput" if phase == 1 else "ExternalInput"
        io["dt_s"] = p.dram("dt_s", [nt, 32], F32, kd_)
        io["xbc_s"] = p.dram("xbc_s", [3072, nt], F32, kd_)
        io["xc_s"] = p.dram("xc_s", [D, nt], F32, kd_)
        io["gz_s"] = p.dram("gz_s", [D, nt], F32, kd_)
        io["gg_s"] = p.dram("gg_s", [D, nt], F32, kd_)
        if phase == 1:
            io["bounce"] = nc.dram_tensor("bounce", [128, 2112], F32, kind="ExternalOutput")
            io["bounced"] = Dep()
        else:
            io["gath"] = nc.dram_tensor("gath", [NCORES * 128, 2112], F32, kind="ExternalInput")
            io["use"] = p.dram("use", [128, NCORES], F32, "ExternalInput")
            io["yT"] = p.dram("yT", [D, nt], F32, "ExternalOutput")
        stage_ab(p, C, WStream(p, slot_elems=4096), io, nt // (TA1 if phase == 1 else TA), phase)
    elif kind == "gla":
        io = ext(_GLA_SHAPES)
        io["xT"] = p.dram("xT", [D, nt], F32, "ExternalInput")
        if phase == 1:
            io["bounce"] = nc.dram_tensor("bounce", [128, 4104], F32, kind="ExternalOutput")
            io["bounced"] = Dep()
        else:
            io["gath"] = nc.dram_tensor("gath", [NCORES * 128, 4104], F32, kind="ExternalInput")
            io["use"] = p.dram("use", [128, NCORES], F32, "ExternalInput")
            io["yT"] = p.dram("yT", [D, nt], F32, "ExternalOutput")
        stage_gla(p, C, WStream(p, slot_elems=4096), io, nt // T, phase)
    elif kind == "xa":
        io = ext(_XA_SHAPES)
        io["xT"] = p.dram("xT", [D, nt], F32, "ExternalInput")
        io["memT"] = p.dram("memT", [D, 256], F32, "ExternalInput")
        io["yT"] = p.dram("yT", [D, nt], F32, "ExternalOutput")
        io["hout"] = nc.dram_tensor("hout", [128, 32], F32, kind="ExternalOutput")
        io["houtd"] = Dep()
        stage_xa(p, C, WStream(p), io, nt // T)
    else:
        io = ext(_FFN_SHAPES)
        io["xT"] = p.dram("xT", [D, nt], F32, "ExternalInput")
        io["gath_h"] = nc.dram_tensor("gath_h", [NCORES * 128, 32], F32, kind="ExternalInput")
        io["usel"] = p.dram("usel", [128, NCORES], F32, "ExternalInput")
        io["yT"] = p.dram("yT", [D, nt], F32, "ExternalOutput")
        stage_ffn(p, C, WStream(p), io, nt // T, layer == 1)
    p.finish()
    return p


def kernel_unfused(inp, nt):
    maps = make_maps(inp, nt)

    def run(p, extra):
        keys = None
        ms = []
        for c in range(NCORES):
            m = dict(extra[c])
            ms.append(m)
        return run_bass_kernel_spmd(p.nc, ms, core_ids=list(range(NCORES))).results

    def sub(pre, c, more):
        m = {k[len(pre):]: v for k, v in maps[c].items() if k.startswith(pre)}
        m.update(more)
        return m

    xTs = [maps[c]["xT"] for c in range(NCORES)]
    r = run(build_stage("ab", nt, phase=1), [sub("ab_", c, dict(xT=xTs[c], xh3=maps[c]["xh3"])) for c in range(NCORES)])
    gath = np.ascontiguousarray(np.concatenate([r[c]["bounce"] for c in range(NCORES)], axis=0))
    r = run(build_stage("ab", nt, phase=2), [sub("ab_", c, dict(xT=xTs[c], xh3=maps[c]["xh3"], gath=gath, use=maps[c]["use"],
                                                              dt_s=r[c]["dt_s"], xbc_s=r[c]["xbc_s"], xc_s=r[c]["xc_s"], gz_s=r[c]["gz_s"], gg_s=r[c]["gg_s"])) for c in range(NCORES)])
    xTs = [r[c]["yT"] for c in range(NCORES)]
    for l in range(2):
        if l == 1:
            r = run(build_stage("gla", nt, phase=1), [sub("gla_", c, dict(xT=xTs[c])) for c in range(NCORES)])
            gath = np.ascontiguousarray(np.concatenate([r[c]["bounce"] for c in range(NCORES)], axis=0))
            r = run(build_stage("gla", nt, phase=2), [sub("gla_", c, dict(xT=xTs[c], gath=gath, use=maps[c]["use"])) for c in range(NCORES)])
            xTs = [r[c]["yT"] for c in range(NCORES)]
        r = run(build_stage("xa", nt), [sub("xa%d_" % l, c, dict(xT=xTs[c], memT=maps[c]["memT"])) for c in range(NCORES)])
        xTs = [r[c]["yT"] for c in range(NCORES)]
        gath_h = np.ascontiguousarray(np.concatenate([r[c]["hout"] for c in range(NCORES)], axis=0))
        r = run(build_stage("ffn", nt, layer=l), [sub("ffn%d_" % l, c, dict(xT=xTs[c], gath_h=gath_h, usel=maps[c]["usel"])) for c in range(NCORES)])
        xTs = [r[c]["yT"] for c in range(NCORES)]
    out = np.empty((2, NSEG * nt, D), np.float32)
    for c in range(NCORES):
        b, q = divmod(c, NSEG)
        out[b, q * nt:(q + 1) * nt] = xTs[c].T
    return out


FUSED = False


def kernel(**inp):
    if FUSED:
        return kernel_impl(inp, 4096)
    return kernel_unfused(inp, 4096)
```

```python
import contextlib
import numpy as np
import concourse.bass as bass
import concourse.mybir as mybir
from concourse.bass_utils import run_bass_kernel_spmd

F32 = mybir.dt.float32
BF16 = mybir.dt.bfloat16
AF = mybir.ActivationFunctionType
ALU = mybir.AluOpType
AX = mybir.AxisListType

D = 2048
KD = 16
T = 512
EPS = 1e-6
NCORES = 8


class Dep:
    __slots__ = ("w", "r")

    def __init__(self):
        self.w = {}
        self.r = {}


class _Rec:
    def __init__(self, lst):
        self.lst = lst

    def __getattr__(self, name):
        def f(*a, **kw):
            self.lst.append(["ins", name, a, kw, None, None])
            return self
        return f


class Prog:
    NDMA = 24
    NCC = 4
    ARENA = 52000

    def __init__(self):
        self.nc = bass.Bass("TRN2", target_bir_lowering=False)
        nc = self.nc
        self.es = contextlib.ExitStack()
        self.engs = {"pe": nc.tensor, "act": nc.scalar, "dve": nc.vector, "pool": nc.gpsimd, "sp": nc.sync}
        self.rec = {e: [] for e in self.engs}
        self.prox = {e: _Rec(self.rec[e]) for e in self.engs}
        self.sems = {}
        self.cnt = {}
        for e in self.engs:
            self.sems[e] = self.es.enter_context(nc.semaphore("sem_" + e))
            self.cnt[e] = 0
        for fam in ("d", "g"):
            for i in range(self.NDMA):
                self.sems[(fam, i)] = self.es.enter_context(nc.semaphore("sem%s%d" % (fam, i)))
                self.cnt[(fam, i)] = 0
        for i in range(self.NCC):
            self.sems[("c", i)] = self.es.enter_context(nc.semaphore("semcc%d" % i))
            self.cnt[("c", i)] = 0
        self.ncc = 0
        self.dnext = {"d": 0, "g": 0}
        self.known = {e: {} for e in self.engs}
        self.nins = 0
        self.arena = self.es.enter_context(nc.sbuf_tensor("arena", [128, self.ARENA], F32))
        self.psum = self.es.enter_context(nc.psum_tensor("psum_all", [128, 4096], F32))
        self.off = 0
        self.bank = 0

    @staticmethod
    def _view(base, shape, dt):
        v = base if dt == F32 else base.bitcast(dt)
        n = 1
        for d_ in shape[1:]:
            n *= d_
        v = v[0:shape[0], 0:n]
        if len(shape) == 3:
            v = v.rearrange("p (a b) -> p a b", a=shape[1])
        elif len(shape) == 4:
            v = v.rearrange("p (a b c) -> p a b c", a=shape[1], b=shape[2])
        return v

    def sb(self, name, shape, dt):
        n = 1
        for d_ in shape[1:]:
            n *= d_
        words = (n * (4 if dt == F32 else 2) + 3) // 4
        words = (words + 15) // 16 * 16
        assert self.off + words <= self.ARENA, ("SBUF arena overflow", name, self.off, words)
        base = self.arena[:, self.off:self.off + words]
        self.off += words
        self.peak = max(getattr(self, "peak", 0), self.off)
        return self._view(base, list(shape), dt)

    def ps(self, name, shape, dt=F32):
        n = 1
        for d_ in shape[1:]:
            n *= d_
        nb = (n * (4 if dt == F32 else 2) + 2047) // 2048
        assert self.bank + nb <= 8, ("PSUM overflow", name)
        base = self.psum[:, self.bank * 512:(self.bank + nb) * 512]
        self.bank += nb
        return self._view(base, list(shape), dt)

    def dram(self, name, shape, dt, kind):
        return self.nc.dram_tensor(name, list(shape), dt, kind=kind).ap()

    @contextlib.contextmanager
    def scope(self):
        off, bank = self.off, self.bank
        try:
            yield
        finally:
            self.barrier()
            self.off, self.bank = off, bank

    SAME_ENGINE_WAITS = True

    def _waits(self, eng, reads, writes):
        need = {}
        if not self.SAME_ENGINE_WAITS and eng in ("act", "dve"):
            for d in list(reads) + list(writes):
                for k, v in d.w.items():
                    if k != eng and need.get(k, 0) < v:
                        need[k] = v
            for d in writes:
                for k, v in d.r.items():
                    if k != eng and need.get(k, 0) < v:
                        need[k] = v
            return need
        for d in reads:
            for k, v in d.w.items():
                if eng == "pe" and k == "pe":
                    continue
                if need.get(k, 0) < v:
                    need[k] = v
        for d in writes:
            for k, v in d.w.items():
                if eng == "pe" and k == "pe":
                    continue
                if need.get(k, 0) < v:
                    need[k] = v
            for k, v in d.r.items():
                if k == eng:
                    continue
                if need.get(k, 0) < v:
                    need[k] = v
        return need

    def _emit_waits(self, eng, need):
        kn = self.known[eng]
        for k, v in need.items():
            if kn.get(k, 0) >= v:
                continue
            self.rec[eng].append(["wait", k, v])
            kn[k] = v
            self.nins += 1

    dry = False

    def op(self, eng, fn, reads=(), writes=()):
        if self.dry:
            return
        need = self._waits(eng, reads, writes)
        self._emit_waits(eng, need)
        n0 = len(self.rec[eng])
        fn(self.prox[eng])
        assert len(self.rec[eng]) == n0 + 1
        self.cnt[eng] += 1
        v = self.cnt[eng]
        self.rec[eng][-1][4] = eng
        self.rec[eng][-1][5] = 1
        self.nins += 1
        for d in reads:
            d.r[eng] = v
        for d in writes:
            d.w = {eng: v}
            d.r = {}

    ALLPOOL = False

    def dma(self, q, out, in_, reads=(), writes=()):
        if self.dry:
            return
        if self.ALLPOOL:
            q = "pool"
        fam = "g" if q == "pool" else "d"
        i = self.dnext[fam]
        self.dnext[fam] = (i + 1) % self.NDMA
        sk = (fam, i)
        need = self._waits(q, reads, writes)
        if self.cnt[sk] > 0:
            need[sk] = max(need.get(sk, 0), self.cnt[sk])
        self._emit_waits(q, need)
        self.cnt[sk] += 16
        v = self.cnt[sk]
        self.rec[q].append(["ins", "dma_start", (), dict(out=out, in_=in_), sk, 16])
        self.nins += 1
        for d in reads:
            d.r[sk] = v
        for d in writes:
            d.w = {sk: v}
            d.r = {}

    def barrier(self):
        if self.dry:
            return
        allv = {k: v for k, v in self.cnt.items() if v > 0}
        for eng in self.engs:
            self._emit_waits(eng, {k: v for k, v in allv.items() if k != eng})

    def allgather(self, src, dst):
        sk = ("c", self.ncc)
        self.ncc += 1
        self.rec["pool"].append(["cc", src, dst, sk])
        self.cnt[sk] = 1
        self.nins += 1

    def wait_deps(self, deps, engines=("pe", "act", "dve")):
        if self.dry:
            return
        for eng in engines:
            need = {}
            for d in deps:
                for k, v in d.w.items():
                    if need.get(k, 0) < v:
                        need[k] = v
            self._emit_waits(eng, need)

    def _replay(self, eng, e):
        sems = self.sems
        for it in self.rec[eng]:
            if it[0] == "wait":
                e.wait_ge(sems[it[1]], it[2])
            elif it[0] == "ins":
                ins = getattr(e, it[1])(*it[2], **it[3])
                ins.then_inc(sems[it[4]], it[5])
            else:
                ins = e.collective_compute("AllGather", ALU.bypass, replica_groups=[list(range(NCORES))],
                                           ins=[it[1].ap().opt()], outs=[it[2].ap().opt()])
                ins.then_inc(sems[it[3]])

    def finish(self):
        self.barrier()
        with self.nc.Block() as blk:
            blk.tensor(lambda e: self._replay("pe", e))
            blk.scalar(lambda e: self._replay("act", e))
            blk.vector(lambda e: self._replay("dve", e))
            blk.gpsimd(lambda e: self._replay("pool", e))
            blk.sync(lambda e: self._replay("sp", e))
        self.es.close()


class WStream:
    def __init__(self, p, nslot=4, slot_elems=8192, name="ws"):
        self.p = p
        self.nslot = nslot
        self.slots = [p.sb("%s%d" % (name, i), [128, slot_elems], BF16) for i in range(nslot)]
        self.deps = [Dep() for _ in range(nslot)]
        self.specs = []
        self.issued = 0
        self.used = 0

    def plan(self, specs):
        self.specs.extend(specs)

    def _issue(self, i):
        W, k0, nk, f0, nf = self.specs[i]
        s = i % self.nslot
        p = self.p
        key = (W.name, k0, nk, f0, nf)
        cache = p.__dict__.setdefault("wcache", {})
        if key not in cache:
            scr = p.nc.dram_tensor("wbf_%d" % len(cache), [128, nk * nf], BF16).ap()
            dep = Dep()
            src = W.rearrange("(k p) f -> p k f", p=128)[:, k0:k0 + nk, f0:f0 + nf]
            p.dma("pool", out=scr.rearrange("p (k f) -> p k f", f=nf), in_=src, writes=[dep])
            cache[key] = (scr, dep)
        scr, dep = cache[key]
        p.dma("sp", out=self.slots[s][:, 0:nk * nf], in_=scr, reads=[dep], writes=[self.deps[s]])

    def next(self, spec=None):
        if self.p.dry:
            self.__dict__.setdefault("recording", []).append(spec)
            return None, None
        i = self.used
        if spec is not None:
            assert tuple(spec[1:]) == tuple(self.specs[i][1:]), (spec[1:], self.specs[i][1:])
        self.used += 1
        while self.issued < len(self.specs) and self.issued <= i + self.nslot - 2:
            self._issue(self.issued)
            self.issued += 1
        W, k0, nk, f0, nf = self.specs[i]
        s = i % self.nslot
        view = self.slots[s][:, 0:nk * nf].rearrange("p (k f) -> p k f", f=nf)
        return view, self.deps[s]


def load_const(p, name, ap, shape, dt=F32, q="sp"):
    t = p.sb(name, shape, dt)
    d = Dep()
    p.dma(q, out=t[:], in_=ap, writes=[d])
    return t, d


def rstd_from_ms(p, C, pss, pssd, n, scale=1.0):
    rstd, rstdd = C["rstd"], C["rstdd"]
    p.op("act", lambda e: e.activation(out=rstd[:, 0:n], in_=pss[:, 0:n], func=AF.Sqrt, bias=C["eps"][:, 0:1], scale=scale),
         reads=[pssd, C["constd"]], writes=[rstdd])
    p.op("dve", lambda e: e.reciprocal(out=rstd[:, 0:n], in_=rstd[:, 0:n]), reads=[rstdd], writes=[rstdd])


def rmsnorm(p, C, x, xdep, n, g, gdep, xn, xndep):
    sq, sqd = C["sq"], C["sqd"]
    p.op("act", lambda e: e.activation(out=sq[:, :, 0:n], in_=x[:, :, 0:n], func=AF.Square), reads=[xdep], writes=[sqd])
    pss, pssd = C["ps_ssq"], C["ps_ssqd"]
    for k in range(KD):
        p.op("pe", lambda e, k=k: e.matmul(pss[:, 0:n], lhsT=C["ones_bf"][:], rhs=sq[:, k, 0:n], start=(k == 0), stop=(k == KD - 1)),
             reads=[sqd, C["constd"]], writes=[pssd])
    rstd, rstdd = C["rstd"], C["rstdd"]
    rstd_from_ms(p, C, pss, pssd, n)
    for k in range(KD):
        p.op("dve", lambda e, k=k: e.scalar_tensor_tensor(out=xn[:, k, 0:n], in0=x[:, k, 0:n], scalar=g[:, k:k + 1], in1=rstd[:, 0:n],
                                                           op0=ALU.mult, op1=ALU.mult),
             reads=[xdep, rstdd, gdep], writes=[xndep])


def common(p):
    C = {}
    C["constd"] = Dep()
    C["ones_bf"] = p.sb("ones_bf", [128, 128], BF16)
    p.op("dve", lambda e: e.memset(C["ones_bf"][:], 1.0 / D), writes=[C["constd"]])
    C["eps"] = p.sb("eps_c", [128, 1], F32)
    p.op("dve", lambda e: e.memset(C["eps"][:], EPS), writes=[C["constd"]])
    C["sq"] = p.sb("sq", [128, KD, T], BF16)
    C["sqd"] = Dep()
    C["ps_ssq"] = p.ps("ps_ssq", [128, T])
    C["ps_ssqd"] = Dep()
    C["rstd"] = p.sb("rstd", [128, T], F32)
    C["rstdd"] = Dep()
    return C


def stage_ffn(p, C, ws, io, ntiles, final):
    NH = 2
    HC = 24
    xT, xh, yT = io["xT"], io.get("xh"), io["yT"]
    g, gd = load_const(p, "ffn_g", io["g"], [128, KD])
    cw, cwd = load_const(p, "ffn_cw", io["cw"], [128, 96, 3])
    cb, cbd = load_const(p, "ffn_cb", io["cb"], [128, 96])
    if final:
        fg, fgd = load_const(p, "fin_g", io["fg"], [128, KD])
    x = p.sb("ffn_x", [128, KD, T], F32)
    xd = Dep()
    xn = p.sb("ffn_xn", [128, KD, T], BF16)
    xnd = Dep()
    xhal = p.sb("ffn_xhal", [128, KD, 2], F32)
    xhald = Dep()
    xnh = p.sb("ffn_xnh", [128, KD, 2], BF16)
    xnhd = Dep()
    h = p.sb("ffn_h", [128, HC, T], BF16)
    hd = Dep()
    hal = p.sb("ffn_hal", [128, 96, 2], F32)
    hald = [Dep() for _ in range(96)]
    raw = [[p.sb("ffn_raw%d%d" % (i, j), [128, T + 2], F32) for j in range(2)] for i in range(2)]
    rawd = [[Dep() for j in range(2)] for i in range(2)]
    acc = [[p.sb("ffn_acc%d%d" % (i, j), [128, T], F32) for j in range(2)] for i in range(2)]
    accd = [[Dep() for j in range(2)] for i in range(2)]
    psA = [[p.ps("ffn_ps%d%d" % (i, j), [128, T]) for j in range(2)] for i in range(2)]
    psAd = [[Dep() for j in range(2)] for i in range(2)]
    psO = [p.ps("ffn_pso%d" % i, [128, T]) for i in range(2)]
    psOd = [Dep() for i in range(2)]
    psH = p.ps("ffn_psh", [128, 512])
    psHd = Dep()

    w_in, w_out = io["w_in"], io["w_out"]
    specs = []
    for t in range(ntiles):
        for hf in range(NH):
            for cg in range(HC // 4):
                c0 = hf * HC + cg * 4
                specs.append((w_in, 0, KD, c0 * 128, 512))
                specs.append((w_in, 0, KD, (48 + c0) * 128, 512))
            for og in range(8):
                specs.append((w_out, hf * HC, HC, og * 256, 256))
    ws.plan(specs)

    if "gath_h" in io:
        gh = p.sb("ffn_gh", [128, NCORES, 32], F32)
        ghd = Dep()
        p.dma("sp", out=gh[:], in_=io["gath_h"].ap().rearrange("(r p) w -> p r w", p=128), writes=[ghd])
        usel, useld = load_const(p, "ffn_usel", io["usel"], [128, NCORES])
        xh2 = xhal[:, :, :].rearrange("p k t -> p (k t)")
        p.op("dve", lambda e: e.memset(xhal[:], 0.0), writes=[xhald])
        for r in range(NCORES):
            p.op("dve", lambda e: e.scalar_tensor_tensor(out=xh2, in0=gh[:, r, :], scalar=usel[:, r:r + 1], in1=xh2, op0=ALU.mult, op1=ALU.add),
                 reads=[ghd, useld, xhald], writes=[xhald])
    else:
        p.dma("sp", out=xhal[:], in_=xh.rearrange("(k p) t -> p k t", p=128), writes=[xhald])
    rmsnorm(p, C, xhal, xhald, 2, g, gd, xnh, xnhd)

    pair = 0
    for t in range(ntiles):
        t0 = t * T
        p.dma("sp", out=x[:], in_=xT.rearrange("(k p) t -> p k t", p=128)[:, :, t0:t0 + T], writes=[xd])
        rmsnorm(p, C, x, xd, T, g, gd, xn, xnd)
        for hf in range(NH):
            for cg in range(HC // 4):
                wv, wvd = ws.next()
                wg, wgd = ws.next()
                for j in range(4):
                    cl = cg * 4 + j
                    cidx = [hf * HC + cl, 48 + hf * HC + cl]
                    b = pair % 2
                    pair += 1
                    for vg, (wt, wd) in enumerate(((wv, wvd), (wg, wgd))):
                        c = cidx[vg]
                        if t == 0:
                            for k in range(KD):
                                p.op("pe", lambda e, k=k, wt=wt, c=c: e.matmul(psH[:, 2 * c:2 * c + 2], lhsT=wt[:, k, j * 128:(j + 1) * 128],
                                                                                  rhs=xnh[:, k, :], start=(k == 0), stop=(k == KD - 1)),
                                     reads=[wd, xnhd], writes=[psHd])
                            p.op("act", lambda e, c=c: e.copy(out=hal[:, c, :], in_=psH[:, 2 * c:2 * c + 2]), reads=[psHd], writes=[hald[c]])
                        for k in range(KD):
                            p.op("pe", lambda e, k=k, wt=wt, vg=vg: e.matmul(psA[vg][b][:], lhsT=wt[:, k, j * 128:(j + 1) * 128], rhs=xn[:, k, :],
                                                                            start=(k == 0), stop=(k == KD - 1)),
                                 reads=[wd, xnd], writes=[psAd[vg][b]])
                        r_, rd_ = raw[vg][b], rawd[vg][b]
                        a_, ad_ = acc[vg][b], accd[vg][b]
                        p.op("act", lambda e, r_=r_, vg=vg: e.copy(out=r_[:, 2:T + 2], in_=psA[vg][b][:]), reads=[psAd[vg][b]], writes=[rd_])
                        p.op("act", lambda e, r_=r_, c=c: e.copy(out=r_[:, 0:2], in_=hal[:, c, :]), reads=[hald[c]], writes=[rd_])
                        p.op("dve", lambda e, r_=r_, a_=a_, c=c: e.tensor_scalar(out=a_[:], in0=r_[:, 0:T], scalar1=cw[:, c, 0:1], scalar2=cb[:, c:c + 1],
                                                                                 op0=ALU.mult, op1=ALU.add),
                             reads=[rd_, cwd, cbd], writes=[ad_])
                        for kk in (1, 2):
                            p.op("dve", lambda e, r_=r_, a_=a_, c=c, kk=kk: e.scalar_tensor_tensor(out=a_[:], in0=r_[:, kk:kk + T], scalar=cw[:, c, kk:kk + 1],
                                                                                                  in1=a_[:], op0=ALU.mult, op1=ALU.add),
                                 reads=[rd_, ad_, cwd], writes=[ad_])
                        p.op("act", lambda e, r_=r_, c=c: e.copy(out=hal[:, c, :], in_=r_[:, T:T + 2]), reads=[rd_], writes=[hald[c]])
                    p.op("act", lambda e: e.activation(out=acc[1][b][:], in_=acc[1][b][:], func=AF.Gelu_apprx_tanh), reads=[accd[1][b]], writes=[accd[1][b]])
                    p.op("dve", lambda e, cl=cl: e.tensor_tensor(out=h[:, cl, :], in0=acc[0][b][:], in1=acc[1][b][:], op=ALU.mult),
                         reads=[accd[0][b], accd[1][b]], writes=[hd])
            for og in range(8):
                wo, wod = ws.next()
                for j in range(2):
                    do = og * 2 + j
                    ob = do % 2
                    for k in range(HC):
                        p.op("pe", lambda e, k=k, wo=wo, j=j, ob=ob: e.matmul(psO[ob][:], lhsT=wo[:, k, j * 128:(j + 1) * 128], rhs=h[:, k, :],
                                                                             start=(k == 0), stop=(k == HC - 1)),
                             reads=[wod, hd], writes=[psOd[ob]])
                    p.op("dve", lambda e, do=do, ob=ob: e.tensor_tensor(out=x[:, do, :], in0=x[:, do, :], in1=psO[ob][:], op=ALU.add),
                         reads=[psOd[ob], xd], writes=[xd])
        if final:
            sq = C["sq"]
            p.op("act", lambda e: e.activation(out=sq[:], in_=x[:], func=AF.Square), reads=[xd], writes=[C["sqd"]])
            for k in range(KD):
                p.op("pe", lambda e, k=k: e.matmul(C["ps_ssq"][:], lhsT=C["ones_bf"][:], rhs=sq[:, k, :], start=(k == 0), stop=(k == KD - 1)),
                     reads=[C["sqd"], C["constd"]], writes=[C["ps_ssqd"]])
            rstd = C["rstd"]
            rstd_from_ms(p, C, C["ps_ssq"], C["ps_ssqd"], T)
            for k in range(KD):
                p.op("dve", lambda e, k=k: e.scalar_tensor_tensor(out=x[:, k, :], in0=x[:, k, :], scalar=fg[:, k:k + 1], in1=rstd[:],
                                                                   op0=ALU.mult, op1=ALU.mult),
                     reads=[xd, C["rstdd"], fgd], writes=[xd])
        p.dma("sp", out=yT.rearrange("(k p) t -> p k t", p=128)[:, :, t0:t0 + T], in_=x[:], reads=[xd])


def vecT(v):
    return np.ascontiguousarray(np.asarray(v, np.float32).reshape(-1, 128).T)


def build_ffn(ntok, final):
    p = Prog()
    io = {
        "xT": p.dram("xT", [D, ntok], F32, "ExternalInput"),
        "xh": p.dram("xh", [D, 2], F32, "ExternalInput"),
        "g": p.dram("g", [128, KD], F32, "ExternalInput"),
        "cw": p.dram("cw", [128, 96, 3], F32, "ExternalInput"),
        "cb": p.dram("cb", [128, 96], F32, "ExternalInput"),
        "fg": p.dram("fg", [128, KD], F32, "ExternalInput"),
        "w_in": p.dram("w_in", [D, 12288], F32, "ExternalInput"),
        "w_out": p.dram("w_out", [6144, D], F32, "ExternalInput"),
        "yT": p.dram("yT", [D, ntok], F32, "ExternalOutput"),
    }
    C = common(p)
    ws = WStream(p)
    stage_ffn(p, C, ws, io, ntok // T, final)
    p.finish()
    return p


def ffn_inputs(inp, layer):
    return {
        "g": vecT(inp["norm_ffn_g"][layer]),
        "cw": np.ascontiguousarray(np.asarray(inp["ffn_conv_w"][layer], np.float32).T.reshape(96, 128, 3).transpose(1, 0, 2)),
        "cb": vecT(inp["ffn_conv_b"][layer]),
        "fg": vecT(inp["final_norm_g"]),
        "w_in": np.ascontiguousarray(inp["ffn_w_in"][layer], dtype=np.float32),
        "w_out": np.ascontiguousarray(inp["ffn_w_out"][layer], dtype=np.float32),
    }


def stage_xa(p, C, ws, io, ntiles):
    NM = 256
    scale = 512.0 ** -0.5
    xT, yT = io["xT"], io["yT"]
    g, gd = load_const(p, "xa_g", io["g"], [128, KD])
    mg, mgd = load_const(p, "xa_mg", io["mg"], [128, KD])
    ident = p.sb("xa_ident", [128, 128], BF16)
    identd = Dep()
    p.dma("pool", out=ident[:], in_=io["ident"], writes=[identd])
    x = p.sb("xa_x", [128, KD, T], F32)
    xd = Dep()
    xn = p.sb("xa_xn", [128, KD, T], BF16)
    xnd = Dep()
    KT = p.sb("xa_KT", [128, KD, NM], BF16)
    KTd = Dep()
    V = p.sb("xa_V", [128, 2, D], BF16)
    Vd = Dep()
    qT = p.sb("xa_qT", [128, KD, T], BF16)
    qTd = Dep()
    oT, oTd = C["sq"], C["sqd"]
    PT = p.sb("xa_PT", [128, 2, T], BF16)
    PTd = Dep()
    P = [p.sb("xa_P%d" % i, [128, 4, NM], F32) for i in range(2)]
    Pd = [Dep() for _ in range(2)]
    Pn = [p.sb("xa_Pn%d" % i, [128, 4, NM], BF16) for i in range(2)]
    Pnd = [Dep() for _ in range(2)]
    st = [p.sb("xa_st%d" % i, [128, 12], F32) for i in range(2)]
    std = [Dep() for _ in range(2)]
    psA = [p.ps("xa_psA%d" % i, [128, T]) for i in range(2)]
    psAd = [Dep() for _ in range(2)]
    psS = [p.ps("xa_psS%d" % i, [128, 4, NM]) for i in range(2)]
    psSd = [Dep() for _ in range(2)]
    psT = p.ps("xa_psT", [128, 2 * T], BF16)
    psTd = Dep()

    w_q, w_kv, w_o = io["w_q"], io["w_kv"], io["w_o"]
    specs = [(w_kv, 0, KD, fg * 512, 512) for fg in range(4)] + [(w_kv, 0, KD, D + fg * 512, 512) for fg in range(4)]
    for t in range(ntiles):
        specs += [(w_q, 0, KD, fg * 512, 512) for fg in range(4)]
        specs += [(w_o, 0, KD, fg * 512, 512) for fg in range(4)]
    ws.plan(specs)

    p.dma("sp", out=x[:, :, 0:NM], in_=io["memT"].rearrange("(k p) t -> p k t", p=128), writes=[xd])
    rmsnorm(p, C, x, xd, NM, mg, mgd, xn, xnd)
    nb = 0
    for fg in range(4):
        wt, wd = ws.next()
        for j in range(4):
            fc = fg * 4 + j
            b = nb % 2
            nb += 1
            for k in range(KD):
                p.op("pe", lambda e: e.matmul(psA[b][:, 0:NM], lhsT=wt[:, k, j * 128:(j + 1) * 128], rhs=xn[:, k, 0:NM], start=(k == 0), stop=(k == KD - 1)),
                     reads=[wd, xnd], writes=[psAd[b]])
            p.op("act", lambda e: e.copy(out=KT[:, fc, :], in_=psA[b][:, 0:NM]), reads=[psAd[b]], writes=[KTd])
    for fg in range(4):
        wt, wd = ws.next()
        for mc in range(2):
            b = nb % 2
            nb += 1
            for k in range(KD):
                p.op("pe", lambda e: e.matmul(psA[b][:], lhsT=xn[:, k, mc * 128:(mc + 1) * 128], rhs=wt[:, k, :], start=(k == 0), stop=(k == KD - 1)),
                     reads=[wd, xnd], writes=[psAd[b]])
            p.op("act", lambda e: e.copy(out=V[:, mc, fg * 512:(fg + 1) * 512], in_=psA[b][:]), reads=[psAd[b]], writes=[Vd])

    for t in range(ntiles):
        t0 = t * T
        p.dma("sp", out=x[:], in_=xT.rearrange("(k p) t -> p k t", p=128)[:, :, t0:t0 + T], writes=[xd])
        rmsnorm(p, C, x, xd, T, g, gd, xn, xnd)
        for fg in range(4):
            wt, wd = ws.next()
            for j in range(4):
                fc = fg * 4 + j
                b = nb % 2
                nb += 1
                for k in range(KD):
                    p.op("pe", lambda e: e.matmul(psA[b][:], lhsT=wt[:, k, j * 128:(j + 1) * 128], rhs=xn[:, k, :], start=(k == 0), stop=(k == KD - 1)),
                         reads=[wd, xnd], writes=[psAd[b]])
                p.op("act", lambda e: e.copy(out=qT[:, fc, :], in_=psA[b][:]), reads=[psAd[b]], writes=[qTd])
        for hh in range(4):
            b = hh % 2
            for sc in range(4):
                for dc in range(4):
                    p.op("pe", lambda e: e.matmul(psS[b][:, sc, :], lhsT=qT[:, hh * 4 + dc, sc * 128:(sc + 1) * 128], rhs=KT[:, hh * 4 + dc, :],
                                                  start=(dc == 0), stop=(dc == 3)),
                         reads=[qTd, KTd], writes=[psSd[b]])
            p.op("dve", lambda e: e.reduce_max(out=st[b][:, 0:4], in_=psS[b][:, :, :], axis=AX.X), reads=[psSd[b]], writes=[std[b]])
            p.op("dve", lambda e: e.tensor_tensor(out=P[b][:], in0=psS[b][:, :, :], in1=st[b][:, 0:4].unsqueeze(2).to_broadcast([128, 4, NM]), op=ALU.subtract),
                 reads=[psSd[b], std[b]], writes=[Pd[b]])
            p.op("act", lambda e: e.activation(out=P[b][:], in_=P[b][:], func=AF.Exp, scale=scale), reads=[Pd[b]], writes=[Pd[b]])
            p.op("dve", lambda e: e.reduce_sum(out=st[b][:, 4:8], in_=P[b][:], axis=AX.X), reads=[Pd[b]], writes=[std[b]])
            p.op("dve", lambda e: e.reciprocal(out=st[b][:, 8:12], in_=st[b][:, 4:8]), reads=[std[b]], writes=[std[b]])
            p.op("dve", lambda e: e.tensor_tensor(out=Pn[b][:], in0=P[b][:], in1=st[b][:, 8:12].unsqueeze(2).to_broadcast([128, 4, NM]), op=ALU.mult),
                 reads=[Pd[b], std[b]], writes=[Pnd[b]])
            for sc in range(4):
                for mc in range(2):
                    p.op("pe", lambda e: e.transpose(out=psT[:, (sc * 2 + mc) * 128:(sc * 2 + mc + 1) * 128], in_=Pn[b][:, sc, mc * 128:(mc + 1) * 128], identity=ident[:]),
                         reads=[Pnd[b], identd], writes=[psTd])
            p.op("act", lambda e: e.copy(out=PT[:, :, :].rearrange("p m (c s) -> p c m s", c=4),
                                         in_=psT[:, :].rearrange("p (c m s) -> p c m s", c=4, m=2)),
                 reads=[psTd], writes=[PTd])
            for dc in range(4):
                b = nb % 2
                nb += 1
                for mc in range(2):
                    p.op("pe", lambda e: e.matmul(psA[b][:], lhsT=V[:, mc, hh * 512 + dc * 128:hh * 512 + (dc + 1) * 128], rhs=PT[:, mc, :],
                                                  start=(mc == 0), stop=(mc == 1)),
                         reads=[Vd, PTd], writes=[psAd[b]])
                p.op("act", lambda e: e.copy(out=oT[:, hh * 4 + dc, :], in_=psA[b][:]), reads=[psAd[b]], writes=[oTd])
        for fg in range(4):
            wt, wd = ws.next()
            for j in range(4):
                do = fg * 4 + j
                b = nb % 2
                nb += 1
                for k in range(KD):
                    p.op("pe", lambda e: e.matmul(psA[b][:], lhsT=wt[:, k, j * 128:(j + 1) * 128], rhs=oT[:, k, :], start=(k == 0), stop=(k == KD - 1)),
                         reads=[wd, oTd], writes=[psAd[b]])
                p.op("dve", lambda e: e.tensor_tensor(out=x[:, do, :], in0=x[:, do, :], in1=psA[b][:], op=ALU.add),
                     reads=[psAd[b], xd], writes=[xd])
        p.dma("sp", out=yT.rearrange("(k p) t -> p k t", p=128)[:, :, t0:t0 + T], in_=x[:], reads=[xd])
        if t == ntiles - 1 and "hout" in io:
            p.dma("sp", out=io["hout"].ap().rearrange("p (k t) -> p k t", t=2), in_=x[:, :, T - 2:T], reads=[xd], writes=[io["houtd"]])


def build_xa(ntok):
    p = Prog()
    io = {
        "xT": p.dram("xT", [D, ntok], F32, "ExternalInput"),
        "memT": p.dram("memT", [D, 256], F32, "ExternalInput"),
        "g": p.dram("g", [128, KD], F32, "ExternalInput"),
        "mg": p.dram("mg", [128, KD], F32, "ExternalInput"),
        "ident": p.dram("ident", [128, 128], F32, "ExternalInput"),
        "w_q": p.dram("w_q", [D, D], F32, "ExternalInput"),
        "w_kv": p.dram("w_kv", [D, 2 * D], F32, "ExternalInput"),
        "w_o": p.dram("w_o", [D, D], F32, "ExternalInput"),
        "yT": p.dram("yT", [D, ntok], F32, "ExternalOutput"),
    }
    C = common(p)
    ws = WStream(p)
    stage_xa(p, C, ws, io, ntok // T)
    p.finish()
    return p


def xa_inputs(inp, layer):
    return {
        "g": vecT(inp["norm_cross_g"][layer]),
        "mg": vecT(inp["mem_norm_g"]),
        "ident": np.eye(128, dtype=np.float32),
        "w_q": np.ascontiguousarray(inp["xa_w_q"][layer], dtype=np.float32),
        "w_kv": np.ascontiguousarray(inp["xa_w_kv"][layer], dtype=np.float32),
        "w_o": np.ascontiguousarray(inp["xa_w_o"][layer], dtype=np.float32),
    }


def stage_gla(p, C, ws, io, ntiles, phase):
    xT = io["xT"]
    full = phase == 2
    g, gd = load_const(p, "gla_g", io["g"], [128, KD])
    ng, ngd = load_const(p, "gla_ng", io["ng"], [128, KD])
    bg, bgd = load_const(p, "gla_bg", io["bg"], [128, 8])
    mask, maskd = load_const(p, "gla_mask", io["mask"], [128, 128])
    ones = p.sb("gla_ones", [128, 128], F32)
    onesd = Dep()
    p.op("dve", lambda e: e.memset(ones[:], 1.0), writes=[onesd])
    one1 = p.sb("gla_one1", [128, 1], F32)
    p.op("dve", lambda e: e.memset(one1[:], 1.0), writes=[onesd])
    p.op("dve", lambda e: e.tensor_scalar(out=bg[:], in0=bg[:], scalar1=-1.0, scalar2=None, op0=ALU.mult), reads=[bgd], writes=[bgd])
    ident = p.sb("gla_ident", [128, 128], BF16)
    identd = Dep()
    p.dma("pool", out=ident[:], in_=io["ident"], writes=[identd])
    wgu = p.sb("gla_wgu", [16, 1024], BF16)
    wgud = Dep()
    p.dma("pool", out=wgu[:], in_=io["wgu"], writes=[wgud])

    x = p.sb("gla_x", [128, KD, T], F32)
    xd = Dep()
    xn = p.sb("gla_xn", [128, KD, T], BF16)
    xnd = Dep()
    oT, oTd = C["sq"], C["sqd"]
    S = p.sb("gla_S", [128, 8, 512], F32)
    Sd = [Dep() for _ in range(8)]
    Sb = p.sb("gla_Sb", [128, 8, 512], BF16)
    Sbd = [Dep() for _ in range(8)]
    Bt = p.sb("gla_Bt", [128, 8], F32)
    Btd = Dep()
    glow = p.sb("gla_glow", [16, T], BF16)
    glowd = Dep()
    cs = p.sb("gla_cs", [128, 2, T], F32)
    csd = Dep()
    lt = p.sb("gla_lt", [128, 2, T], F32)
    ltd = Dep()
    eb = p.sb("gla_eb", [128, 2, T], F32)
    ebd = Dep()
    enb = p.sb("gla_enb", [128, 2, T], F32)
    enbd = Dep()
    ebl = p.sb("gla_ebl", [128, 2, T], F32)
    ebld = Dep()
    sm = p.sb("gla_sm", [128, 2, 4, 2], F32)
    smd = Dep()
    qd_ = p.sb("gla_qd", [128, 2, T], BF16)
    qdd = Dep()
    kd_ = p.sb("gla_kd", [128, 2, T], BF16)
    kdd = Dep()
    ke = p.sb("gla_ke", [128, 2, T], BF16)
    ked = Dep()
    vt = p.sb("gla_vt", [128, 4, 512], BF16)
    vtd = Dep()
    rg = p.sb("gla_rg", [128, 4, T], BF16)
    rgd = Dep()
    rtmp = p.sb("gla_rtmp", [128, T], F32)
    rtmpd = Dep()
    osq = p.sb("gla_osq", [128, 512], F32)
    osqd = Dep()
    on = p.sb("gla_on", [128, 512], BF16)
    ond = Dep()
    st = p.sb("gla_st", [128, 4], F32)
    std = Dep()
    attm = p.sb("gla_attm", [128, 128], BF16)
    attmd = Dep()
    keT = p.sb("gla_keT", [128, 256], BF16)
    keTd = Dep()
    psA = [p.ps("gla_psA%d" % i, [128, T]) for i in range(2)]
    psAd = [Dep() for _ in range(2)]
    psAtt = p.ps("gla_psAtt", [128, T])
    psAttd = Dep()
    psO = p.ps("gla_psO", [128, T])
    psOd = Dep()
    psU = p.ps("gla_psU", [128, T])
    psUd = Dep()
    psT = p.ps("gla_psT", [128, 2 * T], BF16)
    psTd = Dep()
    psT2d = psTd

    w_in, w_out = io["w_in"], io["w_out"]
    specs = []
    for t in range(ntiles):
        specs.append((w_in, 0, KD, 6144, 16))
        for hh in range(4):
            if full:
                specs.append((w_in, 0, KD, hh * 256, 256))
            specs.append((w_in, 0, KD, 1024 + hh * 256, 256))
            specs += [(w_in, 0, KD, 2048 + hh * 512 + hv * 256, 256) for hv in range(2)]
            if full:
                specs += [(w_in, 0, KD, 4096 + hh * 512 + hv * 256, 256) for hv in range(2)]
        if full:
            specs += [(w_out, 0, KD, og * 256, 256) for og in range(8)]
    ws.plan(specs)

    if full:
        gath = io["gath"].ap().rearrange("(r p) w -> p r w", p=128)
        use, used = load_const(p, "gla_use", io["use"], [128, NCORES])
        bsm = p.sb("gla_bsm", [128, NCORES, 8], F32)
        bsmd = Dep()
        p.dma("sp", out=bsm[:], in_=gath[:, :, 4096:4104], writes=[bsmd])
        p.op("dve", lambda e: e.memset(S[:], 0.0), writes=Sd)
        tmp = x[:, 0:8, :]
        for r in range(NCORES):
            p.op("act", lambda e: e.activation(out=bsm[:, r, :], in_=bsm[:, r, :], func=AF.Exp, scale=use[:, r:r + 1]), reads=[bsmd, used], writes=[bsmd])
            p.dma("sp", out=tmp, in_=gath[:, r, 0:4096].rearrange("p (c v) -> p c v", c=8), writes=[xd])
            for c in range(8):
                p.op("dve", lambda e: e.tensor_scalar(out=S[:, c, :], in0=S[:, c, :], scalar1=bsm[:, r, c:c + 1], scalar2=None, op0=ALU.mult),
                     reads=[Sd[c], bsmd], writes=[Sd[c]])
                p.op("dve", lambda e: e.scalar_tensor_tensor(out=S[:, c, :], in0=x[:, c, :], scalar=use[:, r:r + 1], in1=S[:, c, :], op0=ALU.mult, op1=ALU.add),
                     reads=[Sd[c], used, xd], writes=[Sd[c]])
    else:
        p.op("dve", lambda e: e.memset(S[:], 0.0), writes=Sd)
        p.op("dve", lambda e: e.memset(Bt[:], 0.0), writes=[Btd])
    for c in range(8):
        p.op("act", lambda e: e.copy(out=Sb[:, c, :], in_=S[:, c, :]), reads=[Sd[c]], writes=[Sbd[c]])

    nb = 0
    for t in range(ntiles):
        t0 = t * T
        p.dma("sp", out=x[:], in_=xT.rearrange("(k p) t -> p k t", p=128)[:, :, t0:t0 + T], writes=[xd])
        rmsnorm(p, C, x, xd, T, g, gd, xn, xnd)
        wt, wd = ws.next()
        b = nb % 2
        nb += 1
        for k in range(KD):
            p.op("pe", lambda e: e.matmul(psA[b][0:16, :], lhsT=wt[:, k, 0:16], rhs=xn[:, k, :], start=(k == 0), stop=(k == KD - 1)),
                 reads=[wd, xnd], writes=[psAd[b]])
        p.op("act", lambda e: e.copy(out=glow[:], in_=psA[b][0:16, :]), reads=[psAd[b]], writes=[glowd])
        for hh in range(4):
            for kc in range(2):
                fc = hh * 2 + kc
                b = nb % 2
                nb += 1
                p.op("pe", lambda e: e.matmul(psA[b][:], lhsT=wgu[0:16, fc * 128:(fc + 1) * 128], rhs=glow[0:16, :], start=True, stop=True),
                     reads=[wgud, glowd], writes=[psAd[b]])
                p.op("act", lambda e: e.activation(out=lt[:, kc, :], in_=psA[b][:], func=AF.Exp, bias=bg[:, fc:fc + 1], scale=-1.0),
                     reads=[psAd[b], bgd], writes=[ltd])
                p.op("act", lambda e: e.activation(out=lt[:, kc, :], in_=lt[:, kc, :], func=AF.Ln, bias=one1[:, 0:1], scale=1.0),
                     reads=[ltd, onesd], writes=[ltd])
                for sc in range(4):
                    c0 = sc * 128
                    p.op("dve", lambda e: e.tensor_tensor_scan(out=cs[:, kc, c0:c0 + 128], data0=ones[:], data1=lt[:, kc, c0:c0 + 128], initial=0.0,
                                                               op0=ALU.mult, op1=ALU.add),
                         reads=[ltd, onesd], writes=[csd])
                    p.op("dve", lambda e: e.tensor_scalar(out=sm[:, kc, sc, 0:1], in0=cs[:, kc, c0 + 127:c0 + 128], scalar1=-1.0 / 16, scalar2=None, op0=ALU.mult),
                         reads=[csd], writes=[smd])
                    p.op("act", lambda e: e.activation(out=sm[:, kc, sc, 1:2], in_=sm[:, kc, sc, 0:1], func=AF.Exp), reads=[smd], writes=[smd])
                    p.op("act", lambda e: e.activation(out=ebl[:, kc, c0:c0 + 128], in_=cs[:, kc, c0:c0 + 128], func=AF.Exp, bias=sm[:, kc, sc, 0:1], scale=1.0 / 16),
                         reads=[csd, smd], writes=[ebld])
                    if not full:
                        p.op("dve", lambda e: e.tensor_tensor(out=Bt[:, fc:fc + 1], in0=Bt[:, fc:fc + 1], in1=sm[:, kc, sc, 0:1], op=ALU.add),
                             reads=[smd, Btd], writes=[Btd])
                if full:
                    p.op("act", lambda e: e.activation(out=eb[:, kc, :], in_=cs[:, kc, :], func=AF.Exp, scale=-1.0 / 16), reads=[csd], writes=[ebd])
                    p.op("act", lambda e: e.activation(out=enb[:, kc, :], in_=cs[:, kc, :], func=AF.Exp, scale=1.0 / 16), reads=[csd], writes=[enbd])
            if full:
                wt, wd = ws.next()
                for kc in range(2):
                    b = nb % 2
                    nb += 1
                    for k in range(KD):
                        p.op("pe", lambda e: e.matmul(psA[b][:], lhsT=wt[:, k, kc * 128:(kc + 1) * 128], rhs=xn[:, k, :], start=(k == 0), stop=(k == KD - 1)),
                             reads=[wd, xnd], writes=[psAd[b]])
                    p.op("dve", lambda e: e.scalar_tensor_tensor(out=qd_[:, kc, :], in0=psA[b][:], scalar=1.0 / 16, in1=eb[:, kc, :], op0=ALU.mult, op1=ALU.mult),
                         reads=[psAd[b], ebd], writes=[qdd])
            wt, wd = ws.next()
            for kc in range(2):
                b = nb % 2
                nb += 1
                for k in range(KD):
                    p.op("pe", lambda e: e.matmul(psA[b][:], lhsT=wt[:, k, kc * 128:(kc + 1) * 128], rhs=xn[:, k, :], start=(k == 0), stop=(k == KD - 1)),
                         reads=[wd, xnd], writes=[psAd[b]])
                if full:
                    p.op("dve", lambda e: e.tensor_tensor(out=kd_[:, kc, :], in0=psA[b][:], in1=enb[:, kc, :], op=ALU.mult),
                         reads=[psAd[b], enbd], writes=[kdd])
                p.op("dve", lambda e: e.tensor_tensor(out=ke[:, kc, :], in0=psA[b][:], in1=ebl[:, kc, :], op=ALU.mult),
                     reads=[psAd[b], ebld], writes=[ked])
            for hv in range(2):
                wt, wd = ws.next()
                for sc in range(4):
                    b = nb % 2
                    nb += 1
                    for k in range(KD):
                        p.op("pe", lambda e: e.matmul(psA[b][:, 0:256], lhsT=xn[:, k, sc * 128:(sc + 1) * 128], rhs=wt[:, k, :], start=(k == 0), stop=(k == KD - 1)),
                             reads=[wd, xnd], writes=[psAd[b]])
                    p.op("act", lambda e: e.copy(out=vt[:, sc, hv * 256:(hv + 1) * 256], in_=psA[b][:, 0:256]), reads=[psAd[b]], writes=[vtd])
            if full:
                for hv in range(2):
                    wt, wd = ws.next()
                    for j in range(2):
                        vc = hv * 2 + j
                        b = nb % 2
                        nb += 1
                        for k in range(KD):
                            p.op("pe", lambda e: e.matmul(psA[b][:], lhsT=wt[:, k, j * 128:(j + 1) * 128], rhs=xn[:, k, :], start=(k == 0), stop=(k == KD - 1)),
                                 reads=[wd, xnd], writes=[psAd[b]])
                        p.op("act", lambda e: e.activation(out=rtmp[:], in_=psA[b][:], func=AF.Silu), reads=[psAd[b]], writes=[rtmpd])
                        p.op("dve", lambda e: e.tensor_scalar(out=rg[:, vc, :], in0=rtmp[:], scalar1=ng[:, hh * 4 + vc:hh * 4 + vc + 1], scalar2=None, op0=ALU.mult),
                             reads=[rtmpd, ngd], writes=[rgd])
            for sc in range(4):
                c0 = sc * 128
                if full:
                    for kc in range(2):
                        p.op("pe", lambda e: e.matmul(psAtt[:, 0:128], lhsT=kd_[:, kc, c0:c0 + 128], rhs=qd_[:, kc, c0:c0 + 128], start=(kc == 0), stop=(kc == 1)),
                             reads=[kdd, qdd], writes=[psAttd])
                    p.op("dve", lambda e: e.tensor_tensor(out=attm[:], in0=psAtt[:, 0:128], in1=mask[:], op=ALU.mult), reads=[psAttd, maskd], writes=[attmd])
                    for kc in range(2):
                        p.op("pe", lambda e: e.matmul(psO[:], lhsT=qd_[:, kc, c0:c0 + 128], rhs=Sb[:, hh * 2 + kc, :], start=(kc == 0), stop=False),
                             reads=[qdd, Sbd[hh * 2 + kc]], writes=[psOd])
                    p.op("pe", lambda e: e.matmul(psO[:], lhsT=attm[:], rhs=vt[:, sc, :], start=False, stop=True), reads=[attmd, vtd], writes=[psOd])
                    p.op("act", lambda e: e.activation(out=osq[:], in_=psO[:], func=AF.Square), reads=[psOd], writes=[osqd])
                    p.op("dve", lambda e: e.reduce_sum(out=st[:, 0:1], in_=osq[:], axis=AX.X), reads=[osqd], writes=[std])
                    p.op("act", lambda e: e.activation(out=st[:, 1:2], in_=st[:, 0:1], func=AF.Sqrt, bias=C["eps"][:, 0:1], scale=1.0 / 512),
                         reads=[std, C["constd"]], writes=[std])
                    p.op("dve", lambda e: e.reciprocal(out=st[:, 2:3], in_=st[:, 1:2]), reads=[std], writes=[std])
                    p.op("dve", lambda e: e.tensor_scalar(out=on[:], in0=psO[:], scalar1=st[:, 2:3], scalar2=None, op0=ALU.mult),
                         reads=[psOd, std], writes=[ond])
                    for vc in range(4):
                        p.op("pe", lambda e: e.transpose(out=psT[:, vc * 128:(vc + 1) * 128], in_=on[:, vc * 128:(vc + 1) * 128], identity=ident[:]),
                             reads=[ond, identd], writes=[psTd])
                    p.op("dve", lambda e: e.tensor_tensor(out=oT[:, hh * 4:(hh + 1) * 4, c0:c0 + 128],
                                                          in0=psT[:, 0:512].rearrange("p (v s) -> p v s", v=4), in1=rg[:, :, c0:c0 + 128], op=ALU.mult),
                         reads=[psTd, rgd], writes=[oTd])
                for kc in range(2):
                    p.op("pe", lambda e: e.transpose(out=psT[:, 512 + kc * 128:512 + (kc + 1) * 128], in_=ke[:, kc, c0:c0 + 128], identity=ident[:]),
                         reads=[ked, identd], writes=[psT2d])
                p.op("act", lambda e: e.copy(out=keT[:], in_=psT[:, 512:768]), reads=[psT2d], writes=[keTd])
                for kc in range(2):
                    sidx = hh * 2 + kc
                    p.op("pe", lambda e: e.matmul(psU[:], lhsT=keT[:, kc * 128:(kc + 1) * 128], rhs=vt[:, sc, :], start=True, stop=True),
                         reads=[keTd, vtd], writes=[psUd])
                    p.op("dve", lambda e: e.scalar_tensor_tensor(out=S[:, sidx, :], in0=S[:, sidx, :], scalar=sm[:, kc, sc, 1:2], in1=psU[:],
                                                                  op0=ALU.mult, op1=ALU.add),
                         reads=[Sd[sidx], smd, psUd], writes=[Sd[sidx]])
                    p.op("act", lambda e: e.copy(out=Sb[:, sidx, :], in_=S[:, sidx, :]), reads=[Sd[sidx]], writes=[Sbd[sidx]])
        if full:
            for og in range(8):
                wt, wd = ws.next()
                for j in range(2):
                    do = og * 2 + j
                    b = nb % 2
                    nb += 1
                    for k in range(KD):
                        p.op("pe", lambda e: e.matmul(psA[b][:], lhsT=wt[:, k, j * 128:(j + 1) * 128], rhs=oT[:, k, :], start=(k == 0), stop=(k == KD - 1)),
                             reads=[wd, oTd], writes=[psAd[b]])
                    p.op("dve", lambda e: e.tensor_tensor(out=x[:, do, :], in0=x[:, do, :], in1=psA[b][:], op=ALU.add),
                         reads=[psAd[b], xd], writes=[xd])
            p.dma("sp", out=io["yT"].rearrange("(k p) t -> p k t", p=128)[:, :, t0:t0 + T], in_=x[:], reads=[xd])
    if not full:
        bn = io["bounce"].ap()
        p.dma("sp", out=bn[:, 0:4096].rearrange("p (c v) -> p c v", c=8), in_=S[:], reads=Sd, writes=[io["bounced"]])
        p.dma("sp", out=bn[:, 4096:4104], in_=Bt[:], reads=[Btd], writes=[io["bounced"]])


def build_gla(ntok, phase):
    p = Prog()
    io = {
        "xT": p.dram("xT", [D, ntok], F32, "ExternalInput"),
        "g": p.dram("g", [128, KD], F32, "ExternalInput"),
        "ng": p.dram("ng", [128, KD], F32, "ExternalInput"),
        "bg": p.dram("bg", [128, 8], F32, "ExternalInput"),
        "mask": p.dram("mask", [128, 128], F32, "ExternalInput"),
        "ident": p.dram("ident", [128, 128], F32, "ExternalInput"),
        "wgu": p.dram("wgu", [16, 1024], F32, "ExternalInput"),
        "w_in": p.dram("w_in", [D, 6160], F32, "ExternalInput"),
        "w_out": p.dram("w_out", [D, D], F32, "ExternalInput"),
    }
    if phase == 2:
        io["Sprev"] = p.dram("Sprev", [3, 128, 8, 512], F32, "ExternalInput")
        io["Bprev"] = p.dram("Bprev", [3, 128, 8], F32, "ExternalInput")
        io["yT"] = p.dram("yT", [D, ntok], F32, "ExternalOutput")
    else:
        io["Sout"] = p.dram("Sout", [128, 8, 512], F32, "ExternalOutput")
        io["Bout"] = p.dram("Bout", [128, 8], F32, "ExternalOutput")
    C = common(p)
    ws = WStream(p, slot_elems=4096)
    stage_gla(p, C, ws, io, ntok // T, phase)
    p.finish()
    return p


def gla_inputs(inp):
    return {
        "g": vecT(inp["norm_mix_g"][1]),
        "ng": vecT(inp["gla_norm_g"][0]),
        "bg": vecT(inp["gla_b_gate"][0]),
        "mask": np.triu(np.ones((128, 128), np.float32)),
        "ident": np.eye(128, dtype=np.float32),
        "wgu": np.ascontiguousarray(inp["gla_w_gate_up"][0], dtype=np.float32),
        "w_in": np.ascontiguousarray(inp["gla_w_in"][0], dtype=np.float32),
        "w_out": np.ascontiguousarray(inp["gla_w_out"][0], dtype=np.float32),
    }


TA = 256
TA1 = 512


def conv4(p, raw, rawd, acc, accd, cw, cb, cwd, c, n):
    p.op("dve", lambda e: e.tensor_scalar(out=acc[:, 0:n], in0=raw[:, 0:n], scalar1=cw[:, c, 0:1], scalar2=cb[:, c:c + 1], op0=ALU.mult, op1=ALU.add),
         reads=[rawd, cwd], writes=[accd])
    for kk in (1, 2, 3):
        p.op("dve", lambda e: e.scalar_tensor_tensor(out=acc[:, 0:n], in0=raw[:, kk:kk + n], scalar=cw[:, c, kk:kk + 1], in1=acc[:, 0:n],
                                                      op0=ALU.mult, op1=ALU.add),
             reads=[rawd, accd, cwd], writes=[accd])


def stage_ab(p, C, ws, io, ntiles, phase):
    off, bank = p.off, p.bank
    p.dry = True
    ws.recording = []
    _stage_ab(p, C, ws, io, ntiles, phase)
    p.dry = False
    p.off, p.bank = off, bank
    ws.plan(ws.recording)
    _stage_ab(p, C, ws, io, ntiles, phase)


def _interleave(gens):
    gens = list(gens)
    while gens:
        for g_ in list(gens):
            try:
                next(g_)
            except StopIteration:
                gens.remove(g_)


def _stage_ab(p, C, ws, io, ntiles, phase):
    full = phase == 2
    n = TA if full else TA1
    NCH = n // 128
    xT = io["xT"]
    g, gd = load_const(p, "ab_g", io["g"], [128, KD])
    lcw, lcwd = load_const(p, "ab_lcw", io["lcw"], [128, 16, 4])
    lcb, lcbd = load_const(p, "ab_lcb", io["lcb"], [128, 16])
    scw, scwd = load_const(p, "ab_scw", io["scw"], [128, 24, 4])
    scb, scbd = load_const(p, "ab_scb", io["scb"], [128, 24])
    ba, bad = load_const(p, "ab_ba", io["ba"], [128, 16])
    bx, bxd = load_const(p, "ab_bx", io["bx"], [128, 16])
    c8, c8d = load_const(p, "ab_lam", io["lam"], [128, 16])
    dtb, dtbd = load_const(p, "ab_dtb", io["dtb"], [128, 32])
    aneg, anegd = load_const(p, "ab_alog", io["alog"], [128, 32])
    Dv, Dvd = load_const(p, "ab_Dv", io["Dv"], [128, 16])
    sng, sngd = load_const(p, "ab_sng", io["sng"], [128, 16])
    mask, maskd = load_const(p, "ab_mask", io["mask"], [128, 128])
    U, Ud = load_const(p, "ab_U", io["U"], [128, 128])
    for d_ in (lcwd, scwd, bad, c8d, dtbd, anegd, Dvd, sngd, maskd, Ud):
        pass
    cd = Dep()
    ones = p.sb("ab_ones", [128, 128], F32)
    one1 = p.sb("ab_one1", [128, 1], F32)
    p.op("dve", lambda e: e.memset(ones[:], 1.0), writes=[cd])
    p.op("dve", lambda e: e.memset(one1[:], 1.0), writes=[cd])
    c16 = p.sb("ab_c16", [128, 16], F32)
    p.op("act", lambda e: e.activation(out=c8[:], in_=c8[:], func=AF.Exp, scale=-1.0), reads=[c8d], writes=[c8d])
    p.op("act", lambda e: e.activation(out=c8[:], in_=c8[:], func=AF.Ln, bias=one1[:, 0:1], scale=1.0), reads=[c8d, cd], writes=[c8d])
    p.op("dve", lambda e: e.tensor_scalar(out=c8[:], in0=c8[:], scalar1=-8.0, scalar2=None, op0=ALU.mult), reads=[c8d], writes=[c8d])
    p.op("dve", lambda e: e.tensor_scalar(out=c16[:], in0=c8[:], scalar1=2.0, scalar2=None, op0=ALU.mult), reads=[c8d], writes=[cd])
    p.op("act", lambda e: e.activation(out=aneg[:], in_=aneg[:], func=AF.Exp), reads=[anegd], writes=[anegd])
    p.op("dve", lambda e: e.tensor_scalar(out=aneg[:], in0=aneg[:], scalar1=-1.0, scalar2=None, op0=ALU.mult), reads=[anegd], writes=[anegd])
    ident = p.sb("ab_ident", [128, 128], BF16)
    identd = Dep()
    p.dma("pool", out=ident[:], in_=io["ident"], writes=[identd])
    wa = p.sb("ab_wa", [128, 16, 128], BF16)
    wx = p.sb("ab_wx", [128, 16, 128], BF16)
    wad = Dep()
    p.dma("pool", out=wa[:], in_=io["wa"], writes=[wad])
    p.dma("pool", out=wx[:], in_=io["wx"], writes=[wad])

    p.wait_deps([gd, lcwd, lcbd, scwd, scbd, bad, bxd, c8d, dtbd, anegd, Dvd, sngd, maskd, Ud, cd, identd, wad])

    x = p.sb("ab_x", [128, KD, n], F32)
    xd = Dep()
    xn = p.sb("ab_xn", [128, KD, n], BF16)
    xnd = Dep()
    xhal = p.sb("ab_xhal", [128, KD, 3], F32)
    xhald = Dep()
    xnh = p.sb("ab_xnh", [128, KD, 3], BF16)
    xnhd = Dep()
    hal = p.sb("ab_hal", [128, 40, 3], F32)
    hald = [Dep() for _ in range(40)]
    raw = [p.sb("ab_raw%d" % i, [128, n + 3], F32) for i in range(2)]
    rawd = [Dep() for _ in range(2)]
    acc = [p.sb("ab_acc%d" % i, [128, n], F32) for i in range(2)]
    accd = [Dep() for _ in range(2)]
    xbcT = p.sb("ab_xbcT", [128, 24, n], BF16)
    xbcTd = Dep()
    Xtok = p.sb("ab_Xtok", [128, NCH, D], BF16)
    Xtokd = Dep()
    Btok = p.sb("ab_Btok", [128, NCH, 512], BF16)
    Btokd = Dep()
    dt = p.sb("ab_dt", [128, NCH, 32], F32)
    dtd = Dep()
    dtA = p.sb("ab_dtA", [128, NCH, 32], F32)
    dtAd = Dep()
    cst = p.sb("ab_cst", [128, 64], F32)
    cstd = Dep()
    ecs = p.sb("ab_ecs", [128, 32], F32)
    etot = p.sb("ab_etot", [128, 32], F32)
    wend = p.sb("ab_wend", [128, 32], F32)
    smd = Dep()
    Xw = p.sb("ab_Xw", [128, D], BF16)
    Xwd = Dep()
    Xdt = p.sb("ab_Xdt", [128, D], BF16) if full else None
    Xdtd = Dep()
    R = p.sb("ab_R", [128, 8, 128], F32) if full else None
    Rd = Dep()
    LT = p.sb("ab_LT", [128, 8, 128], F32) if full else None
    LTd = Dep()
    mCB = p.sb("ab_mCB", [128, 128], F32) if full else None
    mCBd = Dep()
    ST = p.sb("ab_ST", [128, 8, 128], BF16) if full else None
    STd = Dep()
    t1 = p.sb("ab_t1", [128, 512], F32) if full else None
    t1d = Dep()
    ysb = p.sb("ab_ysb", [128, D], BF16) if full else None
    ysbd = Dep()
    yT = p.sb("ab_yT", [128, 16, n], BF16) if full else None
    yTd = Dep()
    St = p.sb("ab_St", [128, 4, 512], F32)
    Std = [Dep() for _ in range(4)]
    Stb = p.sb("ab_Stb", [128, 4, 512], BF16)
    Stbd = [Dep() for _ in range(4)]
    Tsum = p.sb("ab_Tsum", [128, 32], F32)
    Tsumd = Dep()
    hst = p.sb("ab_hst", [128, 16], F32)
    hstd = Dep()
    Ls = p.sb("ab_Ls", [128, 16], F32)
    Lsd = Dep()
    ycat = p.sb("ab_ycat", [128, 32, n], BF16) if full else None
    ycatd = Dep()
    ft = [(p.sb("ab_ft%d" % i, [128, n], F32) if (full or i >= 4) else None) for i in range(12)]
    ftd = [Dep() for _ in range(12)]
    xcb2 = [p.sb("ab_xcb%d" % i, [128, n], BF16) for i in range(2)]
    xcbd2 = [Dep() for _ in range(2)]
    sqc = [(p.sb("ab_sqc%d" % i, [128, n], BF16) if full else None) for i in range(2)]
    sqcd = [Dep() for _ in range(2)]
    psA = [p.ps("ab_psA%d" % i, [128, 512]) for i in range(2)]
    psAd = [Dep() for _ in range(2)]
    psT = p.ps("ab_psT", [128, 1024], BF16)
    psTd = Dep()
    psSeg = p.ps("ab_psSeg", [128, 1024])
    psSegd = Dep()
    psY = p.ps("ab_psY", [128, 512])
    psYd = Dep()
    psM = p.ps("ab_psM", [128, 512])
    psMd = Dep()
    pss, pssd = C["ps_ssq"], C["ps_ssqd"]

    w_in, w_out = io["w_in"], io["w_out"]

    if full:
        gath = io["gath"].ap().rearrange("(r p) w -> p r w", p=128)
        use, used = load_const(p, "ab_use", io["use"], [128, NCORES])
        gsm = p.sb("ab_gsm", [128, NCORES, 64], F32)
        gsmd = Dep()
        p.dma("sp", out=gsm[:], in_=gath[:, :, 2048:2112], writes=[gsmd])
        p.op("dve", lambda e: e.memset(St[:], 0.0), writes=Std)
        p.op("dve", lambda e: e.memset(hst[:], 0.0), writes=[hstd])
        tmpS = x[:, 0:8, :].rearrange("p a t -> p (a t)")
        St2 = St[:, :, :].rearrange("p g f -> p (g f)")
        for r in range(NCORES):
            p.op("act", lambda e: e.activation(out=gsm[:, r, 0:32], in_=gsm[:, r, 0:32], func=AF.Exp, scale=use[:, r:r + 1]), reads=[gsmd, used], writes=[gsmd])
            p.op("act", lambda e: e.activation(out=gsm[:, r, 48:64], in_=gsm[:, r, 48:64], func=AF.Exp, scale=use[:, r:r + 1]), reads=[gsmd, used], writes=[gsmd])
            p.op("dve", lambda e: e.tensor_tensor(out=hst[:], in0=hst[:], in1=gsm[:, r, 48:64], op=ALU.mult), reads=[gsmd, hstd], writes=[hstd])
            p.op("dve", lambda e: e.scalar_tensor_tensor(out=hst[:], in0=gsm[:, r, 32:48], scalar=use[:, r:r + 1], in1=hst[:], op0=ALU.mult, op1=ALU.add),
                 reads=[gsmd, used, hstd], writes=[hstd])
            p.dma("sp", out=tmpS, in_=gath[:, r, 0:2048], writes=[xd])
            p.op("dve", lambda e: e.tensor_tensor(out=St2.rearrange("p (h q) -> p h q", h=32), in0=St2.rearrange("p (h q) -> p h q", h=32),
                                                  in1=gsm[:, r, 0:32].unsqueeze(2).to_broadcast([128, 32, 64]), op=ALU.mult),
                 reads=[gsmd] + Std, writes=Std)
            p.op("dve", lambda e: e.scalar_tensor_tensor(out=St2, in0=tmpS, scalar=use[:, r:r + 1], in1=St2, op0=ALU.mult, op1=ALU.add),
                 reads=[xd, used] + Std, writes=Std)
    else:
        p.op("dve", lambda e: e.memset(St[:], 0.0), writes=Std)
        p.op("dve", lambda e: e.memset(Tsum[:], 0.0), writes=[Tsumd])
        p.op("dve", lambda e: e.memset(hst[:], 0.0), writes=[hstd])
        p.op("dve", lambda e: e.memset(Ls[:], 0.0), writes=[Lsd])
    for gq in range(4):
        p.op("act", lambda e: e.copy(out=Stb[:, gq, :], in_=St[:, gq, :]), reads=[Std[gq]], writes=[Stbd[gq]])

    p.dma("sp", out=xhal[:], in_=io["xh"].rearrange("(k p) t -> p k t", p=128), writes=[xhald])
    rmsnorm(p, C, xhal, xhald, 3, g, gd, xnh, xnhd)

    nb = 0
    nr = 0

    def proj_conv(wt, wd, j, hc, cw_, cb_, cwd_, cidx, t):
        nonlocal nb, nr
        b = nb % 2
        nb += 1
        rb = nr % 2
        nr += 1
        if t == 0:
            for k in range(KD):
                p.op("pe", lambda e: e.matmul(psM[:, 256 + 3 * hc:256 + 3 * hc + 3], lhsT=wt[:, k, j * 128:(j + 1) * 128], rhs=xnh[:, k, :],
                                              start=(k == 0), stop=(k == KD - 1)),
                     reads=[wd, xnhd], writes=[psMd])
            p.op("act", lambda e: e.copy(out=hal[:, hc, :], in_=psM[:, 256 + 3 * hc:256 + 3 * hc + 3]), reads=[psMd], writes=[hald[hc]])
        for k in range(KD):
            p.op("pe", lambda e: e.matmul(psA[b][:, 0:n], lhsT=wt[:, k, j * 128:(j + 1) * 128], rhs=xn[:, k, :], start=(k == 0), stop=(k == KD - 1)),
                 reads=[wd, xnd], writes=[psAd[b]])
        p.op("act", lambda e: e.copy(out=raw[rb][:, 3:n + 3], in_=psA[b][:, 0:n]), reads=[psAd[b]], writes=[rawd[rb]])
        p.op("act", lambda e: e.copy(out=raw[rb][:, 0:3], in_=hal[:, hc, :]), reads=[hald[hc]], writes=[rawd[rb]])
        conv4(p, raw[rb], rawd[rb], acc[rb], accd[rb], cw_, cb_, cwd_, cidx, n)
        p.op("act", lambda e: e.copy(out=hal[:, hc, :], in_=raw[rb][:, n:n + 3]), reads=[rawd[rb]], writes=[hald[hc]])
        return acc[rb], accd[rb]

    for t in range(ntiles):
        t0 = t * n
        p.dma("sp", out=x[:], in_=xT.rearrange("(k p) t -> p k t", p=128)[:, :, t0:t0 + n], writes=[xd])
        rmsnorm(p, C, x, xd, n, g, gd, xn, xnd)
        tc0 = t0 // 128
        dt_view = io["dt_s"].rearrange("(s p) h -> p s h", p=128)[:, tc0:tc0 + NCH, :]
        if full:
            p.dma("sp", out=dt[:], in_=dt_view, writes=[dtd])
        else:
            wt, wd = ws.next((w_in, 0, KD, 9216, 32))
            b = nb % 2
            nb += 1
        for sc in range(NCH if not full else 0):
            for k in range(KD):
                p.op("pe", lambda e: e.matmul(psA[b][:, sc * 32:(sc + 1) * 32], lhsT=xn[:, k, sc * 128:(sc + 1) * 128], rhs=wt[:, k, 0:32],
                                              start=(k == 0), stop=(k == KD - 1)),
                     reads=[wd, xnd], writes=[psAd[b]])
        if not full:
            p.op("dve", lambda e: e.tensor_tensor(out=dt[:], in0=psA[b][:, 0:NCH * 32].rearrange("p (s h) -> p s h", s=NCH),
                                                  in1=dtb[:, :].unsqueeze(1).to_broadcast([128, NCH, 32]), op=ALU.add),
                 reads=[psAd[b], dtbd], writes=[dtd])
            p.op("act", lambda e: e.activation(out=dt[:], in_=dt[:], func=AF.Exp), reads=[dtd], writes=[dtd])
            p.op("act", lambda e: e.activation(out=dt[:], in_=dt[:], func=AF.Ln, bias=one1[:, 0:1], scale=1.0), reads=[dtd, cd], writes=[dtd])
            p.dma("sp", out=dt_view, in_=dt[:], reads=[dtd])
        p.op("dve", lambda e: e.tensor_tensor(out=dtA[:], in0=dt[:], in1=aneg[:, :].unsqueeze(1).to_broadcast([128, NCH, 32]), op=ALU.mult),
             reads=[dtd, anegd], writes=[dtAd])
        xbc_view = io["xbc_s"].rearrange("(c p) t -> p c t", p=128)[:, :, t0:t0 + n]
        if full:
            p.dma("pool", out=xbcT[:], in_=xbc_view, writes=[xbcTd])
        for i in range(12 if not full else 0):
            wt, wd = ws.next((w_in, 0, KD, 6144 + i * 256, 256))
            for j in range(2):
                c = i * 2 + j
                a_, ad_ = proj_conv(wt, wd, j, 16 + c, scw, scb, scwd, c, t)
                p.op("act", lambda e: e.activation(out=xbcT[:, c, :], in_=a_[:, 0:n], func=AF.Silu), reads=[ad_], writes=[xbcTd])
        if not full:
            p.dma("pool", out=xbc_view, in_=xbcT[:], reads=[xbcTd])
        for sc in range(NCH):
            c0 = sc * 128
            for r in range(2):
                for cc in range(8):
                    c = r * 8 + cc
                    p.op("pe", lambda e: e.transpose(out=psT[:, cc * 128:(cc + 1) * 128], in_=xbcT[:, c, c0:c0 + 128], identity=ident[:]),
                         reads=[xbcTd, identd], writes=[psTd])
                p.op("act", lambda e: e.copy(out=Xtok[:, sc, r * 1024:(r + 1) * 1024], in_=psT[:, :]), reads=[psTd], writes=[Xtokd])
            for gq in range(4):
                p.op("pe", lambda e: e.transpose(out=psT[:, gq * 128:(gq + 1) * 128], in_=xbcT[:, 16 + gq, c0:c0 + 128], identity=ident[:]),
                     reads=[xbcTd, identd], writes=[psTd])
            p.op("act", lambda e: e.copy(out=Btok[:, sc, :], in_=psT[:, 0:512]), reads=[psTd], writes=[Btokd])
        def gen_ssd():
            nonlocal nb
            for sc in range(NCH):
                c0 = sc * 128
                p.op("pe", lambda e: e.matmul(psM[:, 0:32], lhsT=mask[:], rhs=dtA[:, sc, :], start=True, stop=True), reads=[maskd, dtAd], writes=[psMd])
                p.op("pe", lambda e: e.matmul(psM[:, 32:64], lhsT=ones[:], rhs=dtA[:, sc, :], start=True, stop=True), reads=[cd, dtAd], writes=[psMd])
                p.op("act", lambda e: e.copy(out=cst[:], in_=psM[:, 0:64]), reads=[psMd], writes=[cstd])
                p.op("act", lambda e: e.activation(out=ecs[:], in_=cst[:, 0:32], func=AF.Exp), reads=[cstd], writes=[smd])
                p.op("act", lambda e: e.activation(out=etot[:], in_=cst[:, 32:64], func=AF.Exp), reads=[cstd], writes=[smd])
                p.op("dve", lambda e: e.tensor_tensor(out=wend[:], in0=cst[:, 32:64], in1=cst[:, 0:32], op=ALU.subtract), reads=[cstd], writes=[smd])
                p.op("act", lambda e: e.activation(out=wend[:], in_=wend[:], func=AF.Exp), reads=[smd], writes=[smd])
                p.op("dve", lambda e: e.tensor_tensor(out=wend[:], in0=wend[:], in1=dt[:, sc, :], op=ALU.mult), reads=[smd, dtd], writes=[smd])
                if not full:
                    p.op("dve", lambda e: e.tensor_tensor(out=Tsum[:], in0=Tsum[:], in1=cst[:, 32:64], op=ALU.add), reads=[cstd, Tsumd], writes=[Tsumd])
                p.op("dve", lambda e: e.tensor_tensor(out=Xw[:, :].rearrange("p (h q) -> p h q", h=32), in0=Xtok[:, sc, :].rearrange("p (h q) -> p h q", h=32),
                                                      in1=wend[:, :].unsqueeze(2).to_broadcast([128, 32, 64]), op=ALU.mult),
                     reads=[Xtokd, smd], writes=[Xwd])
                if full:
                    p.op("dve", lambda e: e.tensor_tensor(out=Xdt[:, :].rearrange("p (h q) -> p h q", h=32), in0=Xtok[:, sc, :].rearrange("p (h q) -> p h q", h=32),
                                                          in1=dt[:, sc, :].unsqueeze(2).to_broadcast([128, 32, 64]), op=ALU.mult),
                         reads=[Xtokd, dtd], writes=[Xdtd])
                for gq in range(4):
                    if full:
                        p.op("dve", lambda e: e.tensor_tensor(out=R[:], in0=dtA[:, sc, gq * 8:(gq + 1) * 8].unsqueeze(2).to_broadcast([128, 8, 128]),
                                                              in1=mask[:, :].unsqueeze(1).to_broadcast([128, 8, 128]), op=ALU.mult),
                             reads=[dtAd, maskd], writes=[Rd])
                        for hf in range(2):
                            p.op("pe", lambda e: e.matmul(psSeg[:, hf * 512:(hf + 1) * 512], lhsT=U[:], rhs=R[:, hf * 4:(hf + 1) * 4, :].rearrange("p h i -> p (h i)"),
                                                          start=True, stop=True),
                                 reads=[Ud, Rd], writes=[psSegd])
                        for hf in range(2):
                            p.op("act", lambda e: e.activation(out=LT[:, hf * 4:(hf + 1) * 4, :].rearrange("p h i -> p (h i)"), in_=psSeg[:, hf * 512:(hf + 1) * 512], func=AF.Exp),
                                 reads=[psSegd], writes=[LTd])
                        p.op("pe", lambda e: e.matmul(psM[:, 128:256], lhsT=xbcT[:, 16 + gq, c0:c0 + 128], rhs=xbcT[:, 20 + gq, c0:c0 + 128], start=True, stop=True),
                             reads=[xbcTd], writes=[psMd])
                        p.op("dve", lambda e: e.tensor_tensor(out=mCB[:], in0=psM[:, 128:256], in1=mask[:], op=ALU.mult), reads=[psMd, maskd], writes=[mCBd])
                        p.op("dve", lambda e: e.tensor_tensor(out=ST[:], in0=LT[:], in1=mCB[:, :].unsqueeze(1).to_broadcast([128, 8, 128]), op=ALU.mult),
                             reads=[LTd, mCBd], writes=[STd])
                        for hl in range(8):
                            h_ = gq * 8 + hl
                            p.op("pe", lambda e: e.matmul(psY[:, hl * 64:(hl + 1) * 64], lhsT=ST[:, hl, :], rhs=Xdt[:, h_ * 64:(h_ + 1) * 64], start=True, stop=True),
                                 reads=[STd, Xdtd], writes=[psYd])
                        b = nb % 2
                        nb += 1
                        p.op("pe", lambda e: e.matmul(psA[b][:], lhsT=xbcT[:, 20 + gq, c0:c0 + 128], rhs=Stb[:, gq, :], start=True, stop=True),
                             reads=[xbcTd, Stbd[gq]], writes=[psAd[b]])
                        p.op("dve", lambda e: e.tensor_tensor(out=t1[:, :].rearrange("p (h q) -> p h q", h=8), in0=psA[b][:, :].rearrange("p (h q) -> p h q", h=8),
                                                              in1=ecs[:, gq * 8:(gq + 1) * 8].unsqueeze(2).to_broadcast([128, 8, 64]), op=ALU.mult),
                             reads=[psAd[b], smd], writes=[t1d])
                        p.op("dve", lambda e: e.tensor_tensor(out=ysb[:, gq * 512:(gq + 1) * 512], in0=t1[:], in1=psY[:], op=ALU.add),
                             reads=[t1d, psYd], writes=[ysbd])
                    b = nb % 2
                    nb += 1
                    p.op("pe", lambda e: e.matmul(psA[b][:], lhsT=Btok[:, sc, gq * 128:(gq + 1) * 128], rhs=Xw[:, gq * 512:(gq + 1) * 512], start=True, stop=True),
                         reads=[Btokd, Xwd], writes=[psAd[b]])
                    p.op("dve", lambda e: e.tensor_tensor(out=St[:, gq, :].rearrange("p (h q) -> p h q", h=8), in0=St[:, gq, :].rearrange("p (h q) -> p h q", h=8),
                                                          in1=etot[:, gq * 8:(gq + 1) * 8].unsqueeze(2).to_broadcast([128, 8, 64]), op=ALU.mult),
                         reads=[smd, Std[gq]], writes=[Std[gq]])
                    p.op("dve", lambda e: e.tensor_tensor(out=St[:, gq, :], in0=St[:, gq, :], in1=psA[b][:], op=ALU.add), reads=[psAd[b], Std[gq]], writes=[Std[gq]])
                    p.op("act", lambda e: e.copy(out=Stb[:, gq, :], in_=St[:, gq, :]), reads=[Std[gq]], writes=[Stbd[gq]])
                    yield
                if full:
                    for r in range(2):
                        for cc in range(8):
                            c = r * 8 + cc
                            p.op("pe", lambda e: e.transpose(out=psT[:, cc * 128:(cc + 1) * 128], in_=ysb[:, c * 128:(c + 1) * 128], identity=ident[:]),
                                 reads=[ysbd, identd], writes=[psTd])
                        p.op("act", lambda e: e.copy(out=yT[:, r * 8:(r + 1) * 8, c0:c0 + 128], in_=psT[:, :].rearrange("p (c s) -> p c s", c=8)),
                             reads=[psTd], writes=[yTd])

        def gen_lru():
            nonlocal nb
            for i in range(8):
                if not full:
                    wtx, wdx = ws.next((w_in, 0, KD, 2048 + i * 256, 256))
                if full:
                    wtg, wdg = ws.next((w_in, 0, KD, i * 256, 256))
                for j in range(2):
                    hb = i * 2 + j
                    if full:
                        nonlocal nr
                        rb_ = nr % 2
                        nr += 1
                        xc, xcd = acc[rb_], accd[rb_]
                        p.dma("sp", out=xc[:, 0:n], in_=io["xc_s"][hb * 128:(hb + 1) * 128, t0:t0 + n], writes=[xcd])
                    else:
                        xc, xcd = proj_conv(wtx, wdx, j, hb, lcw, lcb, lcwd, hb, t)
                        p.dma("sp", out=io["xc_s"][hb * 128:(hb + 1) * 128, t0:t0 + n], in_=xc[:, 0:n], reads=[xcd])
                    par = hb % 2
                    xcb, xcbd = xcb2[par], xcbd2[par]
                    p.op("act", lambda e: e.copy(out=xcb[:], in_=xc[:, 0:n]), reads=[xcd], writes=[xcbd])
                    r_, rd_ = ft[4 + 4 * par], ftd[4 + 4 * par]
                    i_, id_ = ft[5 + 4 * par], ftd[5 + 4 * par]
                    a_, ad_ = ft[6 + 4 * par], ftd[6 + 4 * par]
                    s_, sd_ = ft[7 + 4 * par], ftd[7 + 4 * par]
                    b = nb % 2
                    nb += 1
                    p.op("pe", lambda e: e.matmul(psA[b][:, 0:n], lhsT=wa[:, hb, :], rhs=xcb[:], start=True, stop=True), reads=[wad, xcbd], writes=[psAd[b]])
                    p.op("act", lambda e: e.activation(out=r_[:], in_=psA[b][:, 0:n], func=AF.Sigmoid, bias=ba[:, hb:hb + 1], scale=1.0), reads=[psAd[b], bad], writes=[rd_])
                    b = nb % 2
                    nb += 1
                    p.op("pe", lambda e: e.matmul(psA[b][:, 0:n], lhsT=wx[:, hb, :], rhs=xcb[:], start=True, stop=True), reads=[wad, xcbd], writes=[psAd[b]])
                    p.op("act", lambda e: e.activation(out=i_[:], in_=psA[b][:, 0:n], func=AF.Sigmoid, bias=bx[:, hb:hb + 1], scale=1.0), reads=[psAd[b], bad], writes=[id_])
                    p.op("act", lambda e: e.activation(out=a_[:], in_=r_[:], func=AF.Exp, scale=c8[:, hb:hb + 1]), reads=[rd_, c8d], writes=[ad_])
                    p.op("act", lambda e: e.activation(out=s_[:], in_=r_[:], func=AF.Exp, scale=c16[:, hb:hb + 1]), reads=[rd_, cd], writes=[sd_])
                    p.op("act", lambda e: e.activation(out=s_[:], in_=s_[:], func=AF.Sqrt, bias=one1[:, 0:1], scale=-1.0), reads=[sd_, cd], writes=[sd_])
                    p.op("dve", lambda e: e.tensor_tensor(out=i_[:], in0=i_[:], in1=xc[:, 0:n], op=ALU.mult), reads=[id_, xcd], writes=[id_])
                    p.op("dve", lambda e: e.tensor_tensor(out=i_[:], in0=i_[:], in1=s_[:], op=ALU.mult), reads=[id_, sd_], writes=[id_])
                    p.op("dve", lambda e: e.tensor_tensor_scan(out=s_[:], data0=a_[:], data1=i_[:], initial=hst[:, hb:hb + 1], op0=ALU.mult, op1=ALU.add),
                         reads=[ad_, id_, hstd, sd_], writes=[sd_])
                    p.op("dve", lambda e: e.tensor_copy(out=hst[:, hb:hb + 1], in_=s_[:, n - 1:n]), reads=[sd_], writes=[hstd])
                    if not full:
                        p.op("dve", lambda e: e.reduce_sum(out=a_[:, 0:1], in_=r_[:], axis=AX.X), reads=[rd_, ad_], writes=[ad_])
                        p.op("dve", lambda e: e.tensor_tensor(out=Ls[:, hb:hb + 1], in0=Ls[:, hb:hb + 1], in1=a_[:, 0:1], op=ALU.add), reads=[ad_, Lsd], writes=[Lsd])
                    else:
                        b = nb % 2
                        nb += 1
                        for k in range(KD):
                            p.op("pe", lambda e: e.matmul(psA[b][:, 0:n], lhsT=wtg[:, k, j * 128:(j + 1) * 128], rhs=xn[:, k, :], start=(k == 0), stop=(k == KD - 1)),
                                 reads=[wdg, xnd], writes=[psAd[b]])
                        p.op("act", lambda e: e.activation(out=r_[:], in_=psA[b][:, 0:n], func=AF.Gelu_apprx_tanh), reads=[psAd[b]], writes=[rd_])
                        p.op("dve", lambda e: e.tensor_tensor(out=ycat[:, hb, :], in0=r_[:], in1=s_[:], op=ALU.mult), reads=[rd_, sd_], writes=[ycatd])
                    yield

        _interleave([gen_ssd(), gen_lru()])
        if full:
            for i in range(8):
                wt, wd = ws.next((w_in, 0, KD, 4096 + i * 256, 256))
                for j in range(2):
                    c = i * 2 + j
                    b = nb % 2
                    nb += 1
                    for k in range(KD):
                        p.op("pe", lambda e: e.matmul(psA[b][:, 0:n], lhsT=wt[:, k, j * 128:(j + 1) * 128], rhs=xn[:, k, :], start=(k == 0), stop=(k == KD - 1)),
                             reads=[wd, xnd], writes=[psAd[b]])
                    f0, f0d = ft[c % 2], ftd[c % 2]
                    f1, f1d = ft[2 + c % 2], ftd[2 + c % 2]
                    p.op("act", lambda e: e.activation(out=f0[:], in_=psA[b][:, 0:n], func=AF.Silu), reads=[psAd[b]], writes=[f0d])
                    p.op("dve", lambda e: e.scalar_tensor_tensor(out=f1[:], in0=xbcT[:, c, :], scalar=Dv[:, c:c + 1], in1=yT[:, c, :], op0=ALU.mult, op1=ALU.add),
                         reads=[xbcTd, Dvd, yTd], writes=[f1d])
                    p.op("dve", lambda e: e.tensor_tensor(out=ycat[:, 16 + c, :], in0=f1[:], in1=f0[:], op=ALU.mult), reads=[f0d, f1d], writes=[ycatd])
                    sb_ = c % 2
                    p.op("act", lambda e: e.activation(out=sqc[sb_][:], in_=ycat[:, 16 + c, :], func=AF.Square), reads=[ycatd], writes=[sqcd[sb_]])
                    p.op("pe", lambda e: e.matmul(pss[:, 0:n], lhsT=C["ones_bf"][:], rhs=sqc[sb_][:], start=(c % 4 == 0), stop=(c % 4 == 3)),
                         reads=[sqcd[sb_], C["constd"]], writes=[pssd])
                    if c % 4 == 3:
                        rstd_from_ms(p, C, pss, pssd, n, scale=4.0)
                        for c2 in range(c - 3, c + 1):
                            p.op("dve", lambda e: e.scalar_tensor_tensor(out=ycat[:, 16 + c2, :], in0=ycat[:, 16 + c2, :], scalar=sng[:, c2:c2 + 1],
                                                                          in1=C["rstd"][:, 0:n], op0=ALU.mult, op1=ALU.mult),
                                 reads=[ycatd, sngd, C["rstdd"]], writes=[ycatd])
        if full:
            for og in range(8):
                w0, w0d = ws.next((w_out, 0, KD, og * 256, 256))
                w1, w1d = ws.next((w_out, KD, KD, og * 256, 256))
                for j in range(2):
                    do = og * 2 + j
                    b = nb % 2
                    nb += 1
                    for k in range(32):
                        wt_, wd_ = (w0, w0d) if k < 16 else (w1, w1d)
                        p.op("pe", lambda e: e.matmul(psA[b][:, 0:n], lhsT=wt_[:, k % 16, j * 128:(j + 1) * 128], rhs=ycat[:, k, :], start=(k == 0), stop=(k == 31)),
                             reads=[wd_, ycatd], writes=[psAd[b]])
                    p.op("dve", lambda e: e.tensor_tensor(out=x[:, do, :], in0=x[:, do, :], in1=psA[b][:, 0:n], op=ALU.add), reads=[psAd[b], xd], writes=[xd])
            p.dma("sp", out=io["yT"].rearrange("(k p) t -> p k t", p=128)[:, :, t0:t0 + n], in_=x[:], reads=[xd])
    if not full:
        p.op("dve", lambda e: e.tensor_tensor(out=Ls[:], in0=Ls[:], in1=c8[:], op=ALU.mult), reads=[Lsd, c8d], writes=[Lsd])
        bn = io["bounce"].ap()
        p.dma("sp", out=bn[:, 0:2048].rearrange("p (g f) -> p g f", g=4), in_=St[:], reads=Std, writes=[io["bounced"]])
        p.dma("sp", out=bn[:, 2048:2080], in_=Tsum[:], reads=[Tsumd], writes=[io["bounced"]])
        p.dma("sp", out=bn[:, 2080:2096], in_=hst[:], reads=[hstd], writes=[io["bounced"]])
        p.dma("sp", out=bn[:, 2096:2112], in_=Ls[:], reads=[Lsd], writes=[io["bounced"]])


def build_ab(ntok, phase):
    p = Prog()
    io = {"xT": p.dram("xT", [D, ntok], F32, "ExternalInput"), "xh": p.dram("xh", [D, 3], F32, "ExternalInput")}
    for nm, shp in (("g", [128, KD]), ("lcw", [128, 16, 4]), ("lcb", [128, 16]), ("scw", [128, 24, 4]), ("scb", [128, 24]), ("ba", [128, 16]),
                    ("bx", [128, 16]), ("lam", [128, 16]), ("dtb", [128, 32]), ("alog", [128, 32]), ("Dv", [128, 16]), ("sng", [128, 16]),
                    ("mask", [128, 128]), ("U", [128, 128]), ("ident", [128, 128]), ("wa", [128, 16, 128]), ("wx", [128, 16, 128]),
                    ("w_in", [D, 9248]), ("w_out", [2 * D, D])):
        io[nm] = p.dram(nm, shp, F32, "ExternalInput")
    if phase == 2:
        io["Sprev"] = p.dram("Sprev", [3, 128, 4, 512], F32, "ExternalInput")
        io["Tprev"] = p.dram("Tprev", [3, 128, 32], F32, "ExternalInput")
        io["Hprev"] = p.dram("Hprev", [3, 128, 2, 16], F32, "ExternalInput")
        io["yT"] = p.dram("yT", [D, ntok], F32, "ExternalOutput")
    else:
        io["Sout"] = p.dram("Sout", [128, 4, 512], F32, "ExternalOutput")
        io["Tout"] = p.dram("Tout", [128, 32], F32, "ExternalOutput")
        io["Hout"] = p.dram("Hout", [128, 2, 16], F32, "ExternalOutput")
    C = common(p)
    ws = WStream(p, slot_elems=4096)
    stage_ab(p, C, ws, io, ntok // TA, phase)
    p.finish()
    return p


def chunkT(w, nch, kk):
    return np.ascontiguousarray(np.asarray(w, np.float32).T.reshape(nch, 128, kk).transpose(1, 0, 2))


def ab_inputs(inp):
    f = lambda a: np.ascontiguousarray(a, dtype=np.float32)
    return {
        "g": vecT(inp["norm_mix_g"][0]),
        "lcw": chunkT(inp["lru_conv_w"][0], 16, 4), "lcb": vecT(inp["lru_conv_b"][0]),
        "scw": chunkT(inp["ssd_conv_w"][0], 24, 4), "scb": vecT(inp["ssd_conv_b"][0]),
        "ba": vecT(inp["lru_b_a"][0]), "bx": vecT(inp["lru_b_x"][0]), "lam": vecT(inp["lru_lambda"][0]),
        "dtb": f(np.tile(np.asarray(inp["ssd_dt_bias"][0], np.float32)[None, :], (128, 1))),
        "alog": f(np.tile(np.asarray(inp["ssd_a_log"][0], np.float32)[None, :], (128, 1))),
        "Dv": vecT(np.repeat(np.asarray(inp["ssd_d"][0], np.float32), 64)),
        "sng": vecT(inp["ssd_norm_g"][0]),
        "mask": np.triu(np.ones((128, 128), np.float32)),
        "U": np.tril(np.ones((128, 128), np.float32), -1),
        "ident": np.eye(128, dtype=np.float32),
        "wa": f(np.asarray(inp["lru_w_a"][0]).transpose(1, 0, 2)),
        "wx": f(np.asarray(inp["lru_w_x"][0]).transpose(1, 0, 2)),
        "w_in": f(inp["ab_w_in"][0]), "w_out": f(inp["ab_w_out"][0]),
    }


NSEG = 4

_AB_SHAPES = (("g", [128, KD]), ("lcw", [128, 16, 4]), ("lcb", [128, 16]), ("scw", [128, 24, 4]), ("scb", [128, 24]), ("ba", [128, 16]),
              ("bx", [128, 16]), ("lam", [128, 16]), ("dtb", [128, 32]), ("alog", [128, 32]), ("Dv", [128, 16]), ("sng", [128, 16]),
              ("mask", [128, 128]), ("U", [128, 128]), ("ident", [128, 128]), ("wa", [128, 16, 128]), ("wx", [128, 16, 128]),
              ("w_in", [D, 9248]), ("w_out", [2 * D, D]))
_GLA_SHAPES = (("g", [128, KD]), ("ng", [128, KD]), ("bg", [128, 8]), ("mask", [128, 128]), ("ident", [128, 128]), ("wgu", [16, 1024]),
               ("w_in", [D, 6160]), ("w_out", [D, D]))
_XA_SHAPES = (("g", [128, KD]), ("mg", [128, KD]), ("ident", [128, 128]), ("w_q", [D, D]), ("w_kv", [D, 2 * D]), ("w_o", [D, D]))
_FFN_SHAPES = (("g", [128, KD]), ("cw", [128, 96, 3]), ("cb", [128, 96]), ("fg", [128, KD]), ("w_in", [D, 12288]), ("w_out", [6144, D]))


def build_fused(nt):
    p = Prog()
    nc = p.nc

    def ext(prefix, shapes):
        return {nm: p.dram(prefix + nm, shp, F32, "ExternalInput") for nm, shp in shapes}

    xT = p.dram("xT", [D, nt], F32, "ExternalInput")
    xh3 = p.dram("xh3", [D, 3], F32, "ExternalInput")
    memT = p.dram("memT", [D, 256], F32, "ExternalInput")
    use = p.dram("use", [128, NCORES], F32, "ExternalInput")
    usel = p.dram("usel", [128, NCORES], F32, "ExternalInput")
    yT = p.dram("yT", [D, nt], F32, "ExternalOutput")
    ab = ext("ab_", _AB_SHAPES)
    gla = ext("gla_", _GLA_SHAPES)
    xa = [ext("xa%d_" % l, _XA_SHAPES) for l in range(2)]
    ffn = [ext("ffn%d_" % l, _FFN_SHAPES) for l in range(2)]
    actA = nc.dram_tensor("actA", [D, nt], F32).ap()
    actB = nc.dram_tensor("actB", [D, nt], F32).ap()
    bn_ab = nc.dram_tensor("bn_ab", [128, 2112], F32)
    ga_ab = nc.dram_tensor("ga_ab", [NCORES * 128, 2112], F32)
    bn_g = nc.dram_tensor("bn_g", [128, 4104], F32)
    ga_g = nc.dram_tensor("ga_g", [NCORES * 128, 4104], F32)
    bn_h = [nc.dram_tensor("bn_h%d" % l, [128, 32], F32) for l in range(2)]
    ga_h = [nc.dram_tensor("ga_h%d" % l, [NCORES * 128, 32], F32) for l in range(2)]

    ab["dt_s"] = nc.dram_tensor("dt_s", [nt, 32], F32).ap()
    ab["xbc_s"] = nc.dram_tensor("xbc_s", [3072, nt], F32).ap()
    ab["xc_s"] = nc.dram_tensor("xc_s", [D, nt], F32).ap()
    C = common(p)
    bd = Dep()
    with p.scope():
        stage_ab(p, C, WStream(p, slot_elems=4096), dict(ab, xT=xT, xh=xh3, bounce=bn_ab, bounced=bd), nt // TA1, 1)
    p.allgather(bn_ab, ga_ab)
    p.barrier()
    with p.scope():
        stage_ab(p, C, WStream(p, slot_elems=4096), dict(ab, xT=xT, xh=xh3, gath=ga_ab, use=use, yT=actA), nt // TA, 2)
    cur, nxt = actA, actB
    for l in range(2):
        if l == 1:
            with p.scope():
                stage_gla(p, C, WStream(p, slot_elems=4096), dict(gla, xT=cur, bounce=bn_g, bounced=bd), nt // T, 1)
            p.allgather(bn_g, ga_g)
            p.barrier()
            with p.scope():
                stage_gla(p, C, WStream(p, slot_elems=4096), dict(gla, xT=cur, gath=ga_g, use=use, yT=nxt), nt // T, 2)
            cur, nxt = nxt, cur
        with p.scope():
            stage_xa(p, C, WStream(p), dict(xa[l], xT=cur, memT=memT, yT=nxt, hout=bn_h[l], houtd=bd), nt // T)
        cur, nxt = nxt, cur
        p.allgather(bn_h[l], ga_h[l])
        p.barrier()
        with p.scope():
            stage_ffn(p, C, WStream(p), dict(ffn[l], xT=cur, gath_h=ga_h[l], usel=usel, yT=(yT if l == 1 else nxt)), nt // T, l == 1)
        cur, nxt = nxt, cur
    p.finish()
    return p


def make_maps(inp, nt):
    inp = {k: np.asarray(v) for k, v in inp.items()}
    x = np.asarray(inp["x"], np.float32)
    mem = np.asarray(inp["mem"], np.float32)
    shared = {}
    for pre, d in (("ab_", ab_inputs(inp)), ("gla_", gla_inputs(inp))):
        shared.update({pre + k: v for k, v in d.items()})
    for l in range(2):
        shared.update({"xa%d_%s" % (l, k): v for k, v in xa_inputs(inp, l).items()})
        shared.update({"ffn%d_%s" % (l, k): v for k, v in ffn_inputs(inp, l).items()})
    maps = []
    for c in range(NCORES):
        b, q = divmod(c, NSEG)
        m = dict(shared)
        m["xT"] = np.ascontiguousarray(x[b, q * nt:(q + 1) * nt].T)
        m["xh3"] = np.ascontiguousarray(x[b, q * nt - 3:q * nt].T) if q > 0 else np.zeros((D, 3), np.float32)
        m["memT"] = np.ascontiguousarray(mem[b].T)
        use = np.zeros((128, NCORES), np.float32)
        use[:, b * NSEG:c] = 1.0
        usel = np.zeros((128, NCORES), np.float32)
        if q > 0:
            usel[:, c - 1] = 1.0
        m["use"] = use
        m["usel"] = usel
        maps.append(m)
    return maps


def kernel_impl(inp, nt):
    maps = make_maps(inp, nt)
    p = build_fused(nt)
    res = run_bass_kernel_spmd(p.nc, maps, core_ids=list(range(NCORES))).results
    out = np.empty((2, NSEG * nt, D), np.float32)
    for c in range(NCORES):
        b, q = divmod(c, NSEG)
        out[b, q * nt:(q + 1) * nt] = res[c]["yT"].T
    return out


def build_stage(kind, nt, layer=0, phase=0):
    p = Prog()
    nc = p.nc

    def ext(shapes):
        return {nm: p.dram(nm, shp, F32, "ExternalInput") for nm, shp in shapes}

    C = common(p)
    if kind == "ab":
        io = ext(_AB_SHAPES)
        io["xT"] = p.dram("xT", [D, nt], F32, "ExternalInput")
        io["xh"] = p.dram("xh3", [D, 3], F32, "ExternalInput")
        kd_ = "ExternalOutput" if phase == 1 else "ExternalInput"
        io["dt_s"] = p.dram("dt_s", [nt, 32], F32, kd_)
        io["xbc_s"] = p.dram("xbc_s", [3072, nt], F32, kd_)
        io["xc_s"] = p.dram("xc_s", [D, nt], F32, kd_)
        if phase == 1:
            io["bounce"] = nc.dram_tensor("bounce", [128, 2112], F32, kind="ExternalOutput")
            io["bounced"] = Dep()
        else:
            io["gath"] = nc.dram_tensor("gath", [NCORES * 128, 2112], F32, kind="ExternalInput")
            io["use"] = p.dram("use", [128, NCORES], F32, "ExternalInput")
            io["yT"] = p.dram("yT", [D, nt], F32, "ExternalOutput")
        stage_ab(p, C, WStream(p, slot_elems=4096), io, nt // (TA1 if phase == 1 else TA), phase)
    elif kind == "gla":
        io = ext(_GLA_SHAPES)
        io["xT"] = p.dram("xT", [D, nt], F32, "ExternalInput")
        if phase == 1:
            io["bounce"] = nc.dram_tensor("bounce", [128, 4104], F32, kind="ExternalOutput")
            io["bounced"] = Dep()
        else:
            io["gath"] = nc.dram_tensor("gath", [NCORES * 128, 4104], F32, kind="ExternalInput")
            io["use"] = p.dram("use", [128, NCORES], F32, "ExternalInput")
            io["yT"] = p.dram("yT", [D, nt], F32, "ExternalOutput")
        stage_gla(p, C, WStream(p, slot_elems=4096), io, nt // T, phase)
    elif kind == "xa":
        io = ext(_XA_SHAPES)
        io["xT"] = p.dram("xT", [D, nt], F32, "ExternalInput")
        io["memT"] = p.dram("memT", [D, 256], F32, "ExternalInput")
        io["yT"] = p.dram("yT", [D, nt], F32, "ExternalOutput")
        io["hout"] = nc.dram_tensor("hout", [128, 32], F32, kind="ExternalOutput")
        io["houtd"] = Dep()
        stage_xa(p, C, WStream(p), io, nt // T)
    else:
        io = ext(_FFN_SHAPES)
        io["xT"] = p.dram("xT", [D, nt], F32, "ExternalInput")
        io["gath_h"] = nc.dram_tensor("gath_h", [NCORES * 128, 32], F32, kind="ExternalInput")
        io["usel"] = p.dram("usel", [128, NCORES], F32, "ExternalInput")
        io["yT"] = p.dram("yT", [D, nt], F32, "ExternalOutput")
        stage_ffn(p, C, WStream(p), io, nt // T, layer == 1)
    p.finish()
    return p


def kernel_unfused(inp, nt):
    maps = make_maps(inp, nt)

    def run(p, extra):
        keys = None
        ms = []
        for c in range(NCORES):
            m = dict(extra[c])
            ms.append(m)
        return run_bass_kernel_spmd(p.nc, ms, core_ids=list(range(NCORES))).results

    def sub(pre, c, more):
        m = {k[len(pre):]: v for k, v in maps[c].items() if k.startswith(pre)}
        m.update(more)
        return m

    xTs = [maps[c]["xT"] for c in range(NCORES)]
    r = run(build_stage("ab", nt, phase=1), [sub("ab_", c, dict(xT=xTs[c], xh3=maps[c]["xh3"])) for c in range(NCORES)])
    gath = np.ascontiguousarray(np.concatenate([r[c]["bounce"] for c in range(NCORES)], axis=0))
    r = run(build_stage("ab", nt, phase=2), [sub("ab_", c, dict(xT=xTs[c], xh3=maps[c]["xh3"], gath=gath, use=maps[c]["use"],
                                                              dt_s=r[c]["dt_s"], xbc_s=r[c]["xbc_s"], xc_s=r[c]["xc_s"])) for c in range(NCORES)])
    xTs = [r[c]["yT"] for c in range(NCORES)]
    for l in range(2):
        if l == 1:
            r = run(build_stage("gla", nt, phase=1), [sub("gla_", c, dict(xT=xTs[c])) for c in range(NCORES)])
            gath = np.ascontiguousarray(np.concatenate([r[c]["bounce"] for c in range(NCORES)], axis=0))
            r = run(build_stage("gla", nt, phase=2), [sub("gla_", c, dict(xT=xTs[c], gath=gath, use=maps[c]["use"])) for c in range(NCORES)])
            xTs = [r[c]["yT"] for c in range(NCORES)]
        r = run(build_stage("xa", nt), [sub("xa%d_" % l, c, dict(xT=xTs[c], memT=maps[c]["memT"])) for c in range(NCORES)])
        xTs = [r[c]["yT"] for c in range(NCORES)]
        gath_h = np.ascontiguousarray(np.concatenate([r[c]["hout"] for c in range(NCORES)], axis=0))
        r = run(build_stage("ffn", nt, layer=l), [sub("ffn%d_" % l, c, dict(xT=xTs[c], gath_h=gath_h, usel=maps[c]["usel"])) for c in range(NCORES)])
        xTs = [r[c]["yT"] for c in range(NCORES)]
    out = np.empty((2, NSEG * nt, D), np.float32)
    for c in range(NCORES):
        b, q = divmod(c, NSEG)
        out[b, q * nt:(q + 1) * nt] = xTs[c].T
    return out


def build_group(gid, nt):
    p = Prog()
    nc = p.nc

    def ext(prefix, shapes):
        return {nm: p.dram(prefix + nm, shp, F32, "ExternalInput") for nm, shp in shapes}

    xT = p.dram("xT", [D, nt], F32, "ExternalInput")
    yT = p.dram("yT", [D, nt], F32, "ExternalOutput")
    mid = nc.dram_tensor("mid", [D, nt], F32).ap()
    C = common(p)
    if gid == 0:
        ab = ext("ab_", _AB_SHAPES)
        for nm, shp in (("dt_s", [nt, 32]), ("xbc_s", [3072, nt]), ("xc_s", [D, nt])):
            ab[nm] = p.dram(nm, shp, F32, "ExternalInput")
        ab.update(xT=xT, xh=p.dram("xh3", [D, 3], F32, "ExternalInput"), use=p.dram("use", [128, NCORES], F32, "ExternalInput"),
                  gath=nc.dram_tensor("gath", [NCORES * 128, 2112], F32, kind="ExternalInput"), yT=mid)
        with p.scope():
            stage_ab(p, C, WStream(p, slot_elems=4096), ab, nt // TA, 2)
        xa = ext("xa0_", _XA_SHAPES)
        xa.update(xT=mid, memT=p.dram("memT", [D, 256], F32, "ExternalInput"), yT=yT,
                  hout=nc.dram_tensor("hout", [128, 32], F32, kind="ExternalOutput"), houtd=Dep())
        with p.scope():
            stage_xa(p, C, WStream(p), xa, nt // T)
    elif gid == 1:
        ffn = ext("ffn0_", _FFN_SHAPES)
        ffn.update(xT=xT, gath_h=nc.dram_tensor("gath_h", [NCORES * 128, 32], F32, kind="ExternalInput"),
                   usel=p.dram("usel", [128, NCORES], F32, "ExternalInput"), yT=yT)
        with p.scope():
            stage_ffn(p, C, WStream(p), ffn, nt // T, False)
        gla = ext("gla_", _GLA_SHAPES)
        gla.update(xT=yT, bounce=nc.dram_tensor("bounce", [128, 4104], F32, kind="ExternalOutput"), bounced=Dep())
        with p.scope():
            stage_gla(p, C, WStream(p, slot_elems=4096), gla, nt // T, 1)
    else:
        gla = ext("gla_", _GLA_SHAPES)
        gla.update(xT=xT, use=p.dram("use", [128, NCORES], F32, "ExternalInput"),
                   gath=nc.dram_tensor("gath", [NCORES * 128, 4104], F32, kind="ExternalInput"), yT=mid)
        with p.scope():
            stage_gla(p, C, WStream(p, slot_elems=4096), gla, nt // T, 2)
        xa = ext("xa1_", _XA_SHAPES)
        xa.update(xT=mid, memT=p.dram("memT", [D, 256], F32, "ExternalInput"), yT=yT,
                  hout=nc.dram_tensor("hout", [128, 32], F32, kind="ExternalOutput"), houtd=Dep())
        with p.scope():
            stage_xa(p, C, WStream(p), xa, nt // T)
    p.finish()
    return p


def kernel_merged(inp, nt):
    maps = make_maps(inp, nt)
    R = range(NCORES)

    def run(p, ms):
        return run_bass_kernel_spmd(p.nc, ms, core_ids=list(R)).results

    def pick(c, prefixes, more, strip=None):
        m = {}
        for k, v in maps[c].items():
            for pre in prefixes:
                if k.startswith(pre):
                    m[k[len(pre):] if strip == pre else k] = v
        m.update(more)
        return m

    r1 = run(build_stage("ab", nt, phase=1), [pick(c, ["ab_"], dict(xT=maps[c]["xT"], xh3=maps[c]["xh3"]), strip="ab_") for c in R])
    gath = np.ascontiguousarray(np.concatenate([r1[c]["bounce"] for c in R], axis=0))
    r = run(build_group(0, nt), [pick(c, ["ab_", "xa0_"], dict(xT=maps[c]["xT"], xh3=maps[c]["xh3"], gath=gath, use=maps[c]["use"], memT=maps[c]["memT"],
                                                                   dt_s=r1[c]["dt_s"], xbc_s=r1[c]["xbc_s"], xc_s=r1[c]["xc_s"])) for c in R])
    del r1
    gath_h = np.ascontiguousarray(np.concatenate([r[c]["hout"] for c in R], axis=0))
    r = run(build_group(1, nt), [pick(c, ["ffn0_", "gla_"], dict(xT=r[c]["yT"], gath_h=gath_h, usel=maps[c]["usel"])) for c in R])
    gath = np.ascontiguousarray(np.concatenate([r[c]["bounce"] for c in R], axis=0))
    r = run(build_group(2, nt), [pick(c, ["gla_", "xa1_"], dict(xT=r[c]["yT"], gath=gath, use=maps[c]["use"], memT=maps[c]["memT"])) for c in R])
    gath_h = np.ascontiguousarray(np.concatenate([r[c]["hout"] for c in R], axis=0))
    r = run(build_stage("ffn", nt, layer=1), [pick(c, ["ffn1_"], dict(xT=r[c]["yT"], gath_h=gath_h, usel=maps[c]["usel"]), strip="ffn1_") for c in R])
    out = np.empty((2, NSEG * nt, D), np.float32)
    for c in R:
        b, q = divmod(c, NSEG)
        out[b, q * nt:(q + 1) * nt] = r[c]["yT"].T
    return out


FUSED = False
MERGED = True


def kernel(**inp):
    if FUSED:
        return kernel_impl(inp, 4096)
    if MERGED:
        return kernel_merged(inp, 4096)
    return kernel_unfused(inp, 4096)
```
